# Optimizing a Trainium2 kernel written in Bass

```python
import jax, jax.numpy as jnp
from jax import lax
import numpy as np

D_MODEL = 1024
BATCH = 8
SEQ = 4096
DEPTH = 2
DEC_BATCH = 8
DEC_SEQ = 16
PAST_LEN = 1024

CHUNK = 64
N_EVEN = (DEPTH + 1) // 2
N_ODD = DEPTH // 2
MIX_W = D_MODEL // 2
PROJ_W = 5 * MIX_W
CONV_A_W = 31
CONV_B_W = 3
HEAD_DIM_C = 64
N_HEADS_C = MIX_W // HEAD_DIM_C
BAND_CHUNKS = 8
ATT_WINDOW = BAND_CHUNKS * CHUNK
BAND = ATT_WINDOW + CHUNK
REL_CLIP = 4 * CHUNK
SGU_CHUNK = 128
SGU_GROUPS = 4
SGU_GDIM = MIX_W // SGU_GROUPS
FFN_HIDDEN = ((8 * D_MODEL // 3 + 255) // 256) * 256
NEG_INF = -1e30

kernel_name = 'hybrid_streaming_conv_band_attn_sgu_step'


def rmsnorm(x, g, eps=1e-6):
    xf = x.astype(jnp.float32)
    y = xf * lax.rsqrt(jnp.mean(xf * xf, axis=-1, keepdims=True) + eps)
    return (y * g.astype(jnp.float32)).astype(x.dtype)


def layernorm(x, g, b, eps=1e-5):
    xf = x.astype(jnp.float32)
    mu = jnp.mean(xf, axis=-1, keepdims=True)
    xc = xf - mu
    var = jnp.mean(xc * xc, axis=-1, keepdims=True)
    y = xc * lax.rsqrt(var + eps) * g.astype(jnp.float32) + b.astype(jnp.float32)
    return y.astype(x.dtype)


def causal_dwconv(x, prev, w):
    xp = jnp.concatenate([prev.astype(x.dtype), x], axis=1)
    y = lax.conv_general_dilated(xp, w[:, None, :].astype(x.dtype), window_strides=(1,),
                                 padding='VALID', dimension_numbers=('NWC', 'WIO', 'NWC'),
                                 feature_group_count=x.shape[-1])
    return y, xp[:, -(w.shape[0] - 1):]


def conv_mixers(h, prev_a, prev_b, w_in, a_w, a_b, a_ln_g, a_ln_b, b_w, w_out):
    p = h @ w_in
    a_val, a_gate, g_b, g_c, b_h = jnp.split(p, 5, axis=-1)
    a = a_val * jax.nn.sigmoid(a_gate)
    a_conv, new_a = causal_dwconv(a, prev_a, a_w)
    a_out = jax.nn.silu(layernorm(a_conv + a_b.astype(a_conv.dtype), a_ln_g, a_ln_b))
    z = g_c * b_h
    z_conv, new_b = causal_dwconv(z, prev_b, b_w)
    b_out = g_b * z_conv
    return jnp.concatenate([a_out, b_out], axis=-1) @ w_out, new_a, new_b


def odd_project(h, w_in, q_g, k_g):
    p = h @ w_in
    q, k, v, u, sv = jnp.split(p, 5, axis=-1)
    B, T = h.shape[:2]
    q = rmsnorm(q.reshape(B, T, N_HEADS_C, HEAD_DIM_C), q_g)
    k = rmsnorm(k.reshape(B, T, N_HEADS_C, HEAD_DIM_C), k_g)
    v = v.reshape(B, T, N_HEADS_C, HEAD_DIM_C)
    return q, k, v, u, sv


def rel_bias_lookup(rel_bias, d):
    return rel_bias[:, jnp.clip(d, -REL_CLIP, REL_CLIP) + REL_CLIP].astype(jnp.float32)


def band_attention_prompt(q, k, v, rel_bias):
    B, T = q.shape[:2]
    nc = T // CHUNK
    pad = jnp.zeros((B, ATT_WINDOW, N_HEADS_C, HEAD_DIM_C), k.dtype)
    kp = jnp.concatenate([pad, k], axis=1)
    vp = jnp.concatenate([pad, v], axis=1)
    idx = jnp.arange(nc)[:, None] * CHUNK + jnp.arange(BAND)[None, :]
    kb = kp[:, idx]
    vb = vp[:, idx]
    qc = q.reshape(B, nc, CHUNK, N_HEADS_C, HEAD_DIM_C)
    s = jnp.einsum('bcqhd,bckhd->bhcqk', qc, kb).astype(jnp.float32) * (HEAD_DIM_C ** -0.5)
    d = jnp.arange(CHUNK)[:, None] + ATT_WINDOW - jnp.arange(BAND)[None, :]
    s = s + rel_bias_lookup(rel_bias, d)[None, :, None]
    valid = idx >= ATT_WINDOW
    s = jnp.where(valid[None, None, :, None, :], s, NEG_INF)
    pr = jax.nn.softmax(s, axis=-1).astype(v.dtype)
    o = jnp.einsum('bhcqk,bckhd->bcqhd', pr, vb)
    return o.reshape(B, T, MIX_W)


def band_attention_sample(q, k, v, cache_k, cache_v, rel_bias):
    B, Tn = q.shape[:2]
    L = cache_k.shape[1]
    kf = jnp.concatenate([cache_k.astype(k.dtype), k], axis=1)
    vf = jnp.concatenate([cache_v.astype(v.dtype), v], axis=1)
    s = jnp.einsum('bqhd,bkhd->bhqk', q, kf).astype(jnp.float32) * (HEAD_DIM_C ** -0.5)
    d = jnp.arange(Tn)[:, None] - (jnp.arange(L + Tn)[None, :] - L)
    s = s + rel_bias_lookup(rel_bias, d)[None]
    pr = jax.nn.softmax(s, axis=-1).astype(v.dtype)
    o = jnp.einsum('bhqk,bkhd->bqhd', pr, vf)
    return o.reshape(B, Tn, MIX_W)


def spatial_gating(u, sv, ln_g, ln_b, w_s, b_s):
    B, T = u.shape[:2]
    lc = min(T, SGU_CHUNK)
    nc = T // lc
    svn = layernorm(sv, ln_g, ln_b)
    vc = svn.reshape(B, nc, lc, SGU_GROUPS, SGU_GDIM)
    wm = jnp.tril(w_s[:, :lc, :lc]).astype(u.dtype)
    s = jnp.einsum('gij,bcjgd->bcigd', wm, vc) + jnp.transpose(b_s[:, :lc]).astype(u.dtype)[:, :, None]
    return u * s.reshape(B, T, MIX_W), svn


def swiglu(h, w_gate_up, w_down):
    g, up = jnp.split(h @ w_gate_up, 2, axis=-1)
    return (jax.nn.silu(g) * up) @ w_down


def setup_inputs(seed: int = 0) -> dict:
    key = jax.random.key(seed)
    ks = jax.random.split(key, 32)
    att_cache = min(ATT_WINDOW, PAST_LEN)

    def nrm(k, shape, scale):
        return jax.random.normal(k, shape, jnp.float32) * scale

    return {
        'x_prompt': nrm(ks[0], (BATCH, SEQ, D_MODEL), 1.0),
        'x_sample': nrm(ks[1], (DEC_BATCH, DEC_SEQ, D_MODEL), 1.0),
        'cache_conv_a': nrm(ks[2], (N_EVEN, DEC_BATCH, CONV_A_W - 1, MIX_W), 0.5),
        'cache_conv_b': nrm(ks[3], (N_EVEN, DEC_BATCH, CONV_B_W - 1, MIX_W), 0.5),
        'cache_k': nrm(ks[4], (N_ODD, DEC_BATCH, att_cache, N_HEADS_C, HEAD_DIM_C), 1.0),
        'cache_v': nrm(ks[5], (N_ODD, DEC_BATCH, att_cache, N_HEADS_C, HEAD_DIM_C), 1.0),
        'norm_mix_even': 1.0 + nrm(ks[6], (N_EVEN, D_MODEL), 0.1),
        'w_in_even': nrm(ks[7], (N_EVEN, D_MODEL, PROJ_W), D_MODEL ** -0.5),
        'conv_a_w': nrm(ks[8], (N_EVEN, CONV_A_W, MIX_W), CONV_A_W ** -0.5),
        'conv_a_b': nrm(ks[9], (N_EVEN, MIX_W), 0.01),
        'ln_a_g': 1.0 + nrm(ks[10], (N_EVEN, MIX_W), 0.1),
        'ln_a_b': nrm(ks[11], (N_EVEN, MIX_W), 0.01),
        'conv_b_w': nrm(ks[12], (N_EVEN, CONV_B_W, MIX_W), CONV_B_W ** -0.5),
        'w_out_even': nrm(ks[13], (N_EVEN, 2 * MIX_W, D_MODEL), (2 * MIX_W) ** -0.5),
        'norm_mix_odd': 1.0 + nrm(ks[14], (N_ODD, D_MODEL), 0.1),
        'w_in_odd': nrm(ks[15], (N_ODD, D_MODEL, PROJ_W), D_MODEL ** -0.5),
        'q_norm_g': 1.0 + nrm(ks[16], (N_ODD, HEAD_DIM_C), 0.1),
        'k_norm_g': 1.0 + nrm(ks[17], (N_ODD, HEAD_DIM_C), 0.1),
        'rel_bias': nrm(ks[18], (N_ODD, N_HEADS_C, 2 * REL_CLIP + 1), 0.1),
        'sgu_ln_g': 1.0 + nrm(ks[19], (N_ODD, MIX_W), 0.1),
        'sgu_ln_b': nrm(ks[20], (N_ODD, MIX_W), 0.01),
        'sgu_w': nrm(ks[21], (N_ODD, SGU_GROUPS, SGU_CHUNK, SGU_CHUNK), SGU_CHUNK ** -0.5),
        'sgu_b': 1.0 + nrm(ks[22], (N_ODD, SGU_GROUPS, SGU_CHUNK), 0.1),
        'w_out_odd': nrm(ks[23], (N_ODD, 2 * MIX_W, D_MODEL), (2 * MIX_W) ** -0.5),
        'norm_ffn': 1.0 + nrm(ks[24], (DEPTH, D_MODEL), 0.1),
        'w_gate_up': nrm(ks[25], (DEPTH, D_MODEL, 2 * FFN_HIDDEN), D_MODEL ** -0.5),
        'w_down': nrm(ks[26], (DEPTH, FFN_HIDDEN, D_MODEL), FFN_HIDDEN ** -0.5),
    }


def reference(x_prompt, x_sample, cache_conv_a, cache_conv_b, cache_k, cache_v,
              norm_mix_even, w_in_even, conv_a_w, conv_a_b, ln_a_g, ln_a_b, conv_b_w, w_out_even,
              norm_mix_odd, w_in_odd, q_norm_g, k_norm_g, rel_bias, sgu_ln_g, sgu_ln_b, sgu_w, sgu_b,
              w_out_odd, norm_ffn, w_gate_up, w_down):
    xp, xs = x_prompt, x_sample
    conv_a_p, conv_b_p, k_p, v_p = [], [], [], []
    conv_a_s, conv_b_s, k_s, v_s, sv_s = [], [], [], [], []
    for layer in range(DEPTH):
        i = layer // 2
        if layer % 2 == 0:
            params = (w_in_even[i], conv_a_w[i], conv_a_b[i], ln_a_g[i], ln_a_b[i], conv_b_w[i], w_out_even[i])
            zero_a = jnp.zeros((xp.shape[0], CONV_A_W - 1, MIX_W), xp.dtype)
            zero_b = jnp.zeros((xp.shape[0], CONV_B_W - 1, MIX_W), xp.dtype)
            mix_p, na_p, nb_p = conv_mixers(rmsnorm(xp, norm_mix_even[i]), zero_a, zero_b, *params)
            mix_s, na_s, nb_s = conv_mixers(rmsnorm(xs, norm_mix_even[i]), cache_conv_a[i], cache_conv_b[i], *params)
            conv_a_p.append(na_p)
            conv_b_p.append(nb_p)
            conv_a_s.append(na_s)
            conv_b_s.append(nb_s)
        else:
            q, k, v, u, sv = odd_project(rmsnorm(xp, norm_mix_odd[i]), w_in_odd[i], q_norm_g[i], k_norm_g[i])
            att = band_attention_prompt(q, k, v, rel_bias[i])
            sg, _ = spatial_gating(u, sv, sgu_ln_g[i], sgu_ln_b[i], sgu_w[i], sgu_b[i])
            mix_p = jnp.concatenate([att, sg], axis=-1) @ w_out_odd[i]
            keep = min(ATT_WINDOW, xp.shape[1])
            k_p.append(k[:, -keep:])
            v_p.append(v[:, -keep:])
            q, k, v, u, sv = odd_project(rmsnorm(xs, norm_mix_odd[i]), w_in_odd[i], q_norm_g[i], k_norm_g[i])
            att = band_attention_sample(q, k, v, cache_k[i], cache_v[i], rel_bias[i])
            sg, svn = spatial_gating(u, sv, sgu_ln_g[i], sgu_ln_b[i], sgu_w[i], sgu_b[i])
            mix_s = jnp.concatenate([att, sg], axis=-1) @ w_out_odd[i]
            k_s.append(k)
            v_s.append(v)
            sv_s.append(svn)
        xp = xp + mix_p
        xs = xs + mix_s
        xp = xp + swiglu(rmsnorm(xp, norm_ffn[layer]), w_gate_up[layer], w_down[layer])
        xs = xs + swiglu(rmsnorm(xs, norm_ffn[layer]), w_gate_up[layer], w_down[layer])
    return (xp, xs,
            jnp.stack(conv_a_p), jnp.stack(conv_b_p), jnp.stack(k_p), jnp.stack(v_p),
            jnp.stack(conv_a_s), jnp.stack(conv_b_s), jnp.stack(k_s), jnp.stack(v_s), jnp.stack(sv_s))
```

```python
import os
from contextlib import ExitStack

import numpy as np
import concourse.bass as bass
import concourse.mybir as mybir
from concourse.bass_utils import run_bass_kernel_spmd

F32 = mybir.dt.float32
BF16 = mybir.dt.bfloat16
AF = mybir.ActivationFunctionType
ALU = mybir.AluOpType

D = 1024
TT = 512
SEQ = 4096
NS = 16
NPT = SEQ // TT
R_SLOTS = 4
NEG = -30000.0

ENG = ("pe", "act", "dve", "pool", "sp")


class Sched:
    def __init__(self):
        self.ops = {e: [] for e in ENG}
        self.ms = {e: 0 for e in ENG}
        self.dmac = {}
        self.strict = False

    def add(self, eng, fn, waits=(), sig=False):
        w = [x for x in waits if x is not None]
        if self.strict and eng in ("act", "dve", "pool"):
            sig = True
            if self.ms[eng] > 0:
                w.append((eng, self.ms[eng]))
        self.ops[eng].append((fn, w, eng if sig else None, 1))
        if sig:
            self.ms[eng] += 1
            return (eng, self.ms[eng])
        return None

    def dma(self, eng, fn, semkey, waits=()):
        w = [x for x in waits if x is not None]
        self.ops[eng].append((fn, w, semkey, 16))
        self.dmac[semkey] = self.dmac.get(semkey, 0) + 16
        return (semkey, self.dmac[semkey])

    def last(self, eng):
        return (eng, self.ms[eng]) if self.ms[eng] > 0 else None

    def join(self, *engs):
        return [self.last(e) for e in (engs or ("pe", "act", "dve", "pool"))]


class Psum:
    def __init__(self):
        self.free = [[] for _ in range(8)]
        self.busy = [False] * 8
        self.nxt = 0

    def alloc(self):
        for k in range(8):
            b = (self.nxt + k) % 8
            if not self.busy[b]:
                break
        else:
            raise RuntimeError("PSUM: all 8 banks busy")
        self.nxt = (b + 1) % 8
        self.busy[b] = True
        w = self.free[b]
        self.free[b] = []
        return b, list(w)

    def release(self, b, *evts):
        self.free[b] = [e for e in evts if e is not None]
        self.busy[b] = False


def build_program(n_prompt_tiles=NPT, do_sample=True, stage_lim=99):
    nc = bass.Bass("TRN2", target_bir_lowering=False)
    S = Sched()
    P = Psum()

    def din(name, shape):
        return nc.dram_tensor(name, list(shape), F32, kind="ExternalInput").ap()

    def dout(name, shape):
        return nc.dram_tensor(name, list(shape), F32, kind="ExternalOutput").ap()

    xp = din("xp", [SEQ, D]); xs = din("xs", [NS, D])
    cca = din("cca", [30, 512]); ccb = din("ccb", [2, 512])
    ck = din("ck", [512, 512]); cv = din("cv", [512, 512])
    nme = din("nme", [D]); wie = din("wie", [D, 2560]); caw = din("caw", [31, 512])
    cab = din("cab", [512]); lag = din("lag", [512]); lab = din("lab", [512])
    cbw = din("cbw", [3, 512]); woe = din("woe", [D, D]); nmo = din("nmo", [D])
    wio = din("wio", [D, 2560]); qg = din("qg", [64]); kg = din("kg", [64])
    rb = din("rb", [8, 513]); slg = din("slg", [512]); slb = din("slb", [512])
    sw = din("sw", [4, 128, 128]); sb = din("sb", [4, 128]); woo = din("woo", [D, D])
    nf = din("nf", [2, D]); wgu = din("wgu", [2, D, 5632]); wd = din("wd", [2, 2816, D])

    yp = dout("yp", [SEQ, D]); ys = dout("ys", [NS, D])
    cap = dout("cap", [30, 512]); cbp = dout("cbp", [2, 512])
    kp = dout("kp", [512, 512]); vp = dout("vp", [512, 512])
    cas = dout("cas", [30, 512]); cbs = dout("cbs", [2, 512])
    ks = dout("ks", [NS, 512]); vs = dout("vs", [NS, 512]); svs = dout("svs", [NS, 512])

    if stage_lim < 99:
        dbg_x = dout("dbg_x", [128, 8 * TT]); dbg_c = dout("dbg_c", [128, 8 * TT]); dbg_h = dout("dbg_h", [128, 8 * TT])
    ext_h = nc.dram_tensor("ext_scratch", [8, 128 * 769], F32, kind="Internal")
    ext = ext_h.ap()

    with ExitStack() as es:
        def sb_t(name, shape, dt):
            return es.enter_context(nc.sbuf_tensor(name, list(shape), dt))

        ps = es.enter_context(nc.psum_tensor("ps", [128, 8, 512], F32))
        xT = sb_t("xT", [128, 8, TT], F32)
        hT = sb_t("hT", [128, 8, TT], BF16)
        cat = hT
        ring = sb_t("ring", [128, R_SLOTS, 8, 512], BF16)
        Tb = sb_t("Tb", [128, 5, 8, 128], BF16)
        ident = sb_t("ident", [128, 128], F32)
        o1024 = sb_t("o1024", [128, 128], BF16)
        o512 = sb_t("o512", [128, 128], BF16)
        blk64 = sb_t("blk64", [128, 128], BF16)
        ones1 = sb_t("ones1", [128, 128], BF16)
        vecT = sb_t("vecT", [128, 64], F32)
        cawT = sb_t("cawT", [128, 124], F32)
        wsT = sb_t("wsT", [128, 4, 128], BF16)
        bsrow = sb_t("bsrow", [1, 512], BF16)
        dg = sb_t("dg", [128, 124, 128], BF16)
        slg_bc = sb_t("slg_bc", [128, 512], F32)
        slb_bc = sb_t("slb_bc", [128, 512], F32)
        eps6 = sb_t("eps6", [128, 1], F32)
        eps5 = sb_t("eps5", [128, 1], F32)
        kT = sb_t("kT", [128, 4, 1024], BF16)
        Vb = sb_t("Vb", [128, 8, 512], BF16)
        hist_a = sb_t("hist_a", [128, 4, 30], F32)
        hist_b = sb_t("hist_b", [128, 4, 2], F32)
        stage = sb_t("stage", [128, 2, 1024], F32)
        rstd = sb_t("rstd", [128, TT], F32)
        st6 = sb_t("st6", [128, 4, 6], F32)
        mv = sb_t("mv", [128, 4, 2], F32)
        rsd = sb_t("rsd", [128, 4], F32)
        U = sb_t("U", [128, 18432], F32)
        Ub = U.bitcast(BF16)

        def uf32(off, *shape):
            n = int(np.prod(shape))
            v = U[:, off:off + n]
            if len(shape) == 2:
                v = v.rearrange("p (a b) -> p a b", a=shape[0])
            elif len(shape) == 3:
                v = v.rearrange("p (a b c) -> p a b c", a=shape[0], b=shape[1])
            return v

        def ubf(off32, *shape):
            n = int(np.prod(shape))
            v = Ub[:, 2 * off32:2 * off32 + n]
            if len(shape) == 2:
                v = v.rearrange("p (a b) -> p a b", a=shape[0])
            elif len(shape) == 3:
                v = v.rearrange("p (a b c) -> p a b c", a=shape[0], b=shape[1])
            return v

        abuf = uf32(0, 4, 544)
        zbuf = uf32(2176, 4, 516)
        gbb = uf32(4240, 4, 512)
        sgs = uf32(6288, 4, 512)
        acv = uf32(8336, 4, 512)
        xbf = ubf(10384, 4, 512)
        sqb0 = ubf(11408, 4, 512)
        hid = ubf(0, 22, 512)
        stmp = uf32(5632, 2, 512)
        qf = uf32(0, 4, 512)
        sqb1 = ubf(2048, 4, 512)
        kf = uf32(3072, 4, 512)
        qT = ubf(5120, 4, 512)
        uf = uf32(6144, 4, 512)
        tsv = uf32(8192, 2, 512)
        svn = ubf(9216, 4, 512)
        tbuf = uf32(10240, 2, 8, 128)
        Eb = ubf(12288, 2, 5, 1024)
        rden = uf32(17408, 2, 512)
        T32 = uf32(0, 5, 1024)
        VbF = Vb.bitcast(F32)
        extS = VbF[0:8, 5:8, :].rearrange("p a b -> p (a b)")
        bsrow32 = U[0:1, 10000:10512]
        abuf16 = ubf(12432, 4, 544)
        lnt = uf32(13520, 4, 512)

        sems = {}
        for k in ("pe", "act", "dve", "pool"):
            sems[k] = es.enter_context(nc.semaphore("m_" + k))
        for k in ["cst", "cvl", "ex1", "tbl", "tb2", "ysem", "xpre", "outs", "stl0", "stl1", "sts0", "sts1"] + ["ring%d" % i for i in range(R_SLOTS)]:
            sems[k] = es.enter_context(nc.semaphore("d_" + k))

        def act(out, in_, func, waits=(), sig=False, scale=None, bias=None):
            def fn(e):
                kw = {}
                if scale is not None:
                    kw["scale"] = scale
                if bias is not None:
                    kw["bias"] = bias
                return e.activation(out=out, in_=in_, func=func, **kw)
            return S.add("act", fn, waits, sig)

        def tt(eng, out, in0, in1, op, waits=(), sig=False):
            return S.add(eng, lambda e: e.tensor_tensor(out=out, in0=in0, in1=in1, op=op), waits, sig)

        def stt(out, in0, scalar, in1, op0, op1, waits=(), sig=False):
            return S.add("dve", lambda e: e.scalar_tensor_tensor(out=out, in0=in0, scalar=scalar, in1=in1, op0=op0, op1=op1), waits, sig)

        def ts(eng, out, in0, s1, s2, op0, op1=None, waits=(), sig=False):
            def fn(e):
                if op1 is None:
                    return e.tensor_scalar(out=out, in0=in0, scalar1=s1, scalar2=None, op0=op0)
                return e.tensor_scalar(out=out, in0=in0, scalar1=s1, scalar2=s2, op0=op0, op1=op1)
            return S.add(eng, fn, waits, sig)

        def cp(eng, out, in_, waits=(), sig=False):
            if eng == "act":
                return act(out, in_, AF.Copy, waits, sig)
            return S.add(eng, lambda e: e.tensor_copy(out=out, in_=in_), waits, sig)

        def mm(out, lhsT, rhs, start, stop, waits=(), sig=False):
            return S.add("pe", lambda e: e.matmul(out, lhsT, rhs, start=start, stop=stop), waits, sig)

        def tr(out, in_, idn, waits=(), sig=False):
            return S.add("pe", lambda e: e.transpose(out, in_, idn), waits, sig)

        stage_free = [[], []]
        stage_n = [0]

        def stage_alloc():
            s = stage_n[0] % 2
            stage_n[0] += 1
            w = stage_free[s]
            stage_free[s] = []
            return s, list(w)

        ring_n = [0]
        ring_free = [[] for _ in range(R_SLOTS)]

        def ring_load(pieces):
            slot = ring_n[0] % R_SLOTS
            ring_n[0] += 1
            w = ring_free[slot]
            ring_free[slot] = []
            evt = None
            for i, (src, nkc, dcol, ncols) in enumerate(pieces):
                def fn(e, src=src, nkc=nkc, dcol=dcol, ncols=ncols, slot=slot):
                    return e.dma_start(out=ring[:, slot, 0:nkc, dcol:dcol + ncols], in_=src)
                evt = S.dma("pool", fn, "ring%d" % slot, w if i == 0 else ())
            return slot, evt

        def wview(w_ap):
            return w_ap.rearrange("(kc p) n -> p kc n", p=128)

        def gemm(groups, KC, in_chunk, NT, in_waits, evac, mode="ws", kc_evts=None):
            first = True
            deferred = []
            for gi, (pieces, gmode) in enumerate(groups):
                nslots = (KC + 7) // 8
                slot_info = []
                for si in range(nslots):
                    kc0 = si * 8
                    nkc = min(8, KC - kc0)
                    pl = []
                    dcol = 0
                    for (wv, c0, ncols) in pieces:
                        pl.append((wv[:, kc0:kc0 + nkc, c0:c0 + ncols], nkc, dcol, ncols))
                        dcol += ncols
                    slot_info.append((ring_load(pl), kc0, nkc))
                n_out = 4 if gmode == "ws" else (NT + 127) // 128
                banks = []
                for i in range(n_out):
                    banks.append(P.alloc())
                last_evt = [None] * n_out
                for si, ((slot, levt), kc0, nkc) in enumerate(slot_info):
                    for i in range(n_out):
                        b, bw = banks[i]
                        for k in range(nkc):
                            kc = kc0 + k
                            w = []
                            if k == 0:
                                w = [levt] + (bw if si == 0 else [])
                                if first:
                                    w += list(in_waits)
                                    first = False
                            if kc_evts is not None and gi == 0 and i == 0:
                                w = list(w) + [kc_evts[kc]]
                            is_last_k = (k == nkc - 1)
                            sig = is_last_k and (si == nslots - 1 or i == n_out - 1)
                            if gmode == "ws":
                                e = mm(ps[:, b, 0:NT], ring[:, slot, k, i * 128:(i + 1) * 128], in_chunk(kc)[:, 0:NT],
                                       kc == 0, kc == KC - 1, w, sig)
                            else:
                                rows = min(128, NT - i * 128)
                                e = mm(ps[0:rows, b, 0:512], in_chunk(kc)[:, i * 128:i * 128 + rows], ring[:, slot, k, 0:512],
                                       kc == 0, kc == KC - 1, w, sig)
                            if sig:
                                last_evt[i] = e
                    ring_free[slot] = [S.last("pe")]
                for d in deferred:
                    d()
                deferred = []
                for i in range(n_out):
                    r = evac(gi, i, banks[i][0], last_evt[i])
                    if r is not None:
                        deferred.append(r)
            for d in deferred:
                d()

        cst = []

        def cdma(out, in_, eng="sp", waits=()):
            e = S.dma(eng, lambda e_: e_.dma_start(out=out, in_=in_), "cst", waits)
            cst.append(e)
            return e

        VC = {}
        row = 0
        s0 = stage[:, 0, :]
        s1 = stage[:, 1, :]

        def vrows(name, ap2d, n):
            nonlocal row
            VC[name] = row
            cdma(s0[row:row + n, 0:128], ap2d)
            row += n

        vrows("nme", nme.rearrange("(c p) -> c p", p=128), 8)
        vrows("nmo", nmo.rearrange("(c p) -> c p", p=128), 8)
        vrows("nf", nf.rearrange("l (c p) -> (l c) p", p=128), 16)
        vrows("cab", cab.rearrange("(c p) -> c p", p=128), 4)
        vrows("lag", lag.rearrange("(c p) -> c p", p=128), 4)
        vrows("lab", lab.rearrange("(c p) -> c p", p=128), 4)
        vrows("cbw", cbw.rearrange("k (c p) -> (k c) p", p=128), 12)
        qg2 = qg.rearrange("(o d) -> o d", o=1)
        kg2 = kg.rearrange("(o d) -> o d", o=1)
        VC["qg"] = row
        cdma(s0[row:row + 1, 0:64], qg2)
        cdma(s0[row:row + 1, 64:128], qg2)
        row += 1
        VC["kg"] = row
        cdma(s0[row:row + 1, 0:64], kg2)
        cdma(s0[row:row + 1, 64:128], kg2)
        row += 1
        NV = row
        cdma(s0[0:124, 128:256], caw.rearrange("k (c p) -> (k c) p", p=128))
        for g in range(4):
            cdma(s0[:, 256 + g * 128:256 + (g + 1) * 128], sw[g])
        cdma(bsrow32[0:1, :], sb.rearrange("(o g) i -> o (g i)", o=1))
        cdma(slg_bc[:, :], bass.AP(tensor=slg.tensor, offset=0, ap=[[0, 128], [1, 512]]))
        cdma(slb_bc[:, :], bass.AP(tensor=slb.tensor, offset=0, ap=[[0, 128], [1, 512]]))
        W_ = 768
        GS = 128 * (W_ + 1)
        d0 = S.dma("sp", lambda e_: e_.dma_start(out=extS[:, 0:384], in_=rb[:, 129:513]), "ex1")
        ef = ts("dve", extS[:, 384:768], extS[:, 0:384], 0.0, extS[:, 383:384], ALU.mult, ALU.add, waits=[d0], sig=True)
        cst_all = [("cst", S.dmac["cst"])]
        late = {"done": False, "done2": False}

        def late_prologue():
            if late["done"]:
                return
            late["done"] = True
            strict0 = S.strict
            S.strict = False
            for j in range(124):
                ts("dve", dg[:, j, :], ident[:, :], cawT[:, j:j + 1], None, ALU.mult)
            late["dg_ready"] = [S.add("dve", lambda e: e.nop(), sig=True)]
            S.strict = strict0
            gl = []
            for h in range(8):
                gl.append(S.dma("act", lambda e_, h=h: e_.dma_start(
                    out=bass.AP(tensor=ext_h, offset=h * GS, ap=[[W_ + 1, 128], [1, 768]]),
                    in_=bass.AP(tensor=VbF, offset=h * 2048 + 1280, ap=[[2048, 1], [0, 128], [1, 768]])),
                    "tbl", [ef] if h == 0 else ()))
            late["tbl_done"] = gl[-1]

        def late_prologue2():
            if late["done2"]:
                return
            late["done2"] = True
            tbl_done = late["tbl_done"]
            last = None
            for jb in range(5):
                for par in range(2):
                    src = bass.AP(tensor=ext_h, offset=par * GS + 639 - jb * 128, ap=[[W_, 128], [2 * GS, 4], [1, 128]])
                    last = S.dma("pool", lambda e_, jb=jb, par=par, src=src: e_.dma_start(
                        out=Tb[:, jb, par * 4:(par + 1) * 4, :], in_=src), "tb2", [tbl_done] if (jb == 0 and par == 0) else ())
            S.add("pool", lambda e: e.memset(Tb[0:64, 0, :, 64:128], NEG), waits=[last])
            late["tb_ready"] = S.add("pool", lambda e: e.memset(Tb[64:128, 4, :, 0:64], NEG), sig=True)

        S.strict = True
        S.add("pool", lambda e: e.memset(ident[:, :], 1.0))
        S.add("pool", lambda e: e.affine_select(out=ident[:, :], in_=ident[:, :], pattern=[[-1, 128]],
                                                compare_op=ALU.is_equal, fill=0.0, base=0, channel_multiplier=1))
        S.add("pool", lambda e: e.memset(o1024[:, :], 1.0 / 1024))
        S.add("pool", lambda e: e.memset(o512[:, :], 1.0 / 512))
        S.add("pool", lambda e: e.memset(ones1[:, :], 1.0))
        S.add("pool", lambda e: e.memset(blk64[:, :], 0.0))
        S.add("pool", lambda e: e.memset(blk64[0:64, 0:64], 1.0 / 64))
        S.add("pool", lambda e: e.memset(blk64[64:128, 64:128], 1.0 / 64))
        S.add("pool", lambda e: e.memset(eps6[:, :], 1e-6))
        S.add("pool", lambda e: e.memset(eps5[:, :], 1e-5))
        S.add("pool", lambda e: e.memset(hist_a[:, :, :], 0.0))
        pool_c = S.add("pool", lambda e: e.memset(hist_b[:, :, :], 0.0), sig=True)
        S.strict = False

        b0, w0 = P.alloc()
        tr(ps[:, b0, 0:NV], s0[0:NV, 0:128], ident[0:NV, 0:NV], waits=cst_all + [pool_c] + w0)
        tr(ps[:, b0, 128:252], s0[0:124, 128:256], ident[0:124, 0:124])
        b1, w1 = P.alloc()
        for g in range(4):
            ev = tr(ps[:, b1, g * 128:(g + 1) * 128], s0[:, 256 + g * 128:256 + (g + 1) * 128], ident[:, :],
                    waits=w1 if g == 0 else (), sig=(g == 3))
        cp("dve", vecT[:, 0:NV], ps[:, b0, 0:NV], waits=[ev])
        cp("dve", cawT[:, :], ps[:, b0, 128:252])
        cp("dve", sgs[:, 0, :], ps[:, b1, :])
        ev2 = cp("dve", bsrow[0:1, :], bsrow32[0:1, :], sig=True)
        P.release(b0, ev2)
        P.release(b1, ev2)
        for g in range(4):
            S.add("pool", lambda e, g=g: e.affine_select(out=wsT[:, g, :], in_=sgs[:, 0, g * 128:(g + 1) * 128],
                                                         pattern=[[1, 128]], compare_op=ALU.is_ge, fill=0.0, base=0,
                                                         channel_multiplier=-1), waits=[ev2] if g == 0 else ())
        pool_c2 = S.add("pool", lambda e: e.nop(), sig=True)
        stage_free[0] = [ev]

        if do_sample:
            for half in range(2):
                sslot, sw_ = stage_alloc()
                ld = S.dma("sp", lambda e_, half=half, sslot=sslot: e_.dma_start(
                    out=stage[:, sslot, :].rearrange("p (a b) -> p a b", a=2),
                    in_=ck[half * 256:(half + 1) * 256, :].rearrange("(a p) n -> p a n", p=128)), "stl%d" % sslot, sw_)
                lastpe = None
                for a in range(2):
                    blk = half * 2 + a
                    b, bw = P.alloc()
                    for c in range(4):
                        lastpe = tr(ps[:, b, c * 128:(c + 1) * 128], stage[:, sslot, a * 512 + c * 128:a * 512 + (c + 1) * 128],
                                    ident[:, :], waits=[ld, ev2] + bw if c == 0 else (), sig=(c == 3))
                    e = cp("act", kT[:, :, blk * 128:(blk + 1) * 128], ps[:, b, :].rearrange("p (c t) -> p c t", c=4),
                           waits=[lastpe], sig=True)
                    P.release(b, e)
                stage_free[sslot] = [lastpe]
            cvl = S.dma("pool", lambda e_: e_.dma_start(out=Vb[:, 0:4, :], in_=cv.rearrange("(a p) n -> p a n", p=128)), "cvl")
        else:
            cvl = None

        out_evts = []
        prefetch = {}
        y_dmas = []
        ystage = uf32(0, 4, 1024)
        xpre = uf32(14336, 4, 1024)

        def process_tile(is_sample, ti):
            NT = NS if is_sample else TT
            nblk = (NT + 127) // 128
            xsrc = xs if is_sample else xp
            ydst = ys if is_sample else yp
            t0 = 0 if is_sample else ti * TT
            last_tile = (not is_sample) and (ti == n_prompt_tiles - 1)

            x_ready = []
            for blk in range(nblk):
                rows = min(128, NT - blk * 128)
                if (is_sample, ti, blk) in prefetch:
                    ld = prefetch.pop((is_sample, ti, blk))
                    xin = xpre[:, blk, :]
                    sslot = None
                else:
                    sslot, sw_ = stage_alloc()
                    ld = S.dma("sp", lambda e_, blk=blk, rows=rows, sslot=sslot: e_.dma_start(
                        out=stage[0:rows, sslot, :], in_=xsrc[t0 + blk * 128:t0 + blk * 128 + rows, :]), "stl%d" % sslot, sw_)
                    xin = stage[:, sslot, :]
                lastpe = None
                for half in range(2):
                    b, bw = P.alloc()
                    for c4 in range(4):
                        c = half * 4 + c4
                        lastpe = tr(ps[:, b, c4 * 128:c4 * 128 + rows], xin[0:rows, c * 128:(c + 1) * 128],
                                    ident[0:rows, 0:rows], waits=([ld] + bw + (S.join("act", "dve", "pool") if (blk == 0 and half == 0) else [])) if c4 == 0 else (), sig=(c4 == 3))
                    e = cp("dve", xT[:, half * 4:(half + 1) * 4, blk * 128:blk * 128 + rows],
                           ps[:, b, :].rearrange("p (c t) -> p c t", c=4)[:, :, 0:rows], waits=[lastpe], sig=True)
                    P.release(b, e)
                    x_ready = [e]
                if sslot is not None:
                    stage_free[sslot] = [lastpe]

            if stage_lim < 1:
                return
            def rmsnorm(gcol, x_waits, chunk_evts=None):
                hfree = S.join("pe")
                sq = []
                for c in range(8):
                    w = [chunk_evts[c]] if chunk_evts else []
                    if c == 0:
                        w += list(x_waits) + hfree
                    sq.append(act(hT[:, c, 0:NT], xT[:, c, 0:NT], AF.Square, waits=w, sig=True))
                b, bw = P.alloc()
                e2_ = None
                for c in range(8):
                    e2_ = mm(ps[:, b, 0:NT], o1024[:, :], hT[:, c, 0:NT], c == 0, c == 7, waits=[sq[c]] + (bw if c == 0 else []), sig=(c == 7))
                act(rstd[:, 0:NT], ps[:, b, 0:NT], AF.Ln, waits=[e2_], bias=eps6[:, :])
                e3_ = act(ps[:, b, 0:NT], rstd[:, 0:NT], AF.Exp, scale=-0.5, sig=True)
                hev = []
                for c in range(8):
                    hev.append(stt(hT[:, c, 0:NT], ps[:, b, 0:NT], vecT[:, gcol + c:gcol + c + 1], xT[:, c, 0:NT], ALU.mult, ALU.mult,
                                   waits=[e3_] if c == 0 else (), sig=True))
                P.release(b, hev[-1])
                return hev

            res_evt = {}

            def residual_evac(gi, i, b, pe_evt):
                c = gi * 4 + i
                e = stt(xT[:, c, 0:NT], ps[:, b, 0:NT], 1.0, xT[:, c, 0:NT], ALU.mult, ALU.add, waits=[pe_evt], sig=True)
                res_evt[c] = e
                P.release(b, e)

            def ffn(layer, x_waits):
                h_ready = rmsnorm(VC["nf"] + 8 * layer, [], dict(res_evt))
                if stage_lim < 5.3:
                    return S.join("dve")
                wg = wview(wgu[layer])
                groups = []
                for j in range(11):
                    groups.append(([(wg, j * 256, 256), (wg, 2816 + j * 256, 256)], "ws"))
                pend = {}

                def evac_gu(gi, i, b, pe_evt):
                    if i < 2:
                        e = act(stmp[:, i, 0:NT], ps[:, b, 0:NT], AF.Silu, waits=[pe_evt] + S.join("dve"), sig=True)
                        P.release(b, e)
                        pend[i] = e
                    else:
                        e = stt(hid[:, 2 * gi + (i - 2), 0:NT], ps[:, b, 0:NT], 1.0, stmp[:, i - 2, 0:NT], ALU.mult, ALU.mult,
                               waits=[pe_evt, pend[i - 2]], sig=True)
                        P.release(b, e)

                gemm(groups, 8, lambda kc: hT[:, kc, :], NT, [], evac_gu, kc_evts=h_ready)
                if stage_lim < 5.6:
                    return S.join("dve")
                wdv = wview(wd[layer])
                groups = [([(wdv, g * 512, 512)], "ws") for g in range(2)]
                gemm(groups, 22, lambda kc: hid[:, kc, :], NT, S.join("act", "dve"), residual_evac)
                return S.join("dve")

            h_ready = rmsnorm(VC["nme"], x_ready)
            first_tile = not late["done"]
            late_prologue()
            if stage_lim < 0.5:
                return
            if is_sample:
                sslot, sw_ = stage_alloc()
                ld = S.dma("sp", lambda e_, sslot=sslot: e_.dma_start(out=stage[0:30, sslot, 0:512], in_=cca[:, :]), "stl%d" % sslot, sw_)
                ld = S.dma("sp", lambda e_, sslot=sslot: e_.dma_start(out=stage[0:2, sslot, 512:1024], in_=ccb[:, :]), "stl%d" % sslot)
                b, bw = P.alloc()
                lastpe = None
                for c in range(4):
                    tr(ps[:, b, c * 32:c * 32 + 30], stage[0:30, sslot, c * 128:(c + 1) * 128], ident[0:30, 0:30],
                       waits=[ld] + bw + S.join("act", "dve", "pool") if c == 0 else ())
                    lastpe = tr(ps[:, b, 128 + c * 32:128 + c * 32 + 2], stage[0:2, sslot, 512 + c * 128:512 + (c + 1) * 128],
                                ident[0:2, 0:2], sig=(c == 3))
                cp("act", abuf[:, :, 0:30], ps[:, b, 0:128].rearrange("p (c t) -> p c t", c=4)[:, :, 0:30], waits=[lastpe])
                h16 = cp("act", abuf16[:, :, 0:30], ps[:, b, 0:128].rearrange("p (c t) -> p c t", c=4)[:, :, 0:30], sig=True)
                e = cp("act", zbuf[:, :, 0:2], ps[:, b, 128:256].rearrange("p (c t) -> p c t", c=4)[:, :, 0:2], sig=True)
                hz = e
                P.release(b, e)
                stage_free[sslot] = [lastpe]
            else:
                cp("act", abuf[:, :, 0:30], hist_a[:, :, :], waits=S.join("pe", "dve", "pool") + list(y_dmas))
                h16 = cp("act", abuf16[:, :, 0:30], hist_a[:, :, :], sig=True)
                hz = cp("act", zbuf[:, :, 0:2], hist_b[:, :, :], sig=True)

            if stage_lim < 0.7:
                return
            wv0 = wview(wie)
            l0_order = [0, 1, 3, 4, 2]
            groups = [([(wv0, g * 512, 512)], "ws") for g in l0_order]
            sig_evt = {}
            a16_evt = {}
            convA_state = {}

            def convA_all():
                for c in range(4):
                    b, bw = P.alloc()
                    lp = None
                    for k in range(31):
                        lp = mm(ps[:, b, 0:NT], dg[:, k * 4 + c, :], abuf16[:, c, k:k + NT], k == 0, k == 30,
                                waits=([a16_evt[c], h16] + bw + late["dg_ready"]) if k == 0 else (), sig=(k == 30))
                    cabc = vecT[:, VC["cab"] + c:VC["cab"] + c + 1]
                    act(acv[:, c, 0:NT], ps[:, b, 0:NT], AF.Identity, waits=[lp], bias=cabc)
                    act(xbf[:, c, 0:NT], ps[:, b, 0:NT], AF.Identity, bias=cabc)
                    e = act(sqb0[:, c, 0:NT], ps[:, b, 0:NT], AF.Square, bias=cabc, sig=True)
                    P.release(b, e)
                convA_state["done"] = S.last("act")

            def convB_taps():
                cb0 = VC["cbw"]
                for c in range(4):
                    acc2 = sgs[:, c, 0:NT]
                    ts("dve", acc2, zbuf[:, c, 0:NT], vecT[:, cb0 + c:cb0 + c + 1], None, ALU.mult, waits=[hz] if c == 0 else ())
                    stt(acc2, zbuf[:, c, 1:1 + NT], vecT[:, cb0 + 4 + c:cb0 + 4 + c + 1], acc2, ALU.mult, ALU.add)
                    stt(acc2, zbuf[:, c, 2:2 + NT], vecT[:, cb0 + 8 + c:cb0 + 8 + c + 1], acc2, ALU.mult, ALU.add)

            def evac_l0(gi, i, b, pe_evt):
                gi = l0_order[gi]
                if gi == 0:
                    e = cp("act", abuf[:, i, 30:30 + NT], ps[:, b, 0:NT], waits=[pe_evt], sig=True)
                    P.release(b, e)
                elif gi == 1:
                    e = act(sgs[:, i, 0:NT], ps[:, b, 0:NT], AF.Sigmoid, waits=[pe_evt], sig=True)
                    P.release(b, e)
                    tt("dve", abuf[:, i, 30:30 + NT], abuf[:, i, 30:30 + NT], sgs[:, i, 0:NT], ALU.mult, waits=[e])
                    a16_evt[i] = cp("dve", abuf16[:, i, 30:30 + NT], abuf[:, i, 30:30 + NT], sig=True)
                    if i == 3:
                        return convA_all
                elif gi == 2:
                    e = cp("act", gbb[:, i, 0:NT], ps[:, b, 0:NT], waits=[pe_evt], sig=True)
                    P.release(b, e)
                elif gi == 3:
                    e = cp("act", zbuf[:, i, 2:2 + NT], ps[:, b, 0:NT], waits=[pe_evt], sig=True)
                    P.release(b, e)
                    sig_evt[i] = e
                else:
                    e = stt(zbuf[:, i, 2:2 + NT], ps[:, b, 0:NT], 1.0, zbuf[:, i, 2:2 + NT], ALU.mult, ALU.mult,
                           waits=[pe_evt, sig_evt[i]], sig=True)
                    P.release(b, e)
                    if i == 3:
                        return convB_taps

            gemm(groups, 8, lambda kc: hT[:, kc, :], NT, S.join("act") + list(y_dmas), evac_l0, kc_evts=h_ready)

            if stage_lim < 2:
                return
            join0 = S.join("act", "pe", "pool", "dve")
            convA_done = convA_state["done"]
            for c in range(4):
                e = tt("dve", cat[:, 4 + c, 0:NT], gbb[:, c, 0:NT], sgs[:, c, 0:NT], ALU.mult, waits=join0 if c == 0 else (), sig=True)
            conv_done = S.last("dve")
            if not is_sample:
                cp("pool", hist_a[:, :, :], abuf[:, :, NT:NT + 30], waits=[conv_done] + S.join("act"))
                cp("pool", hist_b[:, :, :], zbuf[:, :, NT:NT + 2], sig=True)
            if is_sample or last_tile:
                oa, ob = (cas, cbs) if is_sample else (cap, cbp)
                b, bw = P.alloc()
                lastpe = None
                for c in range(4):
                    tr(ps[0:30, b, c * 128:(c + 1) * 128], abuf[:, c, NT:NT + 30], ident[:, :], waits=[conv_done] + bw if c == 0 else ())
                b2, bw2 = P.alloc()
                for c in range(4):
                    lastpe = tr(ps[0:2, b2, c * 128:(c + 1) * 128], zbuf[:, c, NT:NT + 2], ident[:, :], waits=bw2 if c == 0 else (), sig=(c == 3))
                sslot, sw_ = stage_alloc()
                cp("act", stage[0:30, sslot, 0:512], ps[0:30, b, :], waits=[lastpe] + sw_)
                e = cp("act", stage[0:2, sslot, 512:1024], ps[0:2, b2, :], sig=True)
                P.release(b, e)
                P.release(b2, e)
                d1 = S.dma("sp", lambda e_, sslot=sslot: e_.dma_start(out=oa[:, :], in_=stage[0:30, sslot, 0:512]), "sts%d" % sslot, [e])
                d2 = S.dma("sp", lambda e_, sslot=sslot: e_.dma_start(out=ob[:, :], in_=stage[0:2, sslot, 512:1024]), "sts%d" % sslot)
                stage_free[sslot] = [d2]
            if stage_lim < 3:
                return
            e1_ = convA_done
            bm, bwm = P.alloc()
            bq, bwq = P.alloc()
            for c in range(4):
                mm(ps[:, bm, 0:NT], o512[:, :], xbf[:, c, 0:NT], c == 0, c == 3, waits=[e1_] + bwm + bwq if c == 0 else ())
            e2_ = None
            for c in range(4):
                e2_ = mm(ps[:, bq, 0:NT], o512[:, :], sqb0[:, c, 0:NT], c == 0, c == 3, sig=(c == 3))
            e3_ = act(sgs[:, 0, 0:NT], ps[:, bm, 0:NT], AF.Square, waits=[e2_] + S.join("dve"), sig=True)
            e4_ = stt(sgs[:, 1, 0:NT], ps[:, bq, 0:NT], 1.0, sgs[:, 0, 0:NT], ALU.mult, ALU.subtract, waits=[e3_], sig=True)
            act(rstd[:, 0:NT], sgs[:, 1, 0:NT], AF.Ln, waits=[e4_], bias=eps5[:, :])
            e5_ = act(rstd[:, 0:NT], rstd[:, 0:NT], AF.Exp, scale=-0.5, sig=True)
            P.release(bq, e4_)
            e7_ = {}
            for c in range(4):
                t1 = lnt[:, c % 2, 0:NT]
                t2 = lnt[:, 2 + c % 2, 0:NT]
                stt(t1, ps[:, bm, 0:NT], -1.0, acv[:, c, 0:NT], ALU.mult, ALU.add, waits=[e5_, e7_.get(c - 2)])
                e6_ = tt("dve", t2, t1, rstd[:, 0:NT], ALU.mult, sig=True)
                e7_[c] = act(cat[:, c, 0:NT], t2, AF.Silu, waits=[e6_],
                             scale=vecT[:, VC["lag"] + c:VC["lag"] + c + 1], bias=vecT[:, VC["lab"] + c:VC["lab"] + c + 1], sig=True)
            P.release(bm, S.last("dve"))
            if stage_lim < 4:
                return
            wvo = wview(woe)
            groups = [([(wvo, g * 512, 512)], "ws") for g in range(2)]
            gemm(groups, 8, lambda kc: cat[:, kc, :], NT, S.join("act", "dve", "pool"), residual_evac)
            if stage_lim < 5:
                return
            xw = ffn(0, S.join("dve"))

            if stage_lim < 6:
                return
            late_prologue2()
            h_ready = rmsnorm(VC["nmo"], [], dict(res_evt))
            wv1 = wview(wio)
            groups = [([(wv1, 0, 512)], "ws"), ([(wv1, 512, 512)], "ws"), ([(wv1, 1024, 512)], "as"),
                      ([(wv1, 1536, 512)], "ws"), ([(wv1, 2048, 512)], "as")]
            Q0 = 4 if is_sample else 4 * ti
            kcol0 = (Q0 % 8) * 128
            qk_state = {"stt": None}
            qstat_evt = {}
            kv_out = is_sample or last_tile
            okd, ovd = (ks, vs) if is_sample else (kp, vp)

            def headnorm(gi, i, b, pe_evt):
                xf = qf if gi == 0 else kf
                cp("act", xf[:, i, 0:NT], ps[:, b, 0:NT], waits=[pe_evt, qstat_evt.get(i) if gi == 1 else None])
                e = act(sqb1[:, i, 0:NT], ps[:, b, 0:NT], AF.Square, sig=True)
                P.release(b, e)
                return lambda: headnorm2(gi, i, e)

            def headnorm2(gi, i, e):
                b2, bw2 = P.alloc()
                ep = mm(ps[:, b2, 0:NT], blk64[:, :], sqb1[:, i, 0:NT], True, True, waits=[e] + bw2, sig=True)
                if gi == 0:
                    qstat_evt[i] = ep
                act(rstd[:, 0:NT], ps[:, b2, 0:NT], AF.Ln, waits=[ep], bias=eps6[:, :])
                e3 = act(ps[:, b2, 0:NT], rstd[:, 0:NT], AF.Exp, scale=-0.5, sig=True)
                if gi == 0:
                    es_ = stt(qT[:, i, 0:NT], ps[:, b2, 0:NT], vecT[:, VC["qg"]:VC["qg"] + 1], qf[:, i, 0:NT], ALU.mult, ALU.mult,
                              waits=[e3], sig=True)
                    P.release(b2, es_)
                else:
                    es_ = stt(kf[:, i, 0:NT], ps[:, b2, 0:NT], vecT[:, VC["kg"]:VC["kg"] + 1], kf[:, i, 0:NT], ALU.mult, ALU.mult,
                              waits=[e3], sig=True)
                    P.release(b2, es_)
                    es_ = cp("dve", kT[:, i, kcol0:kcol0 + NT], kf[:, i, 0:NT], sig=True)
                if i == 3:
                    qk_state["stt"] = es_
                if gi == 1 and i == 3 and kv_out:
                    for blk in range(nblk):
                        rows = min(128, NT - blk * 128)
                        bo, bwo = P.alloc()
                        lp = None
                        for c in range(4):
                            lp = tr(ps[0:rows, bo, c * 128:(c + 1) * 128], kf[:, c, blk * 128:blk * 128 + rows], ident[:, :],
                                    waits=[es_] + bwo if c == 0 else (), sig=(c == 3))
                        sslot, sw_ = stage_alloc()
                        ee = cp("act", stage[0:rows, sslot, 0:512], ps[0:rows, bo, :], waits=[lp] + sw_, sig=True)
                        P.release(bo, ee)
                        dd = S.dma("sp", lambda e_, sslot=sslot, rows=rows, blk=blk: e_.dma_start(
                            out=okd[blk * 128:blk * 128 + rows, :], in_=stage[0:rows, sslot, 0:512]), "sts%d" % sslot, [ee])
                        stage_free[sslot] = [dd]
                    qk_state["stt"] = S.last("pe")

            sv_pending = []

            def evac_l1(gi, i, b, pe_evt):
                if gi in (0, 1):
                    return headnorm(gi, i, b, pe_evt)
                elif gi == 2:
                    rows = min(128, NT - i * 128)
                    vslot = (Q0 + i) % 8
                    e = cp("act", Vb[0:rows, vslot, :], ps[0:rows, b, :], waits=[pe_evt], sig=True)
                    if kv_out:
                        sslot, sw_ = stage_alloc()
                        e = cp("dve", stage[0:rows, sslot, 0:512], ps[0:rows, b, :], waits=[pe_evt, e] + sw_, sig=True)
                        dd = S.dma("sp", lambda e_, sslot=sslot, rows=rows, i=i: e_.dma_start(
                            out=ovd[i * 128:i * 128 + rows, :], in_=stage[0:rows, sslot, 0:512]), "sts%d" % sslot, [e])
                        stage_free[sslot] = [dd]
                        P.release(b, e, S.last("act"))
                    else:
                        P.release(b, e)
                elif gi == 3:
                    e = cp("dve", uf[:, i, 0:NT], ps[:, b, 0:NT], waits=[pe_evt], sig=True)
                    P.release(b, e)
                else:
                    sv_pending.append((i, b, pe_evt))
                    if i < nblk - 1:
                        return None
                    strict0 = S.strict
                    S.strict = True
                    ea = None
                    for (tb, bb, pev) in sv_pending:
                        rows = min(128, NT - tb * 128)
                        S.add("dve", lambda e_, tb=tb, bb=bb, rows=rows: e_.bn_stats(out=st6[0:rows, tb, :], in_=ps[0:rows, bb, :]),
                              waits=[pev] + (S.join("act") if tb == 0 else []))
                        ea = S.add("dve", lambda e_, tb=tb, rows=rows: e_.bn_aggr(out=mv[0:rows, tb, :], in_=st6[0:rows, tb, :]), sig=True)
                    r0 = min(128, NT)
                    act(rsd[0:r0, 0:nblk], mv[0:r0, 0:nblk, 1], AF.Ln, waits=[ea], bias=eps5[0:r0, :])
                    eb_ = act(rsd[0:r0, 0:nblk], rsd[0:r0, 0:nblk], AF.Exp, scale=-0.5, sig=True)
                    S.strict = strict0
                    cast_evt = {}
                    for (tb, bb, pev) in sv_pending:
                        rows = min(128, NT - tb * 128)
                        tsl = tb % 2
                        e = ts("dve", tsv[0:rows, tsl, :], ps[0:rows, bb, :], mv[0:rows, tb, 0:1], rsd[0:rows, tb:tb + 1],
                               ALU.subtract, ALU.mult, waits=[eb_, cast_evt.get(tb - 2)], sig=True)
                        P.release(bb, e)
                        tt("dve", tsv[0:rows, tsl, :], tsv[0:rows, tsl, :], slg_bc[0:rows, :], ALU.mult)
                        e = tt("dve", tsv[0:rows, tsl, :], tsv[0:rows, tsl, :], slb_bc[0:rows, :], ALU.add, sig=True)
                        cast_evt[tb] = cp("act", svn[0:rows, tb, :], tsv[0:rows, tsl, :], waits=[e], sig=True)
                        if is_sample:
                            out_evts.append(S.dma("sp", lambda e_, rows=rows, tsl=tsl: e_.dma_start(out=svs[:, :], in_=tsv[0:rows, tsl, :]),
                                                  "outs", [e]))
                return None

            gemm(groups, 8, lambda kc: hT[:, kc, :], NT, [], evac_l1, kc_evts=h_ready)

            if stage_lim < 7:
                return
            sv_ready = S.join("act", "dve")
            for g in range(4):
                b, bw = P.alloc()
                lp = None
                for tb in range(nblk):
                    rows = min(128, NT - tb * 128)
                    mm(ps[:, b, tb * 128:tb * 128 + rows], svn[0:rows, tb, g * 128:(g + 1) * 128], wsT[0:rows, g, 0:rows], True, False,
                       waits=(sv_ready + bw + [pool_c2]) if tb == 0 else ())
                    lp = mm(ps[:, b, tb * 128:tb * 128 + rows], ones1[0:1, :], bsrow[0:1, g * 128:g * 128 + rows], False, True,
                            sig=(tb == nblk - 1))
                e = stt(cat[:, 4 + g, 0:NT], ps[:, b, 0:NT], 1.0, uf[:, g, 0:NT], ALU.mult, ALU.mult, waits=[lp], sig=True)
                P.release(b, e)

            if stage_lim < 8:
                return
            att_state = {}
            att_in_ready = S.join("act", "dve")
            tbuf_rd = [None, None]
            pv_done = {}
            norm_done = {}
            def att_scores(qb):
                    ntq = min(128, NT - qb * 128)
                    Q = Q0 + qb
                    jbs = []
                    for jb in range(5):
                        kb = Q - 4 + jb
                        if (not is_sample) and kb < 0:
                            continue
                        nkeys = NS if (is_sample and jb == 4) else 128
                        jbs.append((jb, kb % 8, nkeys))
                    ebuf = qb % 2
                    for ji, (jb, slot, nkeys) in enumerate(jbs):
                        bA, wA = P.alloc()
                        bB, wB = P.alloc()
                        lp = None
                        for h in range(8):
                            c, par = h // 2, h % 2
                            pb = par * 64
                            bk = bB if par else bA
                            lp = mm(ps[0:nkeys, bk, c * 128:c * 128 + ntq], kT[pb:pb + 64, c, slot * 128:slot * 128 + nkeys],
                                    qT[pb:pb + 64, c, qb * 128:qb * 128 + ntq], True, True,
                                    waits=(wA + wB + att_in_ready + [cvl, late["tb_ready"]]) if h == 0 else (), sig=(h == 7))
                        tsl = ji % 2
                        ed = None
                        for par, bk in ((0, bA), (1, bB)):
                            ed = stt(tbuf[0:nkeys, tsl, par * 4:(par + 1) * 4, 0:ntq],
                                     ps[0:nkeys, bk, :].rearrange("p (a b) -> p a b", a=4)[:, :, 0:ntq], 0.125,
                                     Tb[0:nkeys, jb, par * 4:(par + 1) * 4, 0:ntq], ALU.mult, ALU.add,
                                     waits=[lp, tbuf_rd[tsl]] if par == 0 else (), sig=(par == 1))
                        P.release(bA, ed)
                        P.release(bB, ed)
                        tbuf_rd[tsl] = act(Eb[0:nkeys, ebuf, ji, :].rearrange("p (a b) -> p a b", a=8)[:, :, 0:ntq],
                                           tbuf[0:nkeys, tsl, :, 0:ntq], AF.Exp, waits=[ed, pv_done.get(qb - 2)], sig=True)
                    e_ready = S.last("act")
                    att_state[qb] = (ntq, jbs, ebuf, e_ready)
            def att_pv(qb):
                    ntq, jbs, ebuf, e_ready = att_state[qb]
                    dbs = []
                    for par in range(2):
                        bd, wd_ = P.alloc()
                        lp = None
                        for ji, (jb, slot, nkeys) in enumerate(jbs):
                            lp = mm(ps[:, bd, 0:4 * ntq], ones1[0:nkeys, :],
                                    Eb[0:nkeys, ebuf, ji, :].rearrange("p (a b) -> p a b", a=8)[:, par * 4:(par + 1) * 4, 0:ntq],
                                    ji == 0, ji == len(jbs) - 1, waits=[e_ready] + wd_ if ji == 0 else (), sig=(ji == len(jbs) - 1))
                        act(rden[:, par, 0:4 * ntq], ps[:, bd, 0:4 * ntq], AF.Ln, waits=[lp, norm_done.get(qb - 1)])
                        e = act(rden[:, par, 0:4 * ntq], rden[:, par, 0:4 * ntq], AF.Exp, scale=-1.0, sig=True)
                        P.release(bd, e)
                    rd_ready = S.last("act")
                    bo, wo = P.alloc()
                    lp = None
                    for h in range(8):
                        c, par = h // 2, h % 2
                        pb = par * 64
                        for ji, (jb, slot, nkeys) in enumerate(jbs):
                            lp = mm(ps[pb:pb + 64, bo, c * 128:c * 128 + ntq], Vb[0:nkeys, slot, h * 64:(h + 1) * 64],
                                    Eb[0:nkeys, ebuf, ji, :].rearrange("p (a b) -> p a b", a=8)[:, par * 4 + c, 0:ntq],
                                    ji == 0, ji == len(jbs) - 1, waits=wo if (h == 0 and ji == 0) else (),
                                    sig=(h == 7 and ji == len(jbs) - 1))
                    e = None
                    for par in range(2):
                        pb = par * 64
                        e = stt(cat[pb:pb + 64, 0:4, qb * 128:qb * 128 + ntq],
                               ps[pb:pb + 64, bo, :].rearrange("p (a b) -> p a b", a=4)[:, :, 0:ntq], 1.0,
                               rden[pb:pb + 64, par, 0:4 * ntq].rearrange("p (a b) -> p a b", a=4), ALU.mult, ALU.mult,
                               waits=[lp, rd_ready] if par == 0 else (), sig=(par == 1))
                    P.release(bo, e)
                    pv_done[qb] = lp
                    norm_done[qb] = e
            if nblk > 0:
                att_scores(0)
            for qb in range(nblk):
                if qb + 1 < nblk:
                    att_scores(qb + 1)
                att_pv(qb)

            if stage_lim < 9:
                return
            wvo1 = wview(woo)
            groups = [([(wvo1, g * 512, 512)], "ws") for g in range(2)]
            gemm(groups, 8, lambda kc: cat[:, kc, :], NT, S.join("act", "dve", "pool"), residual_evac)
            if is_sample and n_prompt_tiles > 0:
                nxt = (False, 0)
            elif (not is_sample) and ti + 1 < n_prompt_tiles:
                nxt = (False, ti + 1)
            else:
                nxt = None
            if nxt is not None:
                pw = S.join("pe", "act", "dve")
                for pblk in range(4):
                    r0 = nxt[1] * TT + pblk * 128
                    ld = S.dma("sp", lambda e_, pblk=pblk, r0=r0: e_.dma_start(out=xpre[:, pblk, :], in_=xp[r0:r0 + 128, :]),
                               "xpre", pw if pblk == 0 else ())
                    prefetch[(nxt[0], nxt[1], pblk)] = ld
                for pblk in range(4):
                    prefetch[(nxt[0], nxt[1], pblk)] = ld
            xw = ffn(1, S.join("dve"))

            del y_dmas[:]
            for blk in range(nblk):
                rows = min(128, NT - blk * 128)
                ee = None
                for half in range(2):
                    b, bw = P.alloc()
                    lp = None
                    for c4 in range(4):
                        c = half * 4 + c4
                        lp = tr(ps[0:rows, b, c4 * 128:(c4 + 1) * 128], xT[:, c, blk * 128:blk * 128 + rows], ident[:, :],
                                waits=(list(xw) + bw) if c4 == 0 else (), sig=(c4 == 3))
                    ee = cp("act", ystage[0:rows, blk, half * 512:(half + 1) * 512], ps[0:rows, b, :], waits=[lp], sig=True)
                    P.release(b, ee)
                y_dmas.append(S.dma("sp", lambda e_, rows=rows, blk=blk: e_.dma_start(
                    out=ydst[t0 + blk * 128:t0 + blk * 128 + rows, :], in_=ystage[0:rows, blk, :]), "ysem", [ee]))

        S.strict = True
        if do_sample:
            process_tile(True, 0)
        for ti in range(n_prompt_tiles):
            process_tile(False, ti)
        if stage_lim < 99:
            jw = S.join()
            out_evts.append(S.dma("sp", lambda e_: e_.dma_start(out=dbg_x[:, :], in_=xT[:, :, :].rearrange("p a b -> p (a b)")), "outs", jw))
            out_evts.append(S.dma("pool", lambda e_: e_.dma_start(out=dbg_c[:, :], in_=cat[:, :, :].rearrange("p a b -> p (a b)")), "outs", jw))
            out_evts.append(S.dma("pool", lambda e_: e_.dma_start(out=dbg_h[:, :], in_=hT[:, :, :].rearrange("p a b -> p (a b)")), "outs", jw))

        final_waits = [(k, S.dmac[k]) for k in ("sts0", "sts1", "outs", "ysem") if S.dmac.get(k, 0) > 0]
        S.add("sp", lambda e: e.nop(), waits=final_waits + S.join())

        handles = {}
        with nc.Block() as block:
            def emit(eng_handle, eng):
                seen = {}
                for fn, waits, sigkey, inc in S.ops[eng]:
                    for (k, v) in waits:
                        if seen.get(k, 0) >= v:
                            continue
                        eng_handle.wait_ge(sems[k], v)
                        seen[k] = v
                    ins = fn(eng_handle)
                    if sigkey is not None:
                        ins.then_inc(sems[sigkey], inc)

            @block.tensor
            def _(e):
                emit(e, "pe")

            @block.scalar
            def _(e):
                emit(e, "act")

            @block.vector
            def _(e):
                emit(e, "dve")

            @block.gpsimd
            def _(e):
                emit(e, "pool")

            @block.sync
            def _(e):
                emit(e, "sp")
    return nc


_CACHE = {}


def kernel(x_prompt, x_sample, cache_conv_a, cache_conv_b, cache_k, cache_v,
           norm_mix_even, w_in_even, conv_a_w, conv_a_b, ln_a_g, ln_a_b, conv_b_w, w_out_even,
           norm_mix_odd, w_in_odd, q_norm_g, k_norm_g, rel_bias, sgu_ln_g, sgu_ln_b, sgu_w, sgu_b,
           w_out_odd, norm_ffn, w_gate_up, w_down):
    npt = NPT
    f = lambda a: np.ascontiguousarray(np.asarray(a, dtype=np.float32))
    key = npt
    if key not in _CACHE:
        _CACHE[key] = build_program(npt, True)
    nc = _CACHE[key]
    shared = {
        "nme": f(norm_mix_even[0]), "wie": f(w_in_even[0]), "caw": f(conv_a_w[0]), "cab": f(conv_a_b[0]),
        "lag": f(ln_a_g[0]), "lab": f(ln_a_b[0]), "cbw": f(conv_b_w[0]), "woe": f(w_out_even[0]),
        "nmo": f(norm_mix_odd[0]), "wio": f(w_in_odd[0]), "qg": f(q_norm_g[0]), "kg": f(k_norm_g[0]),
        "rb": f(rel_bias[0]), "slg": f(sgu_ln_g[0]), "slb": f(sgu_ln_b[0]), "sw": f(sgu_w[0]), "sb": f(sgu_b[0]),
        "woo": f(w_out_odd[0]), "nf": f(norm_ffn), "wgu": f(w_gate_up), "wd": f(w_down),
    }
    xpn, xsn = np.asarray(x_prompt), np.asarray(x_sample)
    cca, ccb = np.asarray(cache_conv_a), np.asarray(cache_conv_b)
    ckn, cvn = np.asarray(cache_k), np.asarray(cache_v)
    in_maps = []
    for b in range(8):
        m = dict(shared)
        m["xp"] = f(xpn[b]); m["xs"] = f(xsn[b])
        m["cca"] = f(cca[0, b]); m["ccb"] = f(ccb[0, b])
        m["ck"] = f(ckn[0, b].reshape(512, 512)); m["cv"] = f(cvn[0, b].reshape(512, 512))
        in_maps.append(m)
    res = run_bass_kernel_spmd(nc, in_maps, core_ids=list(range(8)))
    r = res.results
    st = lambda k: np.stack([np.asarray(r[b][k], dtype=np.float32) for b in range(8)])
    y_p = st("yp"); y_s = st("ys")
    ca_p = st("cap")[None]; cb_p = st("cbp")[None]
    k_p = st("kp").reshape(8, 512, 8, 64)[None]; v_p = st("vp").reshape(8, 512, 8, 64)[None]
    ca_s = st("cas")[None]; cb_s = st("cbs")[None]
    k_s = st("ks").reshape(8, NS, 8, 64)[None]; v_s = st("vs").reshape(8, NS, 8, 64)[None]
    sv_s = st("svs")[None]
    return (y_p, y_s, ca_p, cb_p, k_p, v_p, ca_s, cb_s, k_s, v_s, sv_s)
```

```python
import os
from contextlib import ExitStack

import numpy as np
import concourse.bass as bass
import concourse.mybir as mybir
from concourse.bass_utils import run_bass_kernel_spmd

F32 = mybir.dt.float32
BF16 = mybir.dt.bfloat16
AF = mybir.ActivationFunctionType
ALU = mybir.AluOpType

D = 1024
TT = 512
SEQ = 4096
NS = 16
NPT = SEQ // TT
R_SLOTS = 4
NEG = -30000.0

ENG = ("pe", "act", "dve", "pool", "sp")


def _ap_rng(ap):
    esz = 4 if ap.dtype == F32 else 2
    pairs = ap.ap
    pstep, pcnt = pairs[0]
    off = ap.offset
    if pstep > 0:
        p0, within = off // pstep, off % pstep
    else:
        p0, within = 0, off
    lo = hi = within
    for st, cnt in pairs[1:]:
        if st >= 0:
            hi += st * (cnt - 1)
        else:
            lo += st * (cnt - 1)
    return (ap.tensor.name, p0, p0 + pcnt, lo * esz, (hi + 1) * esz)


def _ovl(a, b):
    return a[0] == b[0] and a[1] < b[2] and b[1] < a[2] and a[3] < b[4] and b[3] < a[4]


class Sched:
    def __init__(self):
        self.ops = {e: [] for e in ENG}
        self.ms = {e: 0 for e in ENG}
        self.dmac = {}
        self.strict = False
        self.hist = {e: [] for e in ("act", "dve", "pool")}

    def add(self, eng, fn, waits=(), sig=False, rd=None, wr=None):
        w = [x for x in waits if x is not None]
        if eng in self.hist:
            sig = True
            h = self.hist[eng]
            if self.strict or rd is None or wr is None:
                if self.ms[eng] > 0:
                    w.append((eng, self.ms[eng]))
                del h[:]
                rds, wrs = None, None
            else:
                rds = [_ap_rng(a) for a in rd]
                wrs = [_ap_rng(a) for a in wr]
                for k in range(len(h) - 1, -1, -1):
                    ev, prd, pwr = h[k]
                    hit = prd is None
                    if not hit:
                        for x in wrs:
                            if any(_ovl(x, y) for y in prd) or any(_ovl(x, y) for y in pwr):
                                hit = True
                                break
                    if not hit:
                        for x in rds:
                            if any(_ovl(x, y) for y in pwr):
                                hit = True
                                break
                    if hit:
                        w.append(ev)
                        del h[:k + 1]
                        break
                if len(h) > 96:
                    del h[:len(h) - 96]
        self.ops[eng].append((fn, w, eng if sig else None, 1))
        if sig:
            self.ms[eng] += 1
            evt = (eng, self.ms[eng])
            if eng in self.hist:
                self.hist[eng].append((evt, rds, wrs))
            return evt
        return None

    def dma(self, eng, fn, semkey, waits=()):
        w = [x for x in waits if x is not None]
        self.ops[eng].append((fn, w, semkey, 16))
        self.dmac[semkey] = self.dmac.get(semkey, 0) + 16
        return (semkey, self.dmac[semkey])

    def last(self, eng):
        return (eng, self.ms[eng]) if self.ms[eng] > 0 else None

    def join(self, *engs):
        return [self.last(e) for e in (engs or ("pe", "act", "dve", "pool"))]


class Psum:
    def __init__(self):
        self.free = [[] for _ in range(8)]
        self.busy = [False] * 8
        self.nxt = 0

    def alloc(self):
        for k in range(8):
            b = (self.nxt + k) % 8
            if not self.busy[b]:
                break
        else:
            raise RuntimeError("PSUM: all 8 banks busy")
        self.nxt = (b + 1) % 8
        self.busy[b] = True
        w = self.free[b]
        self.free[b] = []
        return b, list(w)

    def release(self, b, *evts):
        self.free[b] = [e for e in evts if e is not None]
        self.busy[b] = False


def build_program(n_prompt_tiles=NPT, do_sample=True, stage_lim=99):
    nc = bass.Bass("TRN2", target_bir_lowering=False)
    S = Sched()
    P = Psum()

    def din(name, shape):
        return nc.dram_tensor(name, list(shape), F32, kind="ExternalInput").ap()

    def dout(name, shape):
        return nc.dram_tensor(name, list(shape), F32, kind="ExternalOutput").ap()

    xp = din("xp", [SEQ, D]); xs = din("xs", [NS, D])
    cca = din("cca", [30, 512]); ccb = din("ccb", [2, 512])
    ck = din("ck", [512, 512]); cv = din("cv", [512, 512])
    nme = din("nme", [D]); wie = din("wie", [D, 2560]); caw = din("caw", [31, 512])
    cab = din("cab", [512]); lag = din("lag", [512]); lab = din("lab", [512])
    cbw = din("cbw", [3, 512]); woe = din("woe", [D, D]); nmo = din("nmo", [D])
    wio = din("wio", [D, 2560]); qg = din("qg", [64]); kg = din("kg", [64])
    rb = din("rb", [8, 513]); slg = din("slg", [512]); slb = din("slb", [512])
    sw = din("sw", [4, 128, 128]); sb = din("sb", [4, 128]); woo = din("woo", [D, D])
    nf = din("nf", [2, D]); wgu = din("wgu", [2, D, 5632]); wd = din("wd", [2, 2816, D])

    yp = dout("yp", [SEQ, D]); ys = dout("ys", [NS, D])
    cap = dout("cap", [30, 512]); cbp = dout("cbp", [2, 512])
    kp = dout("kp", [512, 512]); vp = dout("vp", [512, 512])
    cas = dout("cas", [30, 512]); cbs = dout("cbs", [2, 512])
    ks = dout("ks", [NS, 512]); vs = dout("vs", [NS, 512]); svs = dout("svs", [NS, 512])

    if stage_lim < 99:
        dbg_x = dout("dbg_x", [128, 8 * TT]); dbg_c = dout("dbg_c", [128, 8 * TT]); dbg_h = dout("dbg_h", [128, 8 * TT])
    ext_h = nc.dram_tensor("ext_scratch", [8, 128 * 769], F32, kind="Internal")
    ext = ext_h.ap()

    with ExitStack() as es:
        def sb_t(name, shape, dt):
            return es.enter_context(nc.sbuf_tensor(name, list(shape), dt))

        ps = es.enter_context(nc.psum_tensor("ps", [128, 8, 512], F32))
        xT = sb_t("xT", [128, 8, TT], F32)
        hT = sb_t("hT", [128, 8, TT], BF16)
        cat = hT
        ring = sb_t("ring", [128, R_SLOTS, 8, 512], BF16)
        Tb = sb_t("Tb", [128, 5, 8, 128], BF16)
        ident = sb_t("ident", [128, 128], F32)
        o1024 = sb_t("o1024", [128, 128], BF16)
        o512 = sb_t("o512", [128, 128], BF16)
        blk64 = sb_t("blk64", [128, 128], BF16)
        ones1 = sb_t("ones1", [128, 128], BF16)
        vecT = sb_t("vecT", [128, 64], F32)
        cawT = sb_t("cawT", [128, 124], F32)
        wsT = sb_t("wsT", [128, 4, 128], BF16)
        bsrow = sb_t("bsrow", [1, 512], BF16)
        dg = sb_t("dg", [128, 124, 128], BF16)
        slg_bc = sb_t("slg_bc", [128, 512], F32)
        slb_bc = sb_t("slb_bc", [128, 512], F32)
        eps6 = sb_t("eps6", [128, 1], F32)
        eps5 = sb_t("eps5", [128, 1], F32)
        kT = sb_t("kT", [128, 4, 1024], BF16)
        Vb = sb_t("Vb", [128, 8, 512], BF16)
        hist_a = sb_t("hist_a", [128, 4, 30], F32)
        hist_b = sb_t("hist_b", [128, 4, 2], F32)
        stage = sb_t("stage", [128, 2, 1024], F32)
        rstd = sb_t("rstd", [128, TT], F32)
        st6 = sb_t("st6", [128, 4, 6], F32)
        mv = sb_t("mv", [128, 4, 2], F32)
        rsd = sb_t("rsd", [128, 4], F32)
        U = sb_t("U", [128, 18432], F32)
        Ub = U.bitcast(BF16)

        def uf32(off, *shape):
            n = int(np.prod(shape))
            v = U[:, off:off + n]
            if len(shape) == 2:
                v = v.rearrange("p (a b) -> p a b", a=shape[0])
            elif len(shape) == 3:
                v = v.rearrange("p (a b c) -> p a b c", a=shape[0], b=shape[1])
            return v

        def ubf(off32, *shape):
            n = int(np.prod(shape))
            v = Ub[:, 2 * off32:2 * off32 + n]
            if len(shape) == 2:
                v = v.rearrange("p (a b) -> p a b", a=shape[0])
            elif len(shape) == 3:
                v = v.rearrange("p (a b c) -> p a b c", a=shape[0], b=shape[1])
            return v

        abuf = uf32(0, 4, 544)
        zbuf = uf32(2176, 4, 516)
        gbb = uf32(4240, 4, 512)
        sgs = uf32(6288, 4, 512)
        acv = uf32(8336, 4, 512)
        xbf = ubf(10384, 4, 512)
        sqb0 = ubf(11408, 4, 512)
        hid = ubf(0, 22, 512)
        stmp = uf32(5632, 2, 512)
        qf = uf32(0, 4, 512)
        sqb1 = ubf(2048, 4, 512)
        kf = uf32(3072, 4, 512)
        qT = ubf(5120, 4, 512)
        uf = uf32(6144, 4, 512)
        tsv = uf32(8192, 2, 512)
        svn = ubf(9216, 4, 512)
        tbuf = uf32(10240, 2, 8, 128)
        Eb = ubf(12288, 2, 5, 1024)
        rden = uf32(17408, 2, 512)
        T32 = uf32(0, 5, 1024)
        VbF = Vb.bitcast(F32)
        extS = VbF[0:8, 5:8, :].rearrange("p a b -> p (a b)")
        bsrow32 = U[0:1, 10000:10512]
        abuf16 = ubf(12432, 4, 544)
        lnt = uf32(13520, 4, 512)

        sems = {}
        for k in ("pe", "act", "dve", "pool"):
            sems[k] = es.enter_context(nc.semaphore("m_" + k))
        for k in ["cst", "cvl", "ex1", "tbl", "tb2", "ysem", "xpre", "outs", "stl0", "stl1", "sts0", "sts1"] + ["ring%d" % i for i in range(R_SLOTS)]:
            sems[k] = es.enter_context(nc.semaphore("d_" + k))

        def act(out, in_, func, waits=(), sig=False, scale=None, bias=None):
            def fn(e):
                kw = {}
                if scale is not None:
                    kw["scale"] = scale
                if bias is not None:
                    kw["bias"] = bias
                return e.activation(out=out, in_=in_, func=func, **kw)
            rd = [in_] + [x for x in (scale, bias) if x is not None and not isinstance(x, (int, float))]
            return S.add("act", fn, waits, sig, rd=rd, wr=[out])

        def tt(eng, out, in0, in1, op, waits=(), sig=False):
            return S.add(eng, lambda e: e.tensor_tensor(out=out, in0=in0, in1=in1, op=op), waits, sig, rd=[in0, in1], wr=[out])

        def stt(out, in0, scalar, in1, op0, op1, waits=(), sig=False):
            rd = [in0, in1] + ([scalar] if not isinstance(scalar, (int, float)) else [])
            return S.add("dve", lambda e: e.scalar_tensor_tensor(out=out, in0=in0, scalar=scalar, in1=in1, op0=op0, op1=op1), waits, sig,
                         rd=rd, wr=[out])

        def ts(eng, out, in0, s1, s2, op0, op1=None, waits=(), sig=False):
            def fn(e):
                if op1 is None:
                    return e.tensor_scalar(out=out, in0=in0, scalar1=s1, scalar2=None, op0=op0)
                return e.tensor_scalar(out=out, in0=in0, scalar1=s1, scalar2=s2, op0=op0, op1=op1)
            rd = [in0] + [x for x in (s1, s2) if x is not None and not isinstance(x, (int, float))]
            return S.add(eng, fn, waits, sig, rd=rd, wr=[out])

        def cp(eng, out, in_, waits=(), sig=False):
            if eng == "act":
                return act(out, in_, AF.Copy, waits, sig)
            return S.add(eng, lambda e: e.tensor_copy(out=out, in_=in_), waits, sig, rd=[in_], wr=[out])

        def mm(out, lhsT, rhs, start, stop, waits=(), sig=False):
            return S.add("pe", lambda e: e.matmul(out, lhsT, rhs, start=start, stop=stop), waits, sig)

        def tr(out, in_, idn, waits=(), sig=False):
            return S.add("pe", lambda e: e.transpose(out, in_, idn), waits, sig)

        stage_free = [[], []]
        stage_n = [0]

        def stage_alloc():
            s = stage_n[0] % 2
            stage_n[0] += 1
            w = stage_free[s]
            stage_free[s] = []
            return s, list(w)

        ring_n = [0]
        ring_free = [[] for _ in range(R_SLOTS)]

        def ring_load(pieces):
            slot = ring_n[0] % R_SLOTS
            ring_n[0] += 1
            w = ring_free[slot]
            ring_free[slot] = []
            evt = None
            for i, (src, nkc, dcol, ncols) in enumerate(pieces):
                def fn(e, src=src, nkc=nkc, dcol=dcol, ncols=ncols, slot=slot):
                    return e.dma_start(out=ring[:, slot, 0:nkc, dcol:dcol + ncols], in_=src)
                evt = S.dma("pool", fn, "ring%d" % slot, w if i == 0 else ())
            return slot, evt

        def wview(w_ap):
            return w_ap.rearrange("(kc p) n -> p kc n", p=128)

        def gemm(groups, KC, in_chunk, NT, in_waits, evac, mode="ws", kc_evts=None):
            first = True
            deferred = []
            for gi, (pieces, gmode) in enumerate(groups):
                nslots = (KC + 7) // 8
                slot_info = []
                for si in range(nslots):
                    kc0 = si * 8
                    nkc = min(8, KC - kc0)
                    pl = []
                    dcol = 0
                    for (wv, c0, ncols) in pieces:
                        pl.append((wv[:, kc0:kc0 + nkc, c0:c0 + ncols], nkc, dcol, ncols))
                        dcol += ncols
                    slot_info.append((ring_load(pl), kc0, nkc))
                n_out = 4 if gmode == "ws" else (NT + 127) // 128
                banks = []
                for i in range(n_out):
                    banks.append(P.alloc())
                last_evt = [None] * n_out
                for si, ((slot, levt), kc0, nkc) in enumerate(slot_info):
                    for i in range(n_out):
                        b, bw = banks[i]
                        for k in range(nkc):
                            kc = kc0 + k
                            w = []
                            if k == 0:
                                w = [levt] + (bw if si == 0 else [])
                                if first:
                                    w += list(in_waits)
                                    first = False
                            if kc_evts is not None and gi == 0 and i == 0:
                                w = list(w) + [kc_evts[kc]]
                            is_last_k = (k == nkc - 1)
                            sig = is_last_k and (si == nslots - 1 or i == n_out - 1)
                            if gmode == "ws":
                                e = mm(ps[:, b, 0:NT], ring[:, slot, k, i * 128:(i + 1) * 128], in_chunk(kc)[:, 0:NT],
                                       kc == 0, kc == KC - 1, w, sig)
                            else:
                                rows = min(128, NT - i * 128)
                                e = mm(ps[0:rows, b, 0:512], in_chunk(kc)[:, i * 128:i * 128 + rows], ring[:, slot, k, 0:512],
                                       kc == 0, kc == KC - 1, w, sig)
                            if sig:
                                last_evt[i] = e
                    ring_free[slot] = [S.last("pe")]
                for d in deferred:
                    d()
                deferred = []
                for i in range(n_out):
                    r = evac(gi, i, banks[i][0], last_evt[i])
                    if r is not None:
                        deferred.append(r)
            for d in deferred:
                d()

        cst = []

        def cdma(out, in_, eng="sp", waits=()):
            e = S.dma(eng, lambda e_: e_.dma_start(out=out, in_=in_), "cst", waits)
            cst.append(e)
            return e

        VC = {}
        row = 0
        s0 = stage[:, 0, :]
        s1 = stage[:, 1, :]

        def vrows(name, ap2d, n):
            nonlocal row
            VC[name] = row
            cdma(s0[row:row + n, 0:128], ap2d)
            row += n

        vrows("nme", nme.rearrange("(c p) -> c p", p=128), 8)
        vrows("nmo", nmo.rearrange("(c p) -> c p", p=128), 8)
        vrows("nf", nf.rearrange("l (c p) -> (l c) p", p=128), 16)
        vrows("cab", cab.rearrange("(c p) -> c p", p=128), 4)
        vrows("lag", lag.rearrange("(c p) -> c p", p=128), 4)
        vrows("lab", lab.rearrange("(c p) -> c p", p=128), 4)
        vrows("cbw", cbw.rearrange("k (c p) -> (k c) p", p=128), 12)
        qg2 = qg.rearrange("(o d) -> o d", o=1)
        kg2 = kg.rearrange("(o d) -> o d", o=1)
        VC["qg"] = row
        cdma(s0[row:row + 1, 0:64], qg2)
        cdma(s0[row:row + 1, 64:128], qg2)
        row += 1
        VC["kg"] = row
        cdma(s0[row:row + 1, 0:64], kg2)
        cdma(s0[row:row + 1, 64:128], kg2)
        row += 1
        NV = row
        cdma(s0[0:124, 128:256], caw.rearrange("k (c p) -> (k c) p", p=128))
        for g in range(4):
            cdma(s0[:, 256 + g * 128:256 + (g + 1) * 128], sw[g])
        cdma(bsrow32[0:1, :], sb.rearrange("(o g) i -> o (g i)", o=1))
        cdma(slg_bc[:, :], bass.AP(tensor=slg.tensor, offset=0, ap=[[0, 128], [1, 512]]))
        cdma(slb_bc[:, :], bass.AP(tensor=slb.tensor, offset=0, ap=[[0, 128], [1, 512]]))
        W_ = 768
        GS = 128 * (W_ + 1)
        d0 = S.dma("sp", lambda e_: e_.dma_start(out=extS[:, 0:384], in_=rb[:, 129:513]), "ex1")
        ef = ts("dve", extS[:, 384:768], extS[:, 0:384], 0.0, extS[:, 383:384], ALU.mult, ALU.add, waits=[d0], sig=True)
        cst_all = [("cst", S.dmac["cst"])]
        late = {"done": False, "done2": False}

        def late_prologue():
            if late["done"]:
                return
            late["done"] = True
            strict0 = S.strict
            S.strict = False
            for j in range(124):
                ts("dve", dg[:, j, :], ident[:, :], cawT[:, j:j + 1], None, ALU.mult)
            late["dg_ready"] = [S.add("dve", lambda e: e.nop(), sig=True)]
            S.strict = strict0
            gl = []
            for h in range(8):
                gl.append(S.dma("act", lambda e_, h=h: e_.dma_start(
                    out=bass.AP(tensor=ext_h, offset=h * GS, ap=[[W_ + 1, 128], [1, 768]]),
                    in_=bass.AP(tensor=VbF, offset=h * 2048 + 1280, ap=[[2048, 1], [0, 128], [1, 768]])),
                    "tbl", [ef] if h == 0 else ()))
            late["tbl_done"] = gl[-1]

        def late_prologue2():
            if late["done2"]:
                return
            late["done2"] = True
            tbl_done = late["tbl_done"]
            last = None
            for jb in range(5):
                for par in range(2):
                    src = bass.AP(tensor=ext_h, offset=par * GS + 639 - jb * 128, ap=[[W_, 128], [2 * GS, 4], [1, 128]])
                    last = S.dma("pool", lambda e_, jb=jb, par=par, src=src: e_.dma_start(
                        out=Tb[:, jb, par * 4:(par + 1) * 4, :], in_=src), "tb2", [tbl_done] if (jb == 0 and par == 0) else ())
            S.add("pool", lambda e: e.memset(Tb[0:64, 0, :, 64:128], NEG), waits=[last])
            late["tb_ready"] = S.add("pool", lambda e: e.memset(Tb[64:128, 4, :, 0:64], NEG), sig=True)

        S.strict = True
        S.add("pool", lambda e: e.memset(ident[:, :], 1.0))
        S.add("pool", lambda e: e.affine_select(out=ident[:, :], in_=ident[:, :], pattern=[[-1, 128]],
                                                compare_op=ALU.is_equal, fill=0.0, base=0, channel_multiplier=1))
        S.add("pool", lambda e: e.memset(o1024[:, :], 1.0 / 1024))
        S.add("pool", lambda e: e.memset(o512[:, :], 1.0 / 512))
        S.add("pool", lambda e: e.memset(ones1[:, :], 1.0))
        S.add("pool", lambda e: e.memset(blk64[:, :], 0.0))
        S.add("pool", lambda e: e.memset(blk64[0:64, 0:64], 1.0 / 64))
        S.add("pool", lambda e: e.memset(blk64[64:128, 64:128], 1.0 / 64))
        S.add("pool", lambda e: e.memset(eps6[:, :], 1e-6))
        S.add("pool", lambda e: e.memset(eps5[:, :], 1e-5))
        S.add("pool", lambda e: e.memset(hist_a[:, :, :], 0.0))
        pool_c = S.add("pool", lambda e: e.memset(hist_b[:, :, :], 0.0), sig=True)
        S.strict = False

        b0, w0 = P.alloc()
        tr(ps[:, b0, 0:NV], s0[0:NV, 0:128], ident[0:NV, 0:NV], waits=cst_all + [pool_c] + w0)
        tr(ps[:, b0, 128:252], s0[0:124, 128:256], ident[0:124, 0:124])
        b1, w1 = P.alloc()
        for g in range(4):
            ev = tr(ps[:, b1, g * 128:(g + 1) * 128], s0[:, 256 + g * 128:256 + (g + 1) * 128], ident[:, :],
                    waits=w1 if g == 0 else (), sig=(g == 3))
        cp("dve", vecT[:, 0:NV], ps[:, b0, 0:NV], waits=[ev])
        cp("dve", cawT[:, :], ps[:, b0, 128:252])
        cp("dve", sgs[:, 0, :], ps[:, b1, :])
        ev2 = cp("dve", bsrow[0:1, :], bsrow32[0:1, :], sig=True)
        P.release(b0, ev2)
        P.release(b1, ev2)
        for g in range(4):
            S.add("pool", lambda e, g=g: e.affine_select(out=wsT[:, g, :], in_=sgs[:, 0, g * 128:(g + 1) * 128],
                                                         pattern=[[1, 128]], compare_op=ALU.is_ge, fill=0.0, base=0,
                                                         channel_multiplier=-1), waits=[ev2] if g == 0 else ())
        pool_c2 = S.add("pool", lambda e: e.nop(), sig=True)
        stage_free[0] = [ev]

        if do_sample:
            for half in range(2):
                sslot, sw_ = stage_alloc()
                ld = S.dma("sp", lambda e_, half=half, sslot=sslot: e_.dma_start(
                    out=stage[:, sslot, :].rearrange("p (a b) -> p a b", a=2),
                    in_=ck[half * 256:(half + 1) * 256, :].rearrange("(a p) n -> p a n", p=128)), "stl%d" % sslot, sw_)
                lastpe = None
                for a in range(2):
                    blk = half * 2 + a
                    b, bw = P.alloc()
                    for c in range(4):
                        lastpe = tr(ps[:, b, c * 128:(c + 1) * 128], stage[:, sslot, a * 512 + c * 128:a * 512 + (c + 1) * 128],
                                    ident[:, :], waits=[ld, ev2] + bw if c == 0 else (), sig=(c == 3))
                    e = cp("act", kT[:, :, blk * 128:(blk + 1) * 128], ps[:, b, :].rearrange("p (c t) -> p c t", c=4),
                           waits=[lastpe], sig=True)
                    P.release(b, e)
                stage_free[sslot] = [lastpe]
            cvl = S.dma("pool", lambda e_: e_.dma_start(out=Vb[:, 0:4, :], in_=cv.rearrange("(a p) n -> p a n", p=128)), "cvl")
        else:
            cvl = None

        out_evts = []
        prefetch = {}
        y_dmas = []
        ystage = uf32(0, 4, 1024)
        xpre = uf32(14336, 4, 1024)

        def process_tile(is_sample, ti):
            NT = NS if is_sample else TT
            nblk = (NT + 127) // 128
            xsrc = xs if is_sample else xp
            ydst = ys if is_sample else yp
            t0 = 0 if is_sample else ti * TT
            last_tile = (not is_sample) and (ti == n_prompt_tiles - 1)

            x_ready = []
            for blk in range(nblk):
                rows = min(128, NT - blk * 128)
                if (is_sample, ti, blk) in prefetch:
                    ld = prefetch.pop((is_sample, ti, blk))
                    xin = xpre[:, blk, :]
                    sslot = None
                else:
                    sslot, sw_ = stage_alloc()
                    ld = S.dma("sp", lambda e_, blk=blk, rows=rows, sslot=sslot: e_.dma_start(
                        out=stage[0:rows, sslot, :], in_=xsrc[t0 + blk * 128:t0 + blk * 128 + rows, :]), "stl%d" % sslot, sw_)
                    xin = stage[:, sslot, :]
                lastpe = None
                for half in range(2):
                    b, bw = P.alloc()
                    for c4 in range(4):
                        c = half * 4 + c4
                        lastpe = tr(ps[:, b, c4 * 128:c4 * 128 + rows], xin[0:rows, c * 128:(c + 1) * 128],
                                    ident[0:rows, 0:rows], waits=([ld] + bw + (S.join("act", "dve", "pool") + list(out_evts) if (blk == 0 and half == 0) else [])) if c4 == 0 else (), sig=(c4 == 3))
                    e = cp("dve", xT[:, half * 4:(half + 1) * 4, blk * 128:blk * 128 + rows],
                           ps[:, b, :].rearrange("p (c t) -> p c t", c=4)[:, :, 0:rows], waits=[lastpe], sig=True)
                    P.release(b, e)
                    x_ready = [e]
                if sslot is not None:
                    stage_free[sslot] = [lastpe]

            if stage_lim < 1:
                return
            def rmsnorm(gcol, x_waits, chunk_evts=None):
                hfree = S.join("pe")
                sq = []
                for c in range(8):
                    w = [chunk_evts[c]] if chunk_evts else []
                    if c == 0:
                        w += list(x_waits) + hfree
                    sq.append(act(hT[:, c, 0:NT], xT[:, c, 0:NT], AF.Square, waits=w, sig=True))
                b, bw = P.alloc()
                e2_ = None
                for c in range(8):
                    e2_ = mm(ps[:, b, 0:NT], o1024[:, :], hT[:, c, 0:NT], c == 0, c == 7, waits=[sq[c]] + (bw if c == 0 else []), sig=(c == 7))
                act(rstd[:, 0:NT], ps[:, b, 0:NT], AF.Ln, waits=[e2_], bias=eps6[:, :])
                e3_ = act(ps[:, b, 0:NT], rstd[:, 0:NT], AF.Exp, scale=-0.5, sig=True)
                hev = []
                for c in range(8):
                    hev.append(stt(hT[:, c, 0:NT], ps[:, b, 0:NT], vecT[:, gcol + c:gcol + c + 1], xT[:, c, 0:NT], ALU.mult, ALU.mult,
                                   waits=[e3_] if c == 0 else (), sig=True))
                P.release(b, hev[-1])
                return hev

            res_evt = {}

            def residual_evac(gi, i, b, pe_evt):
                c = gi * 4 + i
                e = stt(xT[:, c, 0:NT], ps[:, b, 0:NT], 1.0, xT[:, c, 0:NT], ALU.mult, ALU.add, waits=[pe_evt], sig=True)
                res_evt[c] = e
                P.release(b, e)

            def ffn(layer, x_waits):
                h_ready = rmsnorm(VC["nf"] + 8 * layer, [], dict(res_evt))
                if stage_lim < 5.3:
                    return S.join("dve")
                wg = wview(wgu[layer])
                groups = []
                for j in range(11):
                    groups.append(([(wg, j * 256, 256), (wg, 2816 + j * 256, 256)], "ws"))
                pend = {}

                def evac_gu(gi, i, b, pe_evt):
                    if i < 2:
                        e = act(stmp[:, i, 0:NT], ps[:, b, 0:NT], AF.Silu, waits=[pe_evt] + S.join("dve"), sig=True)
                        P.release(b, e)
                        pend[i] = e
                    else:
                        e = stt(hid[:, 2 * gi + (i - 2), 0:NT], ps[:, b, 0:NT], 1.0, stmp[:, i - 2, 0:NT], ALU.mult, ALU.mult,
                               waits=[pe_evt, pend[i - 2]], sig=True)
                        P.release(b, e)

                gemm(groups, 8, lambda kc: hT[:, kc, :], NT, [], evac_gu, kc_evts=h_ready)
                if stage_lim < 5.6:
                    return S.join("dve")
                wdv = wview(wd[layer])
                groups = [([(wdv, g * 512, 512)], "ws") for g in range(2)]
                gemm(groups, 22, lambda kc: hid[:, kc, :], NT, S.join("act", "dve"), residual_evac)
                return S.join("dve")

            h_ready = rmsnorm(VC["nme"], x_ready)
            first_tile = not late["done"]
            late_prologue()
            if stage_lim < 0.5:
                return
            if is_sample:
                sslot, sw_ = stage_alloc()
                ld = S.dma("sp", lambda e_, sslot=sslot: e_.dma_start(out=stage[0:30, sslot, 0:512], in_=cca[:, :]), "stl%d" % sslot, sw_)
                ld = S.dma("sp", lambda e_, sslot=sslot: e_.dma_start(out=stage[0:2, sslot, 512:1024], in_=ccb[:, :]), "stl%d" % sslot)
                b, bw = P.alloc()
                lastpe = None
                for c in range(4):
                    tr(ps[:, b, c * 32:c * 32 + 30], stage[0:30, sslot, c * 128:(c + 1) * 128], ident[0:30, 0:30],
                       waits=[ld] + bw + S.join("act", "dve", "pool") if c == 0 else ())
                    lastpe = tr(ps[:, b, 128 + c * 32:128 + c * 32 + 2], stage[0:2, sslot, 512 + c * 128:512 + (c + 1) * 128],
                                ident[0:2, 0:2], sig=(c == 3))
                cp("act", abuf[:, :, 0:30], ps[:, b, 0:128].rearrange("p (c t) -> p c t", c=4)[:, :, 0:30], waits=[lastpe])
                h16 = cp("act", abuf16[:, :, 0:30], ps[:, b, 0:128].rearrange("p (c t) -> p c t", c=4)[:, :, 0:30], sig=True)
                e = cp("act", zbuf[:, :, 0:2], ps[:, b, 128:256].rearrange("p (c t) -> p c t", c=4)[:, :, 0:2], sig=True)
                hz = e
                P.release(b, e)
                stage_free[sslot] = [lastpe]
            else:
                cp("act", abuf[:, :, 0:30], hist_a[:, :, :], waits=S.join("pe", "dve", "pool") + list(y_dmas))
                h16 = cp("act", abuf16[:, :, 0:30], hist_a[:, :, :], sig=True)
                hz = cp("act", zbuf[:, :, 0:2], hist_b[:, :, :], sig=True)

            if stage_lim < 0.7:
                return
            wv0 = wview(wie)
            l0_order = [0, 1, 3, 4, 2]
            groups = [([(wv0, g * 512, 512)], "ws") for g in l0_order]
            sig_evt = {}
            a16_evt = {}
            convA_state = {}

            def convA_all():
                for c in range(4):
                    b, bw = P.alloc()
                    lp = None
                    for k in range(31):
                        lp = mm(ps[:, b, 0:NT], dg[:, k * 4 + c, :], abuf16[:, c, k:k + NT], k == 0, k == 30,
                                waits=([a16_evt[c], h16] + bw + late["dg_ready"]) if k == 0 else (), sig=(k == 30))
                    cabc = vecT[:, VC["cab"] + c:VC["cab"] + c + 1]
                    act(acv[:, c, 0:NT], ps[:, b, 0:NT], AF.Identity, waits=[lp], bias=cabc)
                    act(xbf[:, c, 0:NT], ps[:, b, 0:NT], AF.Identity, bias=cabc)
                    e = act(sqb0[:, c, 0:NT], ps[:, b, 0:NT], AF.Square, bias=cabc, sig=True)
                    P.release(b, e)
                convA_state["done"] = S.last("act")

            def convB_taps():
                cb0 = VC["cbw"]
                for c in range(4):
                    acc2 = sgs[:, c, 0:NT]
                    ts("dve", acc2, zbuf[:, c, 0:NT], vecT[:, cb0 + c:cb0 + c + 1], None, ALU.mult, waits=[hz] if c == 0 else ())
                    stt(acc2, zbuf[:, c, 1:1 + NT], vecT[:, cb0 + 4 + c:cb0 + 4 + c + 1], acc2, ALU.mult, ALU.add)
                    stt(acc2, zbuf[:, c, 2:2 + NT], vecT[:, cb0 + 8 + c:cb0 + 8 + c + 1], acc2, ALU.mult, ALU.add)

            def evac_l0(gi, i, b, pe_evt):
                gi = l0_order[gi]
                if gi == 0:
                    e = cp("act", abuf[:, i, 30:30 + NT], ps[:, b, 0:NT], waits=[pe_evt], sig=True)
                    P.release(b, e)
                elif gi == 1:
                    e = act(sgs[:, i, 0:NT], ps[:, b, 0:NT], AF.Sigmoid, waits=[pe_evt], sig=True)
                    P.release(b, e)
                    tt("dve", abuf[:, i, 30:30 + NT], abuf[:, i, 30:30 + NT], sgs[:, i, 0:NT], ALU.mult, waits=[e])
                    a16_evt[i] = cp("dve", abuf16[:, i, 30:30 + NT], abuf[:, i, 30:30 + NT], sig=True)
                    if i == 3:
                        return convA_all
                elif gi == 2:
                    e = cp("act", gbb[:, i, 0:NT], ps[:, b, 0:NT], waits=[pe_evt], sig=True)
                    P.release(b, e)
                elif gi == 3:
                    e = cp("act", zbuf[:, i, 2:2 + NT], ps[:, b, 0:NT], waits=[pe_evt], sig=True)
                    P.release(b, e)
                    sig_evt[i] = e
                else:
                    e = stt(zbuf[:, i, 2:2 + NT], ps[:, b, 0:NT], 1.0, zbuf[:, i, 2:2 + NT], ALU.mult, ALU.mult,
                           waits=[pe_evt, sig_evt[i]], sig=True)
                    P.release(b, e)
                    if i == 3:
                        return convB_taps

            gemm(groups, 8, lambda kc: hT[:, kc, :], NT, S.join("act") + list(y_dmas), evac_l0, kc_evts=h_ready)

            if stage_lim < 2:
                return
            join0 = S.join("act", "pe", "pool", "dve")
            convA_done = convA_state["done"]
            for c in range(4):
                e = tt("dve", cat[:, 4 + c, 0:NT], gbb[:, c, 0:NT], sgs[:, c, 0:NT], ALU.mult, waits=join0 if c == 0 else (), sig=True)
            conv_done = S.last("dve")
            if not is_sample:
                cp("pool", hist_a[:, :, :], abuf[:, :, NT:NT + 30], waits=[conv_done] + S.join("act"))
                cp("pool", hist_b[:, :, :], zbuf[:, :, NT:NT + 2], sig=True)
            if is_sample or last_tile:
                oa, ob = (cas, cbs) if is_sample else (cap, cbp)
                b, bw = P.alloc()
                lastpe = None
                for c in range(4):
                    tr(ps[0:30, b, c * 128:(c + 1) * 128], abuf[:, c, NT:NT + 30], ident[:, :], waits=[conv_done] + bw if c == 0 else ())
                b2, bw2 = P.alloc()
                for c in range(4):
                    lastpe = tr(ps[0:2, b2, c * 128:(c + 1) * 128], zbuf[:, c, NT:NT + 2], ident[:, :], waits=bw2 if c == 0 else (), sig=(c == 3))
                sslot, sw_ = stage_alloc()
                cp("act", stage[0:30, sslot, 0:512], ps[0:30, b, :], waits=[lastpe] + sw_)
                e = cp("act", stage[0:2, sslot, 512:1024], ps[0:2, b2, :], sig=True)
                P.release(b, e)
                P.release(b2, e)
                d1 = S.dma("sp", lambda e_, sslot=sslot: e_.dma_start(out=oa[:, :], in_=stage[0:30, sslot, 0:512]), "sts%d" % sslot, [e])
                d2 = S.dma("sp", lambda e_, sslot=sslot: e_.dma_start(out=ob[:, :], in_=stage[0:2, sslot, 512:1024]), "sts%d" % sslot)
                stage_free[sslot] = [d2]
            if stage_lim < 3:
                return
            e1_ = convA_done
            bm, bwm = P.alloc()
            bq, bwq = P.alloc()
            for c in range(4):
                mm(ps[:, bm, 0:NT], o512[:, :], xbf[:, c, 0:NT], c == 0, c == 3, waits=[e1_] + bwm + bwq if c == 0 else ())
            e2_ = None
            for c in range(4):
                e2_ = mm(ps[:, bq, 0:NT], o512[:, :], sqb0[:, c, 0:NT], c == 0, c == 3, sig=(c == 3))
            e3_ = act(sgs[:, 0, 0:NT], ps[:, bm, 0:NT], AF.Square, waits=[e2_] + S.join("dve"), sig=True)
            e4_ = stt(sgs[:, 1, 0:NT], ps[:, bq, 0:NT], 1.0, sgs[:, 0, 0:NT], ALU.mult, ALU.subtract, waits=[e3_], sig=True)
            act(rstd[:, 0:NT], sgs[:, 1, 0:NT], AF.Ln, waits=[e4_], bias=eps5[:, :])
            e5_ = act(rstd[:, 0:NT], rstd[:, 0:NT], AF.Exp, scale=-0.5, sig=True)
            P.release(bq, e4_)
            e7_ = {}
            for c in range(4):
                t1 = lnt[:, c % 2, 0:NT]
                t2 = lnt[:, 2 + c % 2, 0:NT]
                stt(t1, ps[:, bm, 0:NT], -1.0, acv[:, c, 0:NT], ALU.mult, ALU.add, waits=[e5_, e7_.get(c - 2)])
                e6_ = tt("dve", t2, t1, rstd[:, 0:NT], ALU.mult, sig=True)
                e7_[c] = act(cat[:, c, 0:NT], t2, AF.Silu, waits=[e6_],
                             scale=vecT[:, VC["lag"] + c:VC["lag"] + c + 1], bias=vecT[:, VC["lab"] + c:VC["lab"] + c + 1], sig=True)
            P.release(bm, S.last("dve"))
            if stage_lim < 4:
                return
            wvo = wview(woe)
            groups = [([(wvo, g * 512, 512)], "ws") for g in range(2)]
            gemm(groups, 8, lambda kc: cat[:, kc, :], NT, S.join("act", "dve", "pool"), residual_evac)
            if stage_lim < 5:
                return
            xw = ffn(0, S.join("dve"))

            if stage_lim < 6:
                return
            late_prologue2()
            h_ready = rmsnorm(VC["nmo"], [], dict(res_evt))
            wv1 = wview(wio)
            groups = [([(wv1, 0, 512)], "ws"), ([(wv1, 512, 512)], "ws"), ([(wv1, 1024, 512)], "as"),
                      ([(wv1, 1536, 512)], "ws"), ([(wv1, 2048, 512)], "as")]
            Q0 = 4 if is_sample else 4 * ti
            kcol0 = (Q0 % 8) * 128
            qk_state = {"stt": None}
            qstat_evt = {}
            kv_out = is_sample or last_tile
            okd, ovd = (ks, vs) if is_sample else (kp, vp)

            def headnorm(gi, i, b, pe_evt):
                xf = qf if gi == 0 else kf
                cp("act", xf[:, i, 0:NT], ps[:, b, 0:NT], waits=[pe_evt, qstat_evt.get(i) if gi == 1 else None])
                e = act(sqb1[:, i, 0:NT], ps[:, b, 0:NT], AF.Square, sig=True)
                P.release(b, e)
                return lambda: headnorm2(gi, i, e)

            def headnorm2(gi, i, e):
                b2, bw2 = P.alloc()
                ep = mm(ps[:, b2, 0:NT], blk64[:, :], sqb1[:, i, 0:NT], True, True, waits=[e] + bw2, sig=True)
                if gi == 0:
                    qstat_evt[i] = ep
                act(rstd[:, 0:NT], ps[:, b2, 0:NT], AF.Ln, waits=[ep], bias=eps6[:, :])
                e3 = act(ps[:, b2, 0:NT], rstd[:, 0:NT], AF.Exp, scale=-0.5, sig=True)
                if gi == 0:
                    es_ = stt(qT[:, i, 0:NT], ps[:, b2, 0:NT], vecT[:, VC["qg"]:VC["qg"] + 1], qf[:, i, 0:NT], ALU.mult, ALU.mult,
                              waits=[e3], sig=True)
                    P.release(b2, es_)
                else:
                    es_ = stt(kf[:, i, 0:NT], ps[:, b2, 0:NT], vecT[:, VC["kg"]:VC["kg"] + 1], kf[:, i, 0:NT], ALU.mult, ALU.mult,
                              waits=[e3], sig=True)
                    P.release(b2, es_)
                    es_ = cp("dve", kT[:, i, kcol0:kcol0 + NT], kf[:, i, 0:NT], sig=True)
                if i == 3:
                    qk_state["stt"] = es_
                if gi == 1 and i == 3 and kv_out:
                    for blk in range(nblk):
                        rows = min(128, NT - blk * 128)
                        bo, bwo = P.alloc()
                        lp = None
                        for c in range(4):
                            lp = tr(ps[0:rows, bo, c * 128:(c + 1) * 128], kf[:, c, blk * 128:blk * 128 + rows], ident[:, :],
                                    waits=[es_] + bwo if c == 0 else (), sig=(c == 3))
                        sslot, sw_ = stage_alloc()
                        ee = cp("act", stage[0:rows, sslot, 0:512], ps[0:rows, bo, :], waits=[lp] + sw_, sig=True)
                        P.release(bo, ee)
                        dd = S.dma("sp", lambda e_, sslot=sslot, rows=rows, blk=blk: e_.dma_start(
                            out=okd[blk * 128:blk * 128 + rows, :], in_=stage[0:rows, sslot, 0:512]), "sts%d" % sslot, [ee])
                        stage_free[sslot] = [dd]
                    qk_state["stt"] = S.last("pe")

            sv_pending = []

            def evac_l1(gi, i, b, pe_evt):
                if gi in (0, 1):
                    return headnorm(gi, i, b, pe_evt)
                elif gi == 2:
                    rows = min(128, NT - i * 128)
                    vslot = (Q0 + i) % 8
                    e = cp("act", Vb[0:rows, vslot, :], ps[0:rows, b, :], waits=[pe_evt], sig=True)
                    if kv_out:
                        sslot, sw_ = stage_alloc()
                        e = cp("dve", stage[0:rows, sslot, 0:512], ps[0:rows, b, :], waits=[pe_evt, e] + sw_, sig=True)
                        dd = S.dma("sp", lambda e_, sslot=sslot, rows=rows, i=i: e_.dma_start(
                            out=ovd[i * 128:i * 128 + rows, :], in_=stage[0:rows, sslot, 0:512]), "sts%d" % sslot, [e])
                        stage_free[sslot] = [dd]
                        P.release(b, e, S.last("act"))
                    else:
                        P.release(b, e)
                elif gi == 3:
                    e = cp("dve", uf[:, i, 0:NT], ps[:, b, 0:NT], waits=[pe_evt], sig=True)
                    P.release(b, e)
                else:
                    sv_pending.append((i, b, pe_evt))
                    if i < nblk - 1:
                        return None
                    strict0 = S.strict
                    S.strict = True
                    ea = None
                    for (tb, bb, pev) in sv_pending:
                        rows = min(128, NT - tb * 128)
                        S.add("dve", lambda e_, tb=tb, bb=bb, rows=rows: e_.bn_stats(out=st6[0:rows, tb, :], in_=ps[0:rows, bb, :]),
                              waits=[pev] + (S.join("act") if tb == 0 else []))
                        ea = S.add("dve", lambda e_, tb=tb, rows=rows: e_.bn_aggr(out=mv[0:rows, tb, :], in_=st6[0:rows, tb, :]), sig=True)
                    r0 = min(128, NT)
                    act(rsd[0:r0, 0:nblk], mv[0:r0, 0:nblk, 1], AF.Ln, waits=[ea], bias=eps5[0:r0, :])
                    eb_ = act(rsd[0:r0, 0:nblk], rsd[0:r0, 0:nblk], AF.Exp, scale=-0.5, sig=True)
                    S.strict = strict0
                    cast_evt = {}
                    for (tb, bb, pev) in sv_pending:
                        rows = min(128, NT - tb * 128)
                        tsl = tb % 2
                        e = ts("dve", tsv[0:rows, tsl, :], ps[0:rows, bb, :], mv[0:rows, tb, 0:1], rsd[0:rows, tb:tb + 1],
                               ALU.subtract, ALU.mult, waits=[eb_, cast_evt.get(tb - 2)], sig=True)
                        P.release(bb, e)
                        tt("dve", tsv[0:rows, tsl, :], tsv[0:rows, tsl, :], slg_bc[0:rows, :], ALU.mult)
                        e = tt("dve", tsv[0:rows, tsl, :], tsv[0:rows, tsl, :], slb_bc[0:rows, :], ALU.add, sig=True)
                        cast_evt[tb] = cp("act", svn[0:rows, tb, :], tsv[0:rows, tsl, :], waits=[e], sig=True)
                        if is_sample:
                            out_evts.append(S.dma("sp", lambda e_, rows=rows, tsl=tsl: e_.dma_start(out=svs[:, :], in_=tsv[0:rows, tsl, :]),
                                                  "outs", [e]))
                return None

            gemm(groups, 8, lambda kc: hT[:, kc, :], NT, [], evac_l1, kc_evts=h_ready)

            if stage_lim < 7:
                return
            sv_ready = S.join("act", "dve")
            for g in range(4):
                b, bw = P.alloc()
                lp = None
                for tb in range(nblk):
                    rows = min(128, NT - tb * 128)
                    mm(ps[:, b, tb * 128:tb * 128 + rows], svn[0:rows, tb, g * 128:(g + 1) * 128], wsT[0:rows, g, 0:rows], True, False,
                       waits=(sv_ready + bw + [pool_c2]) if tb == 0 else ())
                    lp = mm(ps[:, b, tb * 128:tb * 128 + rows], ones1[0:1, :], bsrow[0:1, g * 128:g * 128 + rows], False, True,
                            sig=(tb == nblk - 1))
                e = stt(cat[:, 4 + g, 0:NT], ps[:, b, 0:NT], 1.0, uf[:, g, 0:NT], ALU.mult, ALU.mult, waits=[lp], sig=True)
                P.release(b, e)

            if stage_lim < 8:
                return
            att_state = {}
            att_in_ready = S.join("act", "dve")
            tbuf_rd = [None, None]
            pv_done = {}
            norm_done = {}
            def att_scores(qb):
                    ntq = min(128, NT - qb * 128)
                    Q = Q0 + qb
                    jbs = []
                    for jb in range(5):
                        kb = Q - 4 + jb
                        if (not is_sample) and kb < 0:
                            continue
                        nkeys = NS if (is_sample and jb == 4) else 128
                        jbs.append((jb, kb % 8, nkeys))
                    ebuf = qb % 2
                    for ji, (jb, slot, nkeys) in enumerate(jbs):
                        bA, wA = P.alloc()
                        bB, wB = P.alloc()
                        lp = None
                        for h in range(8):
                            c, par = h // 2, h % 2
                            pb = par * 64
                            bk = bB if par else bA
                            lp = mm(ps[0:nkeys, bk, c * 128:c * 128 + ntq], kT[pb:pb + 64, c, slot * 128:slot * 128 + nkeys],
                                    qT[pb:pb + 64, c, qb * 128:qb * 128 + ntq], True, True,
                                    waits=(wA + wB + att_in_ready + [cvl, late["tb_ready"]]) if h == 0 else (), sig=(h == 7))
                        tsl = ji % 2
                        ed = None
                        for par, bk in ((0, bA), (1, bB)):
                            ed = stt(tbuf[0:nkeys, tsl, par * 4:(par + 1) * 4, 0:ntq],
                                     ps[0:nkeys, bk, :].rearrange("p (a b) -> p a b", a=4)[:, :, 0:ntq], 0.125,
                                     Tb[0:nkeys, jb, par * 4:(par + 1) * 4, 0:ntq], ALU.mult, ALU.add,
                                     waits=[lp, tbuf_rd[tsl]] if par == 0 else (), sig=(par == 1))
                        P.release(bA, ed)
                        P.release(bB, ed)
                        tbuf_rd[tsl] = act(Eb[0:nkeys, ebuf, ji, :].rearrange("p (a b) -> p a b", a=8)[:, :, 0:ntq],
                                           tbuf[0:nkeys, tsl, :, 0:ntq], AF.Exp, waits=[ed, pv_done.get(qb - 2)], sig=True)
                    e_ready = S.last("act")
                    att_state[qb] = (ntq, jbs, ebuf, e_ready)
            def att_pv(qb):
                    ntq, jbs, ebuf, e_ready = att_state[qb]
                    dbs = []
                    for par in range(2):
                        bd, wd_ = P.alloc()
                        lp = None
                        for ji, (jb, slot, nkeys) in enumerate(jbs):
                            lp = mm(ps[:, bd, 0:4 * ntq], ones1[0:nkeys, :],
                                    Eb[0:nkeys, ebuf, ji, :].rearrange("p (a b) -> p a b", a=8)[:, par * 4:(par + 1) * 4, 0:ntq],
                                    ji == 0, ji == len(jbs) - 1, waits=[e_ready] + wd_ if ji == 0 else (), sig=(ji == len(jbs) - 1))
                        act(rden[:, par, 0:4 * ntq], ps[:, bd, 0:4 * ntq], AF.Ln, waits=[lp, norm_done.get(qb - 1)])
                        e = act(rden[:, par, 0:4 * ntq], rden[:, par, 0:4 * ntq], AF.Exp, scale=-1.0, sig=True)
                        P.release(bd, e)
                    rd_ready = S.last("act")
                    bo, wo = P.alloc()
                    lp = None
                    for h in range(8):
                        c, par = h // 2, h % 2
                        pb = par * 64
                        for ji, (jb, slot, nkeys) in enumerate(jbs):
                            lp = mm(ps[pb:pb + 64, bo, c * 128:c * 128 + ntq], Vb[0:nkeys, slot, h * 64:(h + 1) * 64],
                                    Eb[0:nkeys, ebuf, ji, :].rearrange("p (a b) -> p a b", a=8)[:, par * 4 + c, 0:ntq],
                                    ji == 0, ji == len(jbs) - 1, waits=wo if (h == 0 and ji == 0) else (),
                                    sig=(h == 7 and ji == len(jbs) - 1))
                    e = None
                    for par in range(2):
                        pb = par * 64
                        e = stt(cat[pb:pb + 64, 0:4, qb * 128:qb * 128 + ntq],
                               ps[pb:pb + 64, bo, :].rearrange("p (a b) -> p a b", a=4)[:, :, 0:ntq], 1.0,
                               rden[pb:pb + 64, par, 0:4 * ntq].rearrange("p (a b) -> p a b", a=4), ALU.mult, ALU.mult,
                               waits=[lp, rd_ready] if par == 0 else (), sig=(par == 1))
                    P.release(bo, e)
                    pv_done[qb] = lp
                    norm_done[qb] = e
            if nblk > 0:
                att_scores(0)
            for qb in range(nblk):
                if qb + 1 < nblk:
                    att_scores(qb + 1)
                att_pv(qb)

            if stage_lim < 9:
                return
            wvo1 = wview(woo)
            groups = [([(wvo1, g * 512, 512)], "ws") for g in range(2)]
            gemm(groups, 8, lambda kc: cat[:, kc, :], NT, S.join("act", "dve", "pool"), residual_evac)
            if is_sample and n_prompt_tiles > 0:
                nxt = (False, 0)
            elif (not is_sample) and ti + 1 < n_prompt_tiles:
                nxt = (False, ti + 1)
            else:
                nxt = None
            if nxt is not None:
                pw = S.join("pe", "act", "dve")
                for pblk in range(4):
                    r0 = nxt[1] * TT + pblk * 128
                    ld = S.dma("sp", lambda e_, pblk=pblk, r0=r0: e_.dma_start(out=xpre[:, pblk, :], in_=xp[r0:r0 + 128, :]),
                               "xpre", pw if pblk == 0 else ())
                    prefetch[(nxt[0], nxt[1], pblk)] = ld
                for pblk in range(4):
                    prefetch[(nxt[0], nxt[1], pblk)] = ld
            xw = ffn(1, S.join("dve"))

            del y_dmas[:]
            for blk in range(nblk):
                rows = min(128, NT - blk * 128)
                ee = None
                for half in range(2):
                    b, bw = P.alloc()
                    lp = None
                    for c4 in range(4):
                        c = half * 4 + c4
                        lp = tr(ps[0:rows, b, c4 * 128:(c4 + 1) * 128], xT[:, c, blk * 128:blk * 128 + rows], ident[:, :],
                                waits=(list(xw) + bw) if c4 == 0 else (), sig=(c4 == 3))
                    ee = cp("act", ystage[0:rows, blk, half * 512:(half + 1) * 512], ps[0:rows, b, :], waits=[lp], sig=True)
                    P.release(b, ee)
                y_dmas.append(S.dma("sp", lambda e_, rows=rows, blk=blk: e_.dma_start(
                    out=ydst[t0 + blk * 128:t0 + blk * 128 + rows, :], in_=ystage[0:rows, blk, :]), "ysem", [ee]))

        if do_sample:
            S.strict = True
            process_tile(True, 0)
            S.strict = False
        for ti in range(n_prompt_tiles):
            process_tile(False, ti)
        if stage_lim < 99:
            jw = S.join()
            out_evts.append(S.dma("sp", lambda e_: e_.dma_start(out=dbg_x[:, :], in_=xT[:, :, :].rearrange("p a b -> p (a b)")), "outs", jw))
            out_evts.append(S.dma("pool", lambda e_: e_.dma_start(out=dbg_c[:, :], in_=cat[:, :, :].rearrange("p a b -> p (a b)")), "outs", jw))
            out_evts.append(S.dma("pool", lambda e_: e_.dma_start(out=dbg_h[:, :], in_=hT[:, :, :].rearrange("p a b -> p (a b)")), "outs", jw))

        final_waits = [(k, S.dmac[k]) for k in ("sts0", "sts1", "outs", "ysem") if S.dmac.get(k, 0) > 0]
        S.add("sp", lambda e: e.nop(), waits=final_waits + S.join())

        handles = {}
        with nc.Block() as block:
            def emit(eng_handle, eng):
                seen = {}
                for fn, waits, sigkey, inc in S.ops[eng]:
                    wmax = {}
                    for (k, v) in waits:
                        wmax[k] = max(wmax.get(k, 0), v)
                    for (k, v) in wmax.items():
                        if seen.get(k, 0) >= v:
                            continue
                        eng_handle.wait_ge(sems[k], v)
                        seen[k] = v
                    ins = fn(eng_handle)
                    if sigkey is not None:
                        ins.then_inc(sems[sigkey], inc)

            @block.tensor
            def _(e):
                emit(e, "pe")

            @block.scalar
            def _(e):
                emit(e, "act")

            @block.vector
            def _(e):
                emit(e, "dve")

            @block.gpsimd
            def _(e):
                emit(e, "pool")

            @block.sync
            def _(e):
                emit(e, "sp")
    return nc


_CACHE = {}


def kernel(x_prompt, x_sample, cache_conv_a, cache_conv_b, cache_k, cache_v,
           norm_mix_even, w_in_even, conv_a_w, conv_a_b, ln_a_g, ln_a_b, conv_b_w, w_out_even,
           norm_mix_odd, w_in_odd, q_norm_g, k_norm_g, rel_bias, sgu_ln_g, sgu_ln_b, sgu_w, sgu_b,
           w_out_odd, norm_ffn, w_gate_up, w_down):
    npt = NPT
    f = lambda a: np.ascontiguousarray(np.asarray(a, dtype=np.float32))
    key = npt
    if key not in _CACHE:
        _CACHE[key] = build_program(npt, True)
    nc = _CACHE[key]
    shared = {
        "nme": f(norm_mix_even[0]), "wie": f(w_in_even[0]), "caw": f(conv_a_w[0]), "cab": f(conv_a_b[0]),
        "lag": f(ln_a_g[0]), "lab": f(ln_a_b[0]), "cbw": f(conv_b_w[0]), "woe": f(w_out_even[0]),
        "nmo": f(norm_mix_odd[0]), "wio": f(w_in_odd[0]), "qg": f(q_norm_g[0]), "kg": f(k_norm_g[0]),
        "rb": f(rel_bias[0]), "slg": f(sgu_ln_g[0]), "slb": f(sgu_ln_b[0]), "sw": f(sgu_w[0]), "sb": f(sgu_b[0]),
        "woo": f(w_out_odd[0]), "nf": f(norm_ffn), "wgu": f(w_gate_up), "wd": f(w_down),
    }
    xpn, xsn = np.asarray(x_prompt), np.asarray(x_sample)
    cca, ccb = np.asarray(cache_conv_a), np.asarray(cache_conv_b)
    ckn, cvn = np.asarray(cache_k), np.asarray(cache_v)
    in_maps = []
    for b in range(8):
        m = dict(shared)
        m["xp"] = f(xpn[b]); m["xs"] = f(xsn[b])
        m["cca"] = f(cca[0, b]); m["ccb"] = f(ccb[0, b])
        m["ck"] = f(ckn[0, b].reshape(512, 512)); m["cv"] = f(cvn[0, b].reshape(512, 512))
        in_maps.append(m)
    res = run_bass_kernel_spmd(nc, in_maps, core_ids=list(range(8)))
    r = res.results
    st = lambda k: np.stack([np.asarray(r[b][k], dtype=np.float32) for b in range(8)])
    y_p = st("yp"); y_s = st("ys")
    ca_p = st("cap")[None]; cb_p = st("cbp")[None]
    k_p = st("kp").reshape(8, 512, 8, 64)[None]; v_p = st("vp").reshape(8, 512, 8, 64)[None]
    ca_s = st("cas")[None]; cb_s = st("cbs")[None]
    k_s = st("ks").reshape(8, NS, 8, 64)[None]; v_s = st("vs").reshape(8, NS, 8, 64)[None]
    sv_s = st("svs")[None]
    return (y_p, y_s, ca_p, cb_p, k_p, v_p, ca_s, cb_s, k_s, v_s, sv_s)
```

```python
import os
from contextlib import ExitStack

import numpy as np
import concourse.bass as bass
import concourse.mybir as mybir
from concourse.bass_utils import run_bass_kernel_spmd

F32 = mybir.dt.float32
BF16 = mybir.dt.bfloat16
AF = mybir.ActivationFunctionType
ALU = mybir.AluOpType

D = 1024
TT = 512
SEQ = 4096
NS = 16
NPT = SEQ // TT
R_SLOTS = 4
NEG = -30000.0

ENG = ("pe", "act", "dve", "pool", "sp")


def _ap_rng(ap):
    esz = 4 if ap.dtype == F32 else 2
    pairs = ap.ap
    pstep, pcnt = pairs[0]
    off = ap.offset
    if pstep > 0:
        p0, within = off // pstep, off % pstep
    else:
        p0, within = 0, off
    lo = hi = within
    for st, cnt in pairs[1:]:
        if st >= 0:
            hi += st * (cnt - 1)
        else:
            lo += st * (cnt - 1)
    return (ap.tensor.name, p0, p0 + pcnt, lo * esz, (hi + 1) * esz)


def _ovl(a, b):
    return a[0] == b[0] and a[1] < b[2] and b[1] < a[2] and a[3] < b[4] and b[3] < a[4]


class Sched:
    def __init__(self):
        self.ops = {e: [] for e in ENG}
        self.ms = {e: 0 for e in ENG}
        self.dmac = {}
        self.strict = False
        self.hist = {e: [] for e in ("act", "dve", "pool")}

    def add(self, eng, fn, waits=(), sig=False, rd=None, wr=None):
        w = [x for x in waits if x is not None]
        if eng in self.hist:
            sig = True
            h = self.hist[eng]
            if self.strict or rd is None or wr is None:
                if self.ms[eng] > 0:
                    w.append((eng, self.ms[eng]))
                del h[:]
                rds, wrs = None, None
            else:
                rds = [_ap_rng(a) for a in rd]
                wrs = [_ap_rng(a) for a in wr]
                for k in range(len(h) - 1, -1, -1):
                    ev, prd, pwr = h[k]
                    hit = prd is None
                    if not hit:
                        for x in wrs:
                            if any(_ovl(x, y) for y in prd) or any(_ovl(x, y) for y in pwr):
                                hit = True
                                break
                    if not hit:
                        for x in rds:
                            if any(_ovl(x, y) for y in pwr):
                                hit = True
                                break
                    if hit:
                        w.append(ev)
                        del h[:k + 1]
                        break
                if len(h) > 96:
                    del h[:len(h) - 96]
        self.ops[eng].append((fn, w, eng if sig else None, 1))
        if sig:
            self.ms[eng] += 1
            evt = (eng, self.ms[eng])
            if eng in self.hist:
                self.hist[eng].append((evt, rds, wrs))
            return evt
        return None

    def dma(self, eng, fn, semkey, waits=()):
        w = [x for x in waits if x is not None]
        self.ops[eng].append((fn, w, semkey, 16))
        self.dmac[semkey] = self.dmac.get(semkey, 0) + 16
        return (semkey, self.dmac[semkey])

    def last(self, eng):
        return (eng, self.ms[eng]) if self.ms[eng] > 0 else None

    def join(self, *engs):
        return [self.last(e) for e in (engs or ("pe", "act", "dve", "pool"))]


class Psum:
    def __init__(self):
        self.free = [[] for _ in range(8)]
        self.busy = [False] * 8
        self.nxt = 0

    def alloc(self):
        for k in range(8):
            b = (self.nxt + k) % 8
            if not self.busy[b]:
                break
        else:
            raise RuntimeError("PSUM: all 8 banks busy")
        self.nxt = (b + 1) % 8
        self.busy[b] = True
        w = self.free[b]
        self.free[b] = []
        return b, list(w)

    def release(self, b, *evts):
        self.free[b] = [e for e in evts if e is not None]
        self.busy[b] = False


def build_program(n_prompt_tiles=NPT, do_sample=True, stage_lim=99):
    nc = bass.Bass("TRN2", target_bir_lowering=False)
    S = Sched()
    P = Psum()

    def din(name, shape):
        return nc.dram_tensor(name, list(shape), F32, kind="ExternalInput").ap()

    def dout(name, shape):
        return nc.dram_tensor(name, list(shape), F32, kind="ExternalOutput").ap()

    xp = din("xp", [SEQ, D]); xs = din("xs", [NS, D])
    cca = din("cca", [30, 512]); ccb = din("ccb", [2, 512])
    ck = din("ck", [512, 512]); cv = din("cv", [512, 512])
    nme = din("nme", [D]); wie = din("wie", [D, 2560]); caw = din("caw", [31, 512])
    cab = din("cab", [512]); lag = din("lag", [512]); lab = din("lab", [512])
    cbw = din("cbw", [3, 512]); woe = din("woe", [D, D]); nmo = din("nmo", [D])
    wio = din("wio", [D, 2560]); qg = din("qg", [64]); kg = din("kg", [64])
    rb = din("rb", [8, 513]); slg = din("slg", [512]); slb = din("slb", [512])
    sw = din("sw", [4, 128, 128]); sb = din("sb", [4, 128]); woo = din("woo", [D, D])
    nf = din("nf", [2, D]); wgu = din("wgu", [2, D, 5632]); wd = din("wd", [2, 2816, D])

    yp = dout("yp", [SEQ, D]); ys = dout("ys", [NS, D])
    cap = dout("cap", [30, 512]); cbp = dout("cbp", [2, 512])
    kp = dout("kp", [512, 512]); vp = dout("vp", [512, 512])
    cas = dout("cas", [30, 512]); cbs = dout("cbs", [2, 512])
    ks = dout("ks", [NS, 512]); vs = dout("vs", [NS, 512]); svs = dout("svs", [NS, 512])

    if stage_lim < 99:
        dbg_x = dout("dbg_x", [128, 8 * TT]); dbg_c = dout("dbg_c", [128, 8 * TT]); dbg_h = dout("dbg_h", [128, 8 * TT])
    ext_h = nc.dram_tensor("ext_scratch", [8, 128 * 769], F32, kind="Internal")
    ext = ext_h.ap()

    with ExitStack() as es:
        def sb_t(name, shape, dt):
            return es.enter_context(nc.sbuf_tensor(name, list(shape), dt))

        ps = es.enter_context(nc.psum_tensor("ps", [128, 8, 512], F32))
        xT = sb_t("xT", [128, 8, TT], F32)
        hT = sb_t("hT", [128, 8, TT], BF16)
        cat = hT
        ring = sb_t("ring", [128, R_SLOTS, 8, 512], BF16)
        Tb = sb_t("Tb", [128, 5, 8, 128], BF16)
        ident = sb_t("ident", [128, 128], F32)
        identb = sb_t("identb", [128, 128], BF16)
        o1024 = sb_t("o1024", [128, 128], BF16)
        o512 = sb_t("o512", [128, 128], BF16)
        blk64 = sb_t("blk64", [128, 128], BF16)
        ones1 = sb_t("ones1", [128, 128], BF16)
        vecT = sb_t("vecT", [128, 64], F32)
        cawT = sb_t("cawT", [128, 124], F32)
        wsT = sb_t("wsT", [128, 4, 128], BF16)
        bsrow = sb_t("bsrow", [1, 512], BF16)
        dg = sb_t("dg", [128, 124, 128], BF16)
        slg_bc = sb_t("slg_bc", [128, 512], F32)
        slb_bc = sb_t("slb_bc", [128, 512], F32)
        eps6 = sb_t("eps6", [128, 1], F32)
        eps5 = sb_t("eps5", [128, 1], F32)
        kT = sb_t("kT", [128, 4, 1024], BF16)
        Vb = sb_t("Vb", [128, 8, 512], BF16)
        hist_a = sb_t("hist_a", [128, 4, 30], F32)
        hist_b = sb_t("hist_b", [128, 4, 2], F32)
        stage = sb_t("stage", [128, 2, 1024], F32)
        rstd = sb_t("rstd", [128, TT], F32)
        st6 = sb_t("st6", [128, 4, 6], F32)
        mv = sb_t("mv", [128, 4, 2], F32)
        rsd = sb_t("rsd", [128, 4], F32)
        U = sb_t("U", [128, 18432], F32)
        Ub = U.bitcast(BF16)

        def uf32(off, *shape):
            n = int(np.prod(shape))
            v = U[:, off:off + n]
            if len(shape) == 2:
                v = v.rearrange("p (a b) -> p a b", a=shape[0])
            elif len(shape) == 3:
                v = v.rearrange("p (a b c) -> p a b c", a=shape[0], b=shape[1])
            return v

        def ubf(off32, *shape):
            n = int(np.prod(shape))
            v = Ub[:, 2 * off32:2 * off32 + n]
            if len(shape) == 2:
                v = v.rearrange("p (a b) -> p a b", a=shape[0])
            elif len(shape) == 3:
                v = v.rearrange("p (a b c) -> p a b c", a=shape[0], b=shape[1])
            return v

        abuf = uf32(0, 4, 544)
        zbuf = uf32(2176, 4, 516)
        gbb = uf32(4240, 4, 512)
        sgs = uf32(6288, 4, 512)
        acv = uf32(8336, 4, 512)
        xbf = ubf(10384, 4, 512)
        sqb0 = ubf(11408, 4, 512)
        hid = ubf(0, 22, 512)
        stmp = uf32(5632, 2, 512)
        qf = uf32(0, 4, 512)
        sqb1 = ubf(2048, 4, 512)
        kf = uf32(3072, 4, 512)
        qT = ubf(5120, 4, 512)
        uf = uf32(6144, 4, 512)
        tsv = uf32(8192, 2, 512)
        svn = ubf(9216, 4, 512)
        tbuf = uf32(10240, 2, 8, 128)
        Eb = ubf(12288, 2, 5, 1024)
        rden = uf32(17408, 2, 512)
        T32 = uf32(0, 5, 1024)
        VbF = Vb.bitcast(F32)
        extS = VbF[0:8, 5:8, :].rearrange("p a b -> p (a b)")
        bsrow32 = U[0:1, 10000:10512]
        abuf16 = ubf(12432, 4, 544)
        lnt = uf32(13520, 4, 512)

        sems = {}
        for k in ("pe", "act", "dve", "pool"):
            sems[k] = es.enter_context(nc.semaphore("m_" + k))
        for k in ["cst", "cvl", "ex1", "tbl", "tb2", "ysem", "xpre", "outs", "stl0", "stl1", "sts0", "sts1"] + ["ring%d" % i for i in range(R_SLOTS)]:
            sems[k] = es.enter_context(nc.semaphore("d_" + k))

        def act(out, in_, func, waits=(), sig=False, scale=None, bias=None):
            def fn(e):
                kw = {}
                if scale is not None:
                    kw["scale"] = scale
                if bias is not None:
                    kw["bias"] = bias
                return e.activation(out=out, in_=in_, func=func, **kw)
            rd = [in_] + [x for x in (scale, bias) if x is not None and not isinstance(x, (int, float))]
            return S.add("act", fn, waits, sig, rd=rd, wr=[out])

        def tt(eng, out, in0, in1, op, waits=(), sig=False):
            return S.add(eng, lambda e: e.tensor_tensor(out=out, in0=in0, in1=in1, op=op), waits, sig, rd=[in0, in1], wr=[out])

        def stt(out, in0, scalar, in1, op0, op1, waits=(), sig=False):
            rd = [in0, in1] + ([scalar] if not isinstance(scalar, (int, float)) else [])
            return S.add("dve", lambda e: e.scalar_tensor_tensor(out=out, in0=in0, scalar=scalar, in1=in1, op0=op0, op1=op1), waits, sig,
                         rd=rd, wr=[out])

        def ts(eng, out, in0, s1, s2, op0, op1=None, waits=(), sig=False):
            def fn(e):
                if op1 is None:
                    return e.tensor_scalar(out=out, in0=in0, scalar1=s1, scalar2=None, op0=op0)
                return e.tensor_scalar(out=out, in0=in0, scalar1=s1, scalar2=s2, op0=op0, op1=op1)
            rd = [in0] + [x for x in (s1, s2) if x is not None and not isinstance(x, (int, float))]
            return S.add(eng, fn, waits, sig, rd=rd, wr=[out])

        def cp(eng, out, in_, waits=(), sig=False):
            if eng == "act":
                return act(out, in_, AF.Copy, waits, sig)
            return S.add(eng, lambda e: e.tensor_copy(out=out, in_=in_), waits, sig, rd=[in_], wr=[out])

        def mm(out, lhsT, rhs, start, stop, waits=(), sig=False):
            return S.add("pe", lambda e: e.matmul(out, lhsT, rhs, start=start, stop=stop), waits, sig)

        def tr(out, in_, idn, waits=(), sig=False):
            return S.add("pe", lambda e: e.transpose(out, in_, idn), waits, sig)

        stage_free = [[], []]
        stage_n = [0]

        def stage_alloc():
            s = stage_n[0] % 2
            stage_n[0] += 1
            w = stage_free[s]
            stage_free[s] = []
            return s, list(w)

        ring_n = [0]
        ring_free = [[] for _ in range(R_SLOTS)]

        def ring_load(pieces):
            slot = ring_n[0] % R_SLOTS
            ring_n[0] += 1
            w = ring_free[slot]
            ring_free[slot] = []
            evt = None
            for i, (src, nkc, dcol, ncols) in enumerate(pieces):
                def fn(e, src=src, nkc=nkc, dcol=dcol, ncols=ncols, slot=slot):
                    return e.dma_start(out=ring[:, slot, 0:nkc, dcol:dcol + ncols], in_=src)
                evt = S.dma("pool", fn, "ring%d" % slot, w if i == 0 else ())
            return slot, evt

        def wview(w_ap):
            return w_ap.rearrange("(kc p) n -> p kc n", p=128)

        def gemm(groups, KC, in_chunk, NT, in_waits, evac, mode="ws", kc_evts=None):
            first = True
            deferred = []
            for gi, (pieces, gmode) in enumerate(groups):
                nslots = (KC + 7) // 8
                slot_info = []
                for si in range(nslots):
                    kc0 = si * 8
                    nkc = min(8, KC - kc0)
                    pl = []
                    dcol = 0
                    for (wv, c0, ncols) in pieces:
                        pl.append((wv[:, kc0:kc0 + nkc, c0:c0 + ncols], nkc, dcol, ncols))
                        dcol += ncols
                    slot_info.append((ring_load(pl), kc0, nkc))
                n_out = 4 if gmode == "ws" else (NT + 127) // 128
                banks = []
                for i in range(n_out):
                    banks.append(P.alloc())
                last_evt = [None] * n_out
                for si, ((slot, levt), kc0, nkc) in enumerate(slot_info):
                    for i in range(n_out):
                        b, bw = banks[i]
                        for k in range(nkc):
                            kc = kc0 + k
                            w = []
                            if k == 0:
                                w = [levt] + (bw if si == 0 else [])
                                if first:
                                    w += list(in_waits)
                                    first = False
                            if kc_evts is not None and gi == 0 and i == 0:
                                w = list(w) + [kc_evts[kc]]
                            is_last_k = (k == nkc - 1)
                            sig = is_last_k and (si == nslots - 1 or i == n_out - 1)
                            if gmode == "ws":
                                e = mm(ps[:, b, 0:NT], ring[:, slot, k, i * 128:(i + 1) * 128], in_chunk(kc)[:, 0:NT],
                                       kc == 0, kc == KC - 1, w, sig)
                            else:
                                rows = min(128, NT - i * 128)
                                e = mm(ps[0:rows, b, 0:512], in_chunk(kc)[:, i * 128:i * 128 + rows], ring[:, slot, k, 0:512],
                                       kc == 0, kc == KC - 1, w, sig)
                            if sig:
                                last_evt[i] = e
                    ring_free[slot] = [S.last("pe")]
                for d in deferred:
                    d()
                deferred = []
                for i in range(n_out):
                    r = evac(gi, i, banks[i][0], last_evt[i])
                    if r is not None:
                        deferred.append(r)
            for d in deferred:
                d()

        cst = []

        def cdma(out, in_, eng="sp", waits=()):
            e = S.dma(eng, lambda e_: e_.dma_start(out=out, in_=in_), "cst", waits)
            cst.append(e)
            return e

        VC = {}
        row = 0
        s0 = stage[:, 0, :]
        s1 = stage[:, 1, :]

        def vrows(name, ap2d, n):
            nonlocal row
            VC[name] = row
            cdma(s0[row:row + n, 0:128], ap2d)
            row += n

        vrows("nme", nme.rearrange("(c p) -> c p", p=128), 8)
        vrows("nmo", nmo.rearrange("(c p) -> c p", p=128), 8)
        vrows("nf", nf.rearrange("l (c p) -> (l c) p", p=128), 16)
        vrows("cab", cab.rearrange("(c p) -> c p", p=128), 4)
        vrows("lag", lag.rearrange("(c p) -> c p", p=128), 4)
        vrows("lab", lab.rearrange("(c p) -> c p", p=128), 4)
        vrows("cbw", cbw.rearrange("k (c p) -> (k c) p", p=128), 12)
        qg2 = qg.rearrange("(o d) -> o d", o=1)
        kg2 = kg.rearrange("(o d) -> o d", o=1)
        VC["qg"] = row
        cdma(s0[row:row + 1, 0:64], qg2)
        cdma(s0[row:row + 1, 64:128], qg2)
        row += 1
        VC["kg"] = row
        cdma(s0[row:row + 1, 0:64], kg2)
        cdma(s0[row:row + 1, 64:128], kg2)
        row += 1
        NV = row
        cdma(s0[0:124, 128:256], caw.rearrange("k (c p) -> (k c) p", p=128))
        for g in range(4):
            cdma(s0[:, 256 + g * 128:256 + (g + 1) * 128], sw[g])
        cdma(bsrow32[0:1, :], sb.rearrange("(o g) i -> o (g i)", o=1))
        cdma(slg_bc[:, :], bass.AP(tensor=slg.tensor, offset=0, ap=[[0, 128], [1, 512]]))
        cdma(slb_bc[:, :], bass.AP(tensor=slb.tensor, offset=0, ap=[[0, 128], [1, 512]]))
        W_ = 768
        GS = 128 * (W_ + 1)
        d0 = S.dma("sp", lambda e_: e_.dma_start(out=extS[:, 0:384], in_=rb[:, 129:513]), "ex1")
        ts("dve", extS[:, 384:768], extS[:, 0:384], 0.0, extS[:, 383:384], ALU.mult, ALU.add, waits=[d0], sig=True)
        ef = ts("dve", extS[:, :], extS[:, :], 8.0, None, ALU.mult, sig=True)
        cst_all = [("cst", S.dmac["cst"])]
        late = {"done": False, "done2": False}

        def late_prologue():
            if late["done"]:
                return
            late["done"] = True
            strict0 = S.strict
            S.strict = False
            for j in range(124):
                ts("dve", dg[:, j, :], ident[:, :], cawT[:, j:j + 1], None, ALU.mult)
            late["dg_ready"] = [S.add("dve", lambda e: e.nop(), sig=True)]
            S.strict = strict0
            gl = []
            for h in range(8):
                gl.append(S.dma("act", lambda e_, h=h: e_.dma_start(
                    out=bass.AP(tensor=ext_h, offset=h * GS, ap=[[W_ + 1, 128], [1, 768]]),
                    in_=bass.AP(tensor=VbF, offset=h * 2048 + 1280, ap=[[2048, 1], [0, 128], [1, 768]])),
                    "tbl", [ef] if h == 0 else ()))
            late["tbl_done"] = gl[-1]

        def late_prologue2():
            if late["done2"]:
                return
            late["done2"] = True
            tbl_done = late["tbl_done"]
            last = None
            for jb in range(5):
                for par in range(2):
                    src = bass.AP(tensor=ext_h, offset=par * GS + 639 - jb * 128, ap=[[W_, 128], [2 * GS, 4], [1, 128]])
                    last = S.dma("pool", lambda e_, jb=jb, par=par, src=src: e_.dma_start(
                        out=Tb[:, jb, par * 4:(par + 1) * 4, :], in_=src), "tb2", [tbl_done] if (jb == 0 and par == 0) else ())
            S.add("pool", lambda e: e.memset(Tb[0:64, 0, :, 64:128], NEG), waits=[last])
            late["tb_ready"] = S.add("pool", lambda e: e.memset(Tb[64:128, 4, :, 0:64], NEG), sig=True)

        S.strict = True
        S.add("pool", lambda e: e.memset(ident[:, :], 1.0))
        S.add("pool", lambda e: e.affine_select(out=ident[:, :], in_=ident[:, :], pattern=[[-1, 128]],
                                                compare_op=ALU.is_equal, fill=0.0, base=0, channel_multiplier=1))
        S.add("pool", lambda e: e.memset(o1024[:, :], 1.0 / 1024))
        S.add("pool", lambda e: e.memset(o512[:, :], 1.0 / 512))
        S.add("pool", lambda e: e.memset(ones1[:, :], 1.0))
        S.add("pool", lambda e: e.memset(blk64[:, :], 0.0))
        S.add("pool", lambda e: e.memset(blk64[0:64, 0:64], 1.0 / 64))
        S.add("pool", lambda e: e.memset(blk64[64:128, 64:128], 1.0 / 64))
        S.add("pool", lambda e: e.memset(eps6[:, :], 1e-6))
        S.add("pool", lambda e: e.memset(eps5[:, :], 1e-5))
        S.add("pool", lambda e: e.memset(hist_a[:, :, :], 0.0))
        pool_c = S.add("pool", lambda e: e.memset(hist_b[:, :, :], 0.0), sig=True)
        S.strict = False

        b0, w0 = P.alloc()
        tr(ps[:, b0, 0:NV], s0[0:NV, 0:128], ident[0:NV, 0:NV], waits=cst_all + [pool_c] + w0)
        tr(ps[:, b0, 128:252], s0[0:124, 128:256], ident[0:124, 0:124])
        b1, w1 = P.alloc()
        for g in range(4):
            ev = tr(ps[:, b1, g * 128:(g + 1) * 128], s0[:, 256 + g * 128:256 + (g + 1) * 128], ident[:, :],
                    waits=w1 if g == 0 else (), sig=(g == 3))
        cp("dve", vecT[:, 0:NV], ps[:, b0, 0:NV], waits=[ev])
        cp("dve", cawT[:, :], ps[:, b0, 128:252])
        cp("dve", identb[:, :], ident[:, :])
        cp("dve", sgs[:, 0, :], ps[:, b1, :])
        ev2 = cp("dve", bsrow[0:1, :], bsrow32[0:1, :], sig=True)
        P.release(b0, ev2)
        P.release(b1, ev2)
        for g in range(4):
            S.add("pool", lambda e, g=g: e.affine_select(out=wsT[:, g, :], in_=sgs[:, 0, g * 128:(g + 1) * 128],
                                                         pattern=[[1, 128]], compare_op=ALU.is_ge, fill=0.0, base=0,
                                                         channel_multiplier=-1), waits=[ev2] if g == 0 else ())
        pool_c2 = S.add("pool", lambda e: e.nop(), sig=True)
        stage_free[0] = [ev]

        if do_sample:
            for half in range(2):
                sslot, sw_ = stage_alloc()
                ld = S.dma("sp", lambda e_, half=half, sslot=sslot: e_.dma_start(
                    out=stage[:, sslot, :].rearrange("p (a b) -> p a b", a=2),
                    in_=ck[half * 256:(half + 1) * 256, :].rearrange("(a p) n -> p a n", p=128)), "stl%d" % sslot, sw_)
                lastpe = None
                for a in range(2):
                    blk = half * 2 + a
                    b, bw = P.alloc()
                    for c in range(4):
                        lastpe = tr(ps[:, b, c * 128:(c + 1) * 128], stage[:, sslot, a * 512 + c * 128:a * 512 + (c + 1) * 128],
                                    ident[:, :], waits=[ld, ev2] + bw if c == 0 else (), sig=(c == 3))
                    e = cp("act", kT[:, :, blk * 128:(blk + 1) * 128], ps[:, b, :].rearrange("p (c t) -> p c t", c=4),
                           waits=[lastpe], sig=True)
                    P.release(b, e)
                stage_free[sslot] = [lastpe]
            cvl = S.dma("pool", lambda e_: e_.dma_start(out=Vb[:, 0:4, :], in_=cv.rearrange("(a p) n -> p a n", p=128)), "cvl")
        else:
            cvl = None

        out_evts = []
        prefetch = {}
        y_dmas = []
        ystage = uf32(0, 4, 1024)
        xpre = uf32(14336, 4, 1024)

        def process_tile(is_sample, ti):
            NT = NS if is_sample else TT
            nblk = (NT + 127) // 128
            xsrc = xs if is_sample else xp
            ydst = ys if is_sample else yp
            t0 = 0 if is_sample else ti * TT
            last_tile = (not is_sample) and (ti == n_prompt_tiles - 1)

            x_ready = []
            for blk in range(nblk):
                rows = min(128, NT - blk * 128)
                if (is_sample, ti, blk) in prefetch:
                    ld = prefetch.pop((is_sample, ti, blk))
                    xin = xpre[:, blk, :]
                    sslot = None
                else:
                    sslot, sw_ = stage_alloc()
                    ld = S.dma("sp", lambda e_, blk=blk, rows=rows, sslot=sslot: e_.dma_start(
                        out=stage[0:rows, sslot, :], in_=xsrc[t0 + blk * 128:t0 + blk * 128 + rows, :]), "stl%d" % sslot, sw_)
                    xin = stage[:, sslot, :]
                lastpe = None
                for half in range(2):
                    b, bw = P.alloc()
                    for c4 in range(4):
                        c = half * 4 + c4
                        lastpe = tr(ps[:, b, c4 * 128:c4 * 128 + rows], xin[0:rows, c * 128:(c + 1) * 128],
                                    ident[0:rows, 0:rows], waits=([ld] + bw + (S.join("act", "dve", "pool") + list(out_evts) if (blk == 0 and half == 0) else [])) if c4 == 0 else (), sig=(c4 == 3))
                    e = cp("dve", xT[:, half * 4:(half + 1) * 4, blk * 128:blk * 128 + rows],
                           ps[:, b, :].rearrange("p (c t) -> p c t", c=4)[:, :, 0:rows], waits=[lastpe], sig=True)
                    P.release(b, e)
                    x_ready = [e]
                if sslot is not None:
                    stage_free[sslot] = [lastpe]

            if stage_lim < 1:
                return
            def rmsnorm(gcol, x_waits, chunk_evts=None):
                hfree = S.join("pe")
                sq = []
                for c in range(8):
                    w = [chunk_evts[c]] if chunk_evts else []
                    if c == 0:
                        w += list(x_waits) + hfree
                    sq.append(act(hT[:, c, 0:NT], xT[:, c, 0:NT], AF.Square, waits=w, sig=True))
                b, bw = P.alloc()
                e2_ = None
                for c in range(8):
                    e2_ = mm(ps[:, b, 0:NT], o1024[:, :], hT[:, c, 0:NT], c == 0, c == 7, waits=[sq[c]] + (bw if c == 0 else []), sig=(c == 7))
                act(rstd[:, 0:NT], ps[:, b, 0:NT], AF.Ln, waits=[e2_], bias=eps6[:, :])
                e3_ = act(ps[:, b, 0:NT], rstd[:, 0:NT], AF.Exp, scale=-0.5, sig=True)
                hev = []
                for c in range(8):
                    hev.append(stt(hT[:, c, 0:NT], ps[:, b, 0:NT], vecT[:, gcol + c:gcol + c + 1], xT[:, c, 0:NT], ALU.mult, ALU.mult,
                                   waits=[e3_] if c == 0 else (), sig=True))
                P.release(b, hev[-1])
                return hev

            res_evt = {}

            def residual_evac(gi, i, b, pe_evt):
                c = gi * 4 + i
                e = stt(xT[:, c, 0:NT], ps[:, b, 0:NT], 1.0, xT[:, c, 0:NT], ALU.mult, ALU.add, waits=[pe_evt], sig=True)
                res_evt[c] = e
                P.release(b, e)

            def ffn(layer, x_waits):
                h_ready = rmsnorm(VC["nf"] + 8 * layer, [], dict(res_evt))
                if stage_lim < 5.3:
                    return S.join("dve")
                wg = wview(wgu[layer])
                groups = []
                for j in range(11):
                    groups.append(([(wg, j * 256, 256), (wg, 2816 + j * 256, 256)], "ws"))
                pend = {}

                def evac_gu(gi, i, b, pe_evt):
                    if i < 2:
                        e = act(stmp[:, i, 0:NT], ps[:, b, 0:NT], AF.Silu, waits=[pe_evt] + S.join("dve"), sig=True)
                        P.release(b, e)
                        pend[i] = e
                    else:
                        e = stt(hid[:, 2 * gi + (i - 2), 0:NT], ps[:, b, 0:NT], 1.0, stmp[:, i - 2, 0:NT], ALU.mult, ALU.mult,
                               waits=[pe_evt, pend[i - 2]], sig=True)
                        P.release(b, e)

                gemm(groups, 8, lambda kc: hT[:, kc, :], NT, [], evac_gu, kc_evts=h_ready)
                if stage_lim < 5.6:
                    return S.join("dve")
                wdv = wview(wd[layer])
                groups = [([(wdv, g * 512, 512)], "ws") for g in range(2)]
                gemm(groups, 22, lambda kc: hid[:, kc, :], NT, S.join("act", "dve"), residual_evac)
                return S.join("dve")

            h_ready = rmsnorm(VC["nme"], x_ready)
            first_tile = not late["done"]
            late_prologue()
            if stage_lim < 0.5:
                return
            if is_sample:
                sslot, sw_ = stage_alloc()
                ld = S.dma("sp", lambda e_, sslot=sslot: e_.dma_start(out=stage[0:30, sslot, 0:512], in_=cca[:, :]), "stl%d" % sslot, sw_)
                ld = S.dma("sp", lambda e_, sslot=sslot: e_.dma_start(out=stage[0:2, sslot, 512:1024], in_=ccb[:, :]), "stl%d" % sslot)
                b, bw = P.alloc()
                lastpe = None
                for c in range(4):
                    tr(ps[:, b, c * 32:c * 32 + 30], stage[0:30, sslot, c * 128:(c + 1) * 128], ident[0:30, 0:30],
                       waits=[ld] + bw + S.join("act", "dve", "pool") if c == 0 else ())
                    lastpe = tr(ps[:, b, 128 + c * 32:128 + c * 32 + 2], stage[0:2, sslot, 512 + c * 128:512 + (c + 1) * 128],
                                ident[0:2, 0:2], sig=(c == 3))
                cp("act", abuf[:, :, 0:30], ps[:, b, 0:128].rearrange("p (c t) -> p c t", c=4)[:, :, 0:30], waits=[lastpe])
                h16 = cp("act", abuf16[:, :, 0:30], ps[:, b, 0:128].rearrange("p (c t) -> p c t", c=4)[:, :, 0:30], sig=True)
                e = cp("act", zbuf[:, :, 0:2], ps[:, b, 128:256].rearrange("p (c t) -> p c t", c=4)[:, :, 0:2], sig=True)
                hz = e
                P.release(b, e)
                stage_free[sslot] = [lastpe]
            else:
                cp("act", abuf[:, :, 0:30], hist_a[:, :, :], waits=S.join("pe", "dve", "pool") + list(y_dmas))
                h16 = cp("act", abuf16[:, :, 0:30], hist_a[:, :, :], sig=True)
                hz = cp("act", zbuf[:, :, 0:2], hist_b[:, :, :], sig=True)

            if stage_lim < 0.7:
                return
            wv0 = wview(wie)
            l0_order = [0, 1, 3, 4, 2]
            groups = [([(wv0, g * 512, 512)], "ws") for g in l0_order]
            sig_evt = {}
            a16_evt = {}
            convA_state = {}

            def convA_all():
                for c in range(4):
                    b, bw = P.alloc()
                    lp = None
                    for k in range(31):
                        lp = mm(ps[:, b, 0:NT], dg[:, k * 4 + c, :], abuf16[:, c, k:k + NT], k == 0, k == 30,
                                waits=([a16_evt[c], h16] + bw + late["dg_ready"]) if k == 0 else (), sig=(k == 30))
                    cabc = vecT[:, VC["cab"] + c:VC["cab"] + c + 1]
                    act(acv[:, c, 0:NT], ps[:, b, 0:NT], AF.Identity, waits=[lp], bias=cabc)
                    act(xbf[:, c, 0:NT], ps[:, b, 0:NT], AF.Identity, bias=cabc)
                    e = act(sqb0[:, c, 0:NT], ps[:, b, 0:NT], AF.Square, bias=cabc, sig=True)
                    P.release(b, e)
                convA_state["done"] = S.last("act")

            def convB_taps():
                cb0 = VC["cbw"]
                for c in range(4):
                    acc2 = sgs[:, c, 0:NT]
                    ts("dve", acc2, zbuf[:, c, 0:NT], vecT[:, cb0 + c:cb0 + c + 1], None, ALU.mult, waits=[hz] if c == 0 else ())
                    stt(acc2, zbuf[:, c, 1:1 + NT], vecT[:, cb0 + 4 + c:cb0 + 4 + c + 1], acc2, ALU.mult, ALU.add)
                    stt(acc2, zbuf[:, c, 2:2 + NT], vecT[:, cb0 + 8 + c:cb0 + 8 + c + 1], acc2, ALU.mult, ALU.add)

            def evac_l0(gi, i, b, pe_evt):
                gi = l0_order[gi]
                if gi == 0:
                    e = cp("act", abuf[:, i, 30:30 + NT], ps[:, b, 0:NT], waits=[pe_evt], sig=True)
                    P.release(b, e)
                elif gi == 1:
                    e = act(sgs[:, i, 0:NT], ps[:, b, 0:NT], AF.Sigmoid, waits=[pe_evt], sig=True)
                    P.release(b, e)
                    tt("dve", abuf[:, i, 30:30 + NT], abuf[:, i, 30:30 + NT], sgs[:, i, 0:NT], ALU.mult, waits=[e])
                    a16_evt[i] = cp("dve", abuf16[:, i, 30:30 + NT], abuf[:, i, 30:30 + NT], sig=True)
                    if i == 3:
                        return convA_all
                elif gi == 2:
                    e = cp("act", gbb[:, i, 0:NT], ps[:, b, 0:NT], waits=[pe_evt], sig=True)
                    P.release(b, e)
                elif gi == 3:
                    e = cp("act", zbuf[:, i, 2:2 + NT], ps[:, b, 0:NT], waits=[pe_evt], sig=True)
                    P.release(b, e)
                    sig_evt[i] = e
                else:
                    e = stt(zbuf[:, i, 2:2 + NT], ps[:, b, 0:NT], 1.0, zbuf[:, i, 2:2 + NT], ALU.mult, ALU.mult,
                           waits=[pe_evt, sig_evt[i]], sig=True)
                    P.release(b, e)
                    if i == 3:
                        return convB_taps

            gemm(groups, 8, lambda kc: hT[:, kc, :], NT, S.join("act") + list(y_dmas), evac_l0, kc_evts=h_ready)

            if stage_lim < 2:
                return
            join0 = S.join("act", "pe", "pool", "dve")
            convA_done = convA_state["done"]
            for c in range(4):
                e = tt("dve", cat[:, 4 + c, 0:NT], gbb[:, c, 0:NT], sgs[:, c, 0:NT], ALU.mult, waits=join0 if c == 0 else (), sig=True)
            conv_done = S.last("dve")
            if not is_sample:
                cp("pool", hist_a[:, :, :], abuf[:, :, NT:NT + 30], waits=[conv_done] + S.join("act"))
                cp("pool", hist_b[:, :, :], zbuf[:, :, NT:NT + 2], sig=True)
            if is_sample or last_tile:
                oa, ob = (cas, cbs) if is_sample else (cap, cbp)
                b, bw = P.alloc()
                lastpe = None
                for c in range(4):
                    tr(ps[0:30, b, c * 128:(c + 1) * 128], abuf[:, c, NT:NT + 30], ident[:, :], waits=[conv_done] + bw if c == 0 else ())
                b2, bw2 = P.alloc()
                for c in range(4):
                    lastpe = tr(ps[0:2, b2, c * 128:(c + 1) * 128], zbuf[:, c, NT:NT + 2], ident[:, :], waits=bw2 if c == 0 else (), sig=(c == 3))
                sslot, sw_ = stage_alloc()
                cp("act", stage[0:30, sslot, 0:512], ps[0:30, b, :], waits=[lastpe] + sw_)
                e = cp("act", stage[0:2, sslot, 512:1024], ps[0:2, b2, :], sig=True)
                P.release(b, e)
                P.release(b2, e)
                d1 = S.dma("sp", lambda e_, sslot=sslot: e_.dma_start(out=oa[:, :], in_=stage[0:30, sslot, 0:512]), "sts%d" % sslot, [e])
                d2 = S.dma("sp", lambda e_, sslot=sslot: e_.dma_start(out=ob[:, :], in_=stage[0:2, sslot, 512:1024]), "sts%d" % sslot)
                stage_free[sslot] = [d2]
            if stage_lim < 3:
                return
            e1_ = convA_done
            bm, bwm = P.alloc()
            bq, bwq = P.alloc()
            for c in range(4):
                mm(ps[:, bm, 0:NT], o512[:, :], xbf[:, c, 0:NT], c == 0, c == 3, waits=[e1_] + bwm + bwq if c == 0 else ())
            e2_ = None
            for c in range(4):
                e2_ = mm(ps[:, bq, 0:NT], o512[:, :], sqb0[:, c, 0:NT], c == 0, c == 3, sig=(c == 3))
            e3_ = act(sgs[:, 0, 0:NT], ps[:, bm, 0:NT], AF.Square, waits=[e2_] + S.join("dve"), sig=True)
            e4_ = stt(sgs[:, 1, 0:NT], ps[:, bq, 0:NT], 1.0, sgs[:, 0, 0:NT], ALU.mult, ALU.subtract, waits=[e3_], sig=True)
            act(rstd[:, 0:NT], sgs[:, 1, 0:NT], AF.Ln, waits=[e4_], bias=eps5[:, :])
            e5_ = act(rstd[:, 0:NT], rstd[:, 0:NT], AF.Exp, scale=-0.5, sig=True)
            P.release(bq, e4_)
            e7_ = {}
            for c in range(4):
                t1 = lnt[:, c % 2, 0:NT]
                t2 = lnt[:, 2 + c % 2, 0:NT]
                stt(t1, ps[:, bm, 0:NT], -1.0, acv[:, c, 0:NT], ALU.mult, ALU.add, waits=[e5_, e7_.get(c - 2)])
                e6_ = tt("dve", t2, t1, rstd[:, 0:NT], ALU.mult, sig=True)
                e7_[c] = act(cat[:, c, 0:NT], t2, AF.Silu, waits=[e6_],
                             scale=vecT[:, VC["lag"] + c:VC["lag"] + c + 1], bias=vecT[:, VC["lab"] + c:VC["lab"] + c + 1], sig=True)
            P.release(bm, S.last("dve"))
            if stage_lim < 4:
                return
            wvo = wview(woe)
            groups = [([(wvo, g * 512, 512)], "ws") for g in range(2)]
            gemm(groups, 8, lambda kc: cat[:, kc, :], NT, S.join("act", "dve", "pool"), residual_evac)
            if stage_lim < 5:
                return
            xw = ffn(0, S.join("dve"))

            if stage_lim < 6:
                return
            late_prologue2()
            h_ready = rmsnorm(VC["nmo"], [], dict(res_evt))
            wv1 = wview(wio)
            groups = [([(wv1, 0, 512)], "ws"), ([(wv1, 512, 512)], "ws"), ([(wv1, 1024, 512)], "as"),
                      ([(wv1, 1536, 512)], "ws"), ([(wv1, 2048, 512)], "as")]
            Q0 = 4 if is_sample else 4 * ti
            kcol0 = (Q0 % 8) * 128
            qk_state = {"stt": None}
            qstat_evt = {}
            kv_out = is_sample or last_tile
            okd, ovd = (ks, vs) if is_sample else (kp, vp)

            def headnorm(gi, i, b, pe_evt):
                xf = qf if gi == 0 else kf
                cp("act", xf[:, i, 0:NT], ps[:, b, 0:NT], waits=[pe_evt, qstat_evt.get(i) if gi == 1 else None])
                e = act(sqb1[:, i, 0:NT], ps[:, b, 0:NT], AF.Square, sig=True)
                P.release(b, e)
                return lambda: headnorm2(gi, i, e)

            def headnorm2(gi, i, e):
                b2, bw2 = P.alloc()
                ep = mm(ps[:, b2, 0:NT], blk64[:, :], sqb1[:, i, 0:NT], True, True, waits=[e] + bw2, sig=True)
                if gi == 0:
                    qstat_evt[i] = ep
                act(rstd[:, 0:NT], ps[:, b2, 0:NT], AF.Ln, waits=[ep], bias=eps6[:, :])
                e3 = act(ps[:, b2, 0:NT], rstd[:, 0:NT], AF.Exp, scale=-0.5, sig=True)
                if gi == 0:
                    es_ = stt(qT[:, i, 0:NT], ps[:, b2, 0:NT], vecT[:, VC["qg"]:VC["qg"] + 1], qf[:, i, 0:NT], ALU.mult, ALU.mult,
                              waits=[e3], sig=True)
                    P.release(b2, es_)
                else:
                    es_ = stt(kf[:, i, 0:NT], ps[:, b2, 0:NT], vecT[:, VC["kg"]:VC["kg"] + 1], kf[:, i, 0:NT], ALU.mult, ALU.mult,
                              waits=[e3], sig=True)
                    P.release(b2, es_)
                    es_ = cp("dve", kT[:, i, kcol0:kcol0 + NT], kf[:, i, 0:NT], sig=True)
                if i == 3:
                    qk_state["stt"] = es_
                if gi == 1 and i == 3 and kv_out:
                    for blk in range(nblk):
                        rows = min(128, NT - blk * 128)
                        bo, bwo = P.alloc()
                        lp = None
                        for c in range(4):
                            lp = tr(ps[0:rows, bo, c * 128:(c + 1) * 128], kf[:, c, blk * 128:blk * 128 + rows], ident[:, :],
                                    waits=[es_] + bwo if c == 0 else (), sig=(c == 3))
                        sslot, sw_ = stage_alloc()
                        ee = cp("act", stage[0:rows, sslot, 0:512], ps[0:rows, bo, :], waits=[lp] + sw_, sig=True)
                        P.release(bo, ee)
                        dd = S.dma("sp", lambda e_, sslot=sslot, rows=rows, blk=blk: e_.dma_start(
                            out=okd[blk * 128:blk * 128 + rows, :], in_=stage[0:rows, sslot, 0:512]), "sts%d" % sslot, [ee])
                        stage_free[sslot] = [dd]
                    qk_state["stt"] = S.last("pe")

            sv_pending = []

            def evac_l1(gi, i, b, pe_evt):
                if gi in (0, 1):
                    return headnorm(gi, i, b, pe_evt)
                elif gi == 2:
                    rows = min(128, NT - i * 128)
                    vslot = (Q0 + i) % 8
                    e = cp("act", Vb[0:rows, vslot, :], ps[0:rows, b, :], waits=[pe_evt], sig=True)
                    if kv_out:
                        sslot, sw_ = stage_alloc()
                        e = cp("dve", stage[0:rows, sslot, 0:512], ps[0:rows, b, :], waits=[pe_evt, e] + sw_, sig=True)
                        dd = S.dma("sp", lambda e_, sslot=sslot, rows=rows, i=i: e_.dma_start(
                            out=ovd[i * 128:i * 128 + rows, :], in_=stage[0:rows, sslot, 0:512]), "sts%d" % sslot, [e])
                        stage_free[sslot] = [dd]
                        P.release(b, e, S.last("act"))
                    else:
                        P.release(b, e)
                elif gi == 3:
                    e = cp("dve", uf[:, i, 0:NT], ps[:, b, 0:NT], waits=[pe_evt], sig=True)
                    P.release(b, e)
                else:
                    sv_pending.append((i, b, pe_evt))
                    if i < nblk - 1:
                        return None
                    strict0 = S.strict
                    S.strict = True
                    ea = None
                    for (tb, bb, pev) in sv_pending:
                        rows = min(128, NT - tb * 128)
                        S.add("dve", lambda e_, tb=tb, bb=bb, rows=rows: e_.bn_stats(out=st6[0:rows, tb, :], in_=ps[0:rows, bb, :]),
                              waits=[pev] + (S.join("act") if tb == 0 else []))
                        ea = S.add("dve", lambda e_, tb=tb, rows=rows: e_.bn_aggr(out=mv[0:rows, tb, :], in_=st6[0:rows, tb, :]), sig=True)
                    r0 = min(128, NT)
                    act(rsd[0:r0, 0:nblk], mv[0:r0, 0:nblk, 1], AF.Ln, waits=[ea], bias=eps5[0:r0, :])
                    eb_ = act(rsd[0:r0, 0:nblk], rsd[0:r0, 0:nblk], AF.Exp, scale=-0.5, sig=True)
                    S.strict = strict0
                    cast_evt = {}
                    for (tb, bb, pev) in sv_pending:
                        rows = min(128, NT - tb * 128)
                        tsl = tb % 2
                        e = ts("dve", tsv[0:rows, tsl, :], ps[0:rows, bb, :], mv[0:rows, tb, 0:1], rsd[0:rows, tb:tb + 1],
                               ALU.subtract, ALU.mult, waits=[eb_, cast_evt.get(tb - 2)], sig=True)
                        P.release(bb, e)
                        tt("dve", tsv[0:rows, tsl, :], tsv[0:rows, tsl, :], slg_bc[0:rows, :], ALU.mult)
                        e = tt("dve", tsv[0:rows, tsl, :], tsv[0:rows, tsl, :], slb_bc[0:rows, :], ALU.add, sig=True)
                        cast_evt[tb] = cp("act", svn[0:rows, tb, :], tsv[0:rows, tsl, :], waits=[e], sig=True)
                        if is_sample:
                            out_evts.append(S.dma("sp", lambda e_, rows=rows, tsl=tsl: e_.dma_start(out=svs[:, :], in_=tsv[0:rows, tsl, :]),
                                                  "outs", [e]))
                return None

            gemm(groups, 8, lambda kc: hT[:, kc, :], NT, [], evac_l1, kc_evts=h_ready)

            if stage_lim < 7:
                return
            sv_ready = S.join("act", "dve")
            for g in range(4):
                b, bw = P.alloc()
                lp = None
                for tb in range(nblk):
                    rows = min(128, NT - tb * 128)
                    mm(ps[:, b, tb * 128:tb * 128 + rows], svn[0:rows, tb, g * 128:(g + 1) * 128], wsT[0:rows, g, 0:rows], True, False,
                       waits=(sv_ready + bw + [pool_c2]) if tb == 0 else ())
                    lp = mm(ps[:, b, tb * 128:tb * 128 + rows], ones1[0:1, :], bsrow[0:1, g * 128:g * 128 + rows], False, True,
                            sig=(tb == nblk - 1))
                e = stt(cat[:, 4 + g, 0:NT], ps[:, b, 0:NT], 1.0, uf[:, g, 0:NT], ALU.mult, ALU.mult, waits=[lp], sig=True)
                P.release(b, e)

            if stage_lim < 8:
                return
            att_state = {}
            att_in_ready = S.join("act", "dve")
            tbuf_rd = [None, None]
            pv_done = {}
            norm_done = {}
            def att_scores(qb):
                    ntq = min(128, NT - qb * 128)
                    Q = Q0 + qb
                    jbs = []
                    for jb in range(5):
                        kb = Q - 4 + jb
                        if (not is_sample) and kb < 0:
                            continue
                        nkeys = NS if (is_sample and jb == 4) else 128
                        jbs.append((jb, kb % 8, nkeys))
                    ebuf = qb % 2
                    for ji, (jb, slot, nkeys) in enumerate(jbs):
                        bA, wA = P.alloc()
                        bB, wB = P.alloc()
                        lp = None
                        w0_ = wA + wB + att_in_ready + [cvl, late["tb_ready"]]
                        if not is_sample:
                            for par, bk in ((0, bA), (1, bB)):
                                mm(ps[:, bk, :], identb[:, :], Tb[:, jb, par * 4:(par + 1) * 4, :].rearrange("p a b -> p (a b)"),
                                   True, False, waits=w0_ if par == 0 else ())
                        for h in range(8):
                            c, par = h // 2, h % 2
                            pb = par * 64
                            bk = bB if par else bA
                            lp = mm(ps[0:nkeys, bk, c * 128:c * 128 + ntq], kT[pb:pb + 64, c, slot * 128:slot * 128 + nkeys],
                                    qT[pb:pb + 64, c, qb * 128:qb * 128 + ntq], is_sample, is_sample or h >= 6,
                                    waits=w0_ if (h == 0 and is_sample) else (), sig=(h == 7))
                        if is_sample:
                            tsl = ji % 2
                            ed = None
                            for par, bk in ((0, bA), (1, bB)):
                                ed = stt(tbuf[0:nkeys, tsl, par * 4:(par + 1) * 4, 0:ntq],
                                         ps[0:nkeys, bk, :].rearrange("p (a b) -> p a b", a=4)[:, :, 0:ntq], 1.0,
                                         Tb[0:nkeys, jb, par * 4:(par + 1) * 4, 0:ntq], ALU.mult, ALU.add,
                                         waits=[lp, tbuf_rd[tsl]] if par == 0 else (), sig=(par == 1))
                            P.release(bA, ed)
                            P.release(bB, ed)
                            tbuf_rd[tsl] = act(Eb[0:nkeys, ebuf, ji, :].rearrange("p (a b) -> p a b", a=8)[:, :, 0:ntq],
                                               tbuf[0:nkeys, tsl, :, 0:ntq], AF.Exp, scale=0.125, waits=[ed, pv_done.get(qb - 2)], sig=True)
                        else:
                            for par, bk in ((0, bA), (1, bB)):
                                ee_ = act(Eb[:, ebuf, ji, par * 512:(par + 1) * 512], ps[:, bk, :], AF.Exp, scale=0.125,
                                          waits=[lp, pv_done.get(qb - 2)] if par == 0 else (), sig=True)
                                P.release(bk, ee_)
                    e_ready = S.last("act")
                    att_state[qb] = (ntq, jbs, ebuf, e_ready)
            def att_pv(qb):
                    ntq, jbs, ebuf, e_ready = att_state[qb]
                    dbs = []
                    for par in range(2):
                        bd, wd_ = P.alloc()
                        lp = None
                        for ji, (jb, slot, nkeys) in enumerate(jbs):
                            lp = mm(ps[:, bd, 0:4 * ntq], ones1[0:nkeys, :],
                                    Eb[0:nkeys, ebuf, ji, :].rearrange("p (a b) -> p a b", a=8)[:, par * 4:(par + 1) * 4, 0:ntq],
                                    ji == 0, ji == len(jbs) - 1, waits=[e_ready] + wd_ if ji == 0 else (), sig=(ji == len(jbs) - 1))
                        act(rden[:, par, 0:4 * ntq], ps[:, bd, 0:4 * ntq], AF.Ln, waits=[lp, norm_done.get(qb - 1)])
                        e = act(rden[:, par, 0:4 * ntq], rden[:, par, 0:4 * ntq], AF.Exp, scale=-1.0, sig=True)
                        P.release(bd, e)
                    rd_ready = S.last("act")
                    bo, wo = P.alloc()
                    lp = None
                    for h in range(8):
                        c, par = h // 2, h % 2
                        pb = par * 64
                        for ji, (jb, slot, nkeys) in enumerate(jbs):
                            lp = mm(ps[pb:pb + 64, bo, c * 128:c * 128 + ntq], Vb[0:nkeys, slot, h * 64:(h + 1) * 64],
                                    Eb[0:nkeys, ebuf, ji, :].rearrange("p (a b) -> p a b", a=8)[:, par * 4 + c, 0:ntq],
                                    ji == 0, ji == len(jbs) - 1, waits=wo if (h == 0 and ji == 0) else (),
                                    sig=(h == 7 and ji == len(jbs) - 1))
                    e = None
                    for par in range(2):
                        pb = par * 64
                        e = stt(cat[pb:pb + 64, 0:4, qb * 128:qb * 128 + ntq],
                               ps[pb:pb + 64, bo, :].rearrange("p (a b) -> p a b", a=4)[:, :, 0:ntq], 1.0,
                               rden[pb:pb + 64, par, 0:4 * ntq].rearrange("p (a b) -> p a b", a=4), ALU.mult, ALU.mult,
                               waits=[lp, rd_ready] if par == 0 else (), sig=(par == 1))
                    P.release(bo, e)
                    pv_done[qb] = lp
                    norm_done[qb] = e
            if nblk > 0:
                att_scores(0)
            for qb in range(nblk):
                if qb + 1 < nblk:
                    att_scores(qb + 1)
                att_pv(qb)

            if stage_lim < 9:
                return
            wvo1 = wview(woo)
            groups = [([(wvo1, g * 512, 512)], "ws") for g in range(2)]
            gemm(groups, 8, lambda kc: cat[:, kc, :], NT, S.join("act", "dve", "pool"), residual_evac)
            if is_sample and n_prompt_tiles > 0:
                nxt = (False, 0)
            elif (not is_sample) and ti + 1 < n_prompt_tiles:
                nxt = (False, ti + 1)
            else:
                nxt = None
            if nxt is not None:
                pw = S.join("pe", "act", "dve")
                for pblk in range(4):
                    r0 = nxt[1] * TT + pblk * 128
                    ld = S.dma("sp", lambda e_, pblk=pblk, r0=r0: e_.dma_start(out=xpre[:, pblk, :], in_=xp[r0:r0 + 128, :]),
                               "xpre", pw if pblk == 0 else ())
                    prefetch[(nxt[0], nxt[1], pblk)] = ld
                for pblk in range(4):
                    prefetch[(nxt[0], nxt[1], pblk)] = ld
            xw = ffn(1, S.join("dve"))

            del y_dmas[:]
            for blk in range(nblk):
                rows = min(128, NT - blk * 128)
                ee = None
                for half in range(2):
                    b, bw = P.alloc()
                    lp = None
                    for c4 in range(4):
                        c = half * 4 + c4
                        lp = tr(ps[0:rows, b, c4 * 128:(c4 + 1) * 128], xT[:, c, blk * 128:blk * 128 + rows], ident[:, :],
                                waits=(list(xw) + bw) if c4 == 0 else (), sig=(c4 == 3))
                    ee = cp("act", ystage[0:rows, blk, half * 512:(half + 1) * 512], ps[0:rows, b, :], waits=[lp], sig=True)
                    P.release(b, ee)
                y_dmas.append(S.dma("sp", lambda e_, rows=rows, blk=blk: e_.dma_start(
                    out=ydst[t0 + blk * 128:t0 + blk * 128 + rows, :], in_=ystage[0:rows, blk, :]), "ysem", [ee]))

        if do_sample:
            S.strict = True
            process_tile(True, 0)
            S.strict = False
        for ti in range(n_prompt_tiles):
            process_tile(False, ti)
        if stage_lim < 99:
            jw = S.join()
            out_evts.append(S.dma("sp", lambda e_: e_.dma_start(out=dbg_x[:, :], in_=xT[:, :, :].rearrange("p a b -> p (a b)")), "outs", jw))
            out_evts.append(S.dma("pool", lambda e_: e_.dma_start(out=dbg_c[:, :], in_=cat[:, :, :].rearrange("p a b -> p (a b)")), "outs", jw))
            out_evts.append(S.dma("pool", lambda e_: e_.dma_start(out=dbg_h[:, :], in_=hT[:, :, :].rearrange("p a b -> p (a b)")), "outs", jw))

        final_waits = [(k, S.dmac[k]) for k in ("sts0", "sts1", "outs", "ysem") if S.dmac.get(k, 0) > 0]
        S.add("sp", lambda e: e.nop(), waits=final_waits + S.join())

        handles = {}
        with nc.Block() as block:
            CENG = ("pe", "act", "dve", "pool")
            waited = {e: set() for e in CENG}
            for eng_ in ENG:
                for fn, waits, sigkey, inc in S.ops[eng_]:
                    wm = {}
                    for (k, v) in waits:
                        wm[k] = max(wm.get(k, 0), v)
                    for (k, v) in wm.items():
                        if k in waited:
                            waited[k].add(v)
            newval = {e: {} for e in CENG}
            for eng_ in CENG:
                n_old = 0
                n_new = 0
                for fn, waits, sigkey, inc in S.ops[eng_]:
                    if sigkey == eng_:
                        n_old += 1
                        if n_old in waited[eng_]:
                            n_new += 1
                            newval[eng_][n_old] = n_new

            def emit(eng_handle, eng):
                seen = {}
                n_old = 0
                for fn, waits, sigkey, inc in S.ops[eng]:
                    wmax = {}
                    for (k, v) in waits:
                        wmax[k] = max(wmax.get(k, 0), v)
                    for (k, v) in wmax.items():
                        if k in newval:
                            v = newval[k][v]
                        if seen.get(k, 0) >= v:
                            continue
                        eng_handle.wait_ge(sems[k], v)
                        seen[k] = v
                    ins = fn(eng_handle)
                    if sigkey is not None:
                        if sigkey in newval:
                            n_old += 1
                            if n_old in newval[sigkey]:
                                ins.then_inc(sems[sigkey], inc)
                        else:
                            ins.then_inc(sems[sigkey], inc)

            @block.tensor
            def _(e):
                emit(e, "pe")

            @block.scalar
            def _(e):
                emit(e, "act")

            @block.vector
            def _(e):
                emit(e, "dve")

            @block.gpsimd
            def _(e):
                emit(e, "pool")

            @block.sync
            def _(e):
                emit(e, "sp")
    return nc


_CACHE = {}


def kernel(x_prompt, x_sample, cache_conv_a, cache_conv_b, cache_k, cache_v,
           norm_mix_even, w_in_even, conv_a_w, conv_a_b, ln_a_g, ln_a_b, conv_b_w, w_out_even,
           norm_mix_odd, w_in_odd, q_norm_g, k_norm_g, rel_bias, sgu_ln_g, sgu_ln_b, sgu_w, sgu_b,
           w_out_odd, norm_ffn, w_gate_up, w_down):
    npt = NPT
    f = lambda a: np.ascontiguousarray(np.asarray(a, dtype=np.float32))
    key = npt
    if key not in _CACHE:
        _CACHE[key] = build_program(npt, True)
    nc = _CACHE[key]
    shared = {
        "nme": f(norm_mix_even[0]), "wie": f(w_in_even[0]), "caw": f(conv_a_w[0]), "cab": f(conv_a_b[0]),
        "lag": f(ln_a_g[0]), "lab": f(ln_a_b[0]), "cbw": f(conv_b_w[0]), "woe": f(w_out_even[0]),
        "nmo": f(norm_mix_odd[0]), "wio": f(w_in_odd[0]), "qg": f(q_norm_g[0]), "kg": f(k_norm_g[0]),
        "rb": f(rel_bias[0]), "slg": f(sgu_ln_g[0]), "slb": f(sgu_ln_b[0]), "sw": f(sgu_w[0]), "sb": f(sgu_b[0]),
        "woo": f(w_out_odd[0]), "nf": f(norm_ffn), "wgu": f(w_gate_up), "wd": f(w_down),
    }
    xpn, xsn = np.asarray(x_prompt), np.asarray(x_sample)
    cca, ccb = np.asarray(cache_conv_a), np.asarray(cache_conv_b)
    ckn, cvn = np.asarray(cache_k), np.asarray(cache_v)
    in_maps = []
    for b in range(8):
        m = dict(shared)
        m["xp"] = f(xpn[b]); m["xs"] = f(xsn[b])
        m["cca"] = f(cca[0, b]); m["ccb"] = f(ccb[0, b])
        m["ck"] = f(ckn[0, b].reshape(512, 512)); m["cv"] = f(cvn[0, b].reshape(512, 512))
        in_maps.append(m)
    res = run_bass_kernel_spmd(nc, in_maps, core_ids=list(range(8)))
    r = res.results
    st = lambda k: np.stack([np.asarray(r[b][k], dtype=np.float32) for b in range(8)])
    y_p = st("yp"); y_s = st("ys")
    ca_p = st("cap")[None]; cb_p = st("cbp")[None]
    k_p = st("kp").reshape(8, 512, 8, 64)[None]; v_p = st("vp").reshape(8, 512, 8, 64)[None]
    ca_s = st("cas")[None]; cb_s = st("cbs")[None]
    k_s = st("ks").reshape(8, NS, 8, 64)[None]; v_s = st("vs").reshape(8, NS, 8, 64)[None]
    sv_s = st("svs")[None]
    return (y_p, y_s, ca_p, cb_p, k_p, v_p, ca_s, cb_s, k_s, v_s, sv_s)
```

```python
import os
from contextlib import ExitStack

import numpy as np
import concourse.bass as bass
import concourse.mybir as mybir
from concourse.bass_utils import run_bass_kernel_spmd

F32 = mybir.dt.float32
BF16 = mybir.dt.bfloat16
AF = mybir.ActivationFunctionType
ALU = mybir.AluOpType

D = 1024
TT = 512
SEQ = 4096
NS = 16
NPT = SEQ // TT
R_SLOTS = 4
NEG = -30000.0

ENG = ("pe", "act", "dve", "pool", "sp")


def _ap_rng(ap):
    esz = 4 if ap.dtype == F32 else 2
    pairs = ap.ap
    pstep, pcnt = pairs[0]
    off = ap.offset
    if pstep > 0:
        p0, within = off // pstep, off % pstep
    else:
        p0, within = 0, off
    lo = hi = within
    for st, cnt in pairs[1:]:
        if st >= 0:
            hi += st * (cnt - 1)
        else:
            lo += st * (cnt - 1)
    return (ap.tensor.name, p0, p0 + pcnt, lo * esz, (hi + 1) * esz)


def _ovl(a, b):
    return a[0] == b[0] and a[1] < b[2] and b[1] < a[2] and a[3] < b[4] and b[3] < a[4]


class Sched:
    def __init__(self):
        self.ops = {e: [] for e in ENG}
        self.ms = {e: 0 for e in ENG}
        self.dmac = {}
        self.strict = False
        self.hist = {e: [] for e in ("act", "dve", "pool")}

    def add(self, eng, fn, waits=(), sig=False, rd=None, wr=None):
        w = [x for x in waits if x is not None]
        if eng in self.hist:
            sig = True
            h = self.hist[eng]
            if self.strict or rd is None or wr is None:
                if self.ms[eng] > 0:
                    w.append((eng, self.ms[eng]))
                del h[:]
                rds, wrs = None, None
            else:
                rds = [_ap_rng(a) for a in rd]
                wrs = [_ap_rng(a) for a in wr]
                for k in range(len(h) - 1, -1, -1):
                    ev, prd, pwr = h[k]
                    hit = prd is None
                    if not hit:
                        for x in wrs:
                            if any(_ovl(x, y) for y in prd) or any(_ovl(x, y) for y in pwr):
                                hit = True
                                break
                    if not hit:
                        for x in rds:
                            if any(_ovl(x, y) for y in pwr):
                                hit = True
                                break
                    if hit:
                        w.append(ev)
                        del h[:k + 1]
                        break
                if len(h) > 96:
                    del h[:len(h) - 96]
        self.ops[eng].append((fn, w, eng if sig else None, 1))
        if sig:
            self.ms[eng] += 1
            evt = (eng, self.ms[eng])
            if eng in self.hist:
                self.hist[eng].append((evt, rds, wrs))
            return evt
        return None

    def dma(self, eng, fn, semkey, waits=()):
        w = [x for x in waits if x is not None]
        self.ops[eng].append((fn, w, semkey, 16))
        self.dmac[semkey] = self.dmac.get(semkey, 0) + 16
        return (semkey, self.dmac[semkey])

    def last(self, eng):
        return (eng, self.ms[eng]) if self.ms[eng] > 0 else None

    def join(self, *engs):
        return [self.last(e) for e in (engs or ("pe", "act", "dve", "pool"))]


class Psum:
    def __init__(self):
        self.free = [[] for _ in range(8)]
        self.busy = [False] * 8
        self.nxt = 0

    def alloc(self):
        for k in range(8):
            b = (self.nxt + k) % 8
            if not self.busy[b]:
                break
        else:
            raise RuntimeError("PSUM: all 8 banks busy")
        self.nxt = (b + 1) % 8
        self.busy[b] = True
        w = self.free[b]
        self.free[b] = []
        return b, list(w)

    def release(self, b, *evts):
        self.free[b] = [e for e in evts if e is not None]
        self.busy[b] = False


def build_program(n_prompt_tiles=NPT, do_sample=True, stage_lim=99):
    nc = bass.Bass("TRN2", target_bir_lowering=False)
    S = Sched()
    P = Psum()

    def din(name, shape):
        return nc.dram_tensor(name, list(shape), F32, kind="ExternalInput").ap()

    def dout(name, shape):
        return nc.dram_tensor(name, list(shape), F32, kind="ExternalOutput").ap()

    xp = din("xp", [SEQ, D]); xs = din("xs", [NS, D])
    cca = din("cca", [30, 512]); ccb = din("ccb", [2, 512])
    ck = din("ck", [512, 512]); cv = din("cv", [512, 512])
    nme = din("nme", [D]); wie = din("wie", [D, 2560]); caw = din("caw", [31, 512])
    cab = din("cab", [512]); lag = din("lag", [512]); lab = din("lab", [512])
    cbw = din("cbw", [3, 512]); woe = din("woe", [D, D]); nmo = din("nmo", [D])
    wio = din("wio", [D, 2560]); qg = din("qg", [64]); kg = din("kg", [64])
    rb = din("rb", [8, 513]); slg = din("slg", [512]); slb = din("slb", [512])
    sw = din("sw", [4, 128, 128]); sb = din("sb", [4, 128]); woo = din("woo", [D, D])
    nf = din("nf", [2, D]); wgu = din("wgu", [2, D, 5632]); wd = din("wd", [2, 2816, D])

    yp = dout("yp", [SEQ, D]); ys = dout("ys", [NS, D])
    cap = dout("cap", [30, 512]); cbp = dout("cbp", [2, 512])
    kp = dout("kp", [512, 512]); vp = dout("vp", [512, 512])
    cas = dout("cas", [30, 512]); cbs = dout("cbs", [2, 512])
    ks = dout("ks", [NS, 512]); vs = dout("vs", [NS, 512]); svs = dout("svs", [NS, 512])

    if stage_lim < 99:
        dbg_x = dout("dbg_x", [128, 8 * TT]); dbg_c = dout("dbg_c", [128, 8 * TT]); dbg_h = dout("dbg_h", [128, 8 * TT])
    ext_h = nc.dram_tensor("ext_scratch", [8, 128 * 769], F32, kind="Internal")
    ext = ext_h.ap()

    with ExitStack() as es:
        def sb_t(name, shape, dt):
            return es.enter_context(nc.sbuf_tensor(name, list(shape), dt))

        ps = es.enter_context(nc.psum_tensor("ps", [128, 8, 512], F32))
        xT = sb_t("xT", [128, 8, TT], F32)
        hT = sb_t("hT", [128, 8, TT], BF16)
        cat = hT
        ring = sb_t("ring", [128, R_SLOTS, 8, 512], BF16)
        Tb = sb_t("Tb", [128, 5, 8, 128], BF16)
        ident = sb_t("ident", [128, 128], F32)
        identb = sb_t("identb", [128, 128], BF16)
        o1024 = sb_t("o1024", [128, 128], BF16)
        o512 = sb_t("o512", [128, 128], BF16)
        blk64 = sb_t("blk64", [128, 128], BF16)
        ones1 = sb_t("ones1", [128, 128], BF16)
        vecT = sb_t("vecT", [128, 64], F32)
        cawT = sb_t("cawT", [128, 124], F32)
        wsT = sb_t("wsT", [128, 4, 128], BF16)
        bsrow = sb_t("bsrow", [1, 512], BF16)
        dg = sb_t("dg", [128, 124, 128], BF16)
        slg_bc = sb_t("slg_bc", [128, 512], F32)
        slb_bc = sb_t("slb_bc", [128, 512], F32)
        eps6 = sb_t("eps6", [128, 1], F32)
        lutscr = sb_t("lutscr", [128, 1], F32)
        eps5 = sb_t("eps5", [128, 1], F32)
        kT = sb_t("kT", [128, 4, 1024], BF16)
        Vb = sb_t("Vb", [128, 8, 512], BF16)
        hist_a = sb_t("hist_a", [128, 4, 30], F32)
        hist_b = sb_t("hist_b", [128, 4, 2], F32)
        stage = sb_t("stage", [128, 2, 1024], F32)
        rstd = sb_t("rstd", [128, TT], F32)
        st6 = sb_t("st6", [128, 4, 6], F32)
        mv = sb_t("mv", [128, 4, 2], F32)
        rsd = sb_t("rsd", [128, 4], F32)
        U = sb_t("U", [128, 18432], F32)
        Ub = U.bitcast(BF16)

        def uf32(off, *shape):
            n = int(np.prod(shape))
            v = U[:, off:off + n]
            if len(shape) == 2:
                v = v.rearrange("p (a b) -> p a b", a=shape[0])
            elif len(shape) == 3:
                v = v.rearrange("p (a b c) -> p a b c", a=shape[0], b=shape[1])
            return v

        def ubf(off32, *shape):
            n = int(np.prod(shape))
            v = Ub[:, 2 * off32:2 * off32 + n]
            if len(shape) == 2:
                v = v.rearrange("p (a b) -> p a b", a=shape[0])
            elif len(shape) == 3:
                v = v.rearrange("p (a b c) -> p a b c", a=shape[0], b=shape[1])
            return v

        abuf = uf32(0, 4, 544)
        zbuf = uf32(2176, 4, 516)
        gbb = uf32(4240, 4, 512)
        sgs = uf32(6288, 4, 512)
        acv = uf32(8336, 4, 512)
        xbf = ubf(10384, 4, 512)
        sqb0 = ubf(11408, 4, 512)
        hid = ubf(0, 22, 512)
        stmp = uf32(5632, 2, 512)
        qf = uf32(0, 4, 512)
        sqb1 = ubf(2048, 4, 512)
        kf = uf32(3072, 4, 512)
        qT = ubf(5120, 4, 512)
        uf = uf32(6144, 4, 512)
        tsv = uf32(8192, 2, 512)
        svn = ubf(9216, 4, 512)
        tbuf = uf32(10240, 2, 8, 128)
        Eb = ubf(12288, 2, 5, 1024)
        rden = uf32(17408, 2, 512)
        T32 = uf32(0, 5, 1024)
        VbF = Vb.bitcast(F32)
        extS = VbF[0:8, 5:8, :].rearrange("p a b -> p (a b)")
        bsrow32 = U[0:1, 10000:10512]
        abuf16 = ubf(12432, 4, 544)
        lnt = uf32(13520, 4, 512)

        sems = {}
        for k in ("pe", "act", "dve", "pool"):
            sems[k] = es.enter_context(nc.semaphore("m_" + k))
        for k in ["cst", "cvl", "ex1", "tbl", "tb2", "ysem", "xpre", "outs", "stl0", "stl1", "sts0", "sts1"] + ["ring%d" % i for i in range(R_SLOTS)]:
            sems[k] = es.enter_context(nc.semaphore("d_" + k))

        def act(out, in_, func, waits=(), sig=False, scale=None, bias=None):
            def fn(e):
                kw = {}
                if scale is not None:
                    kw["scale"] = scale
                if bias is not None:
                    kw["bias"] = bias
                return e.activation(out=out, in_=in_, func=func, **kw)
            rd = [in_] + [x for x in (scale, bias) if x is not None and not isinstance(x, (int, float))]
            return S.add("act", fn, waits, sig, rd=rd, wr=[out])

        def lut_prefetch(func):
            return S.add("act", lambda e: e.activation(out=lutscr[:, :], in_=eps6[:, :], func=func), rd=[eps6[:, :]], wr=[lutscr[:, :]])

        def tt(eng, out, in0, in1, op, waits=(), sig=False):
            return S.add(eng, lambda e: e.tensor_tensor(out=out, in0=in0, in1=in1, op=op), waits, sig, rd=[in0, in1], wr=[out])

        def stt(out, in0, scalar, in1, op0, op1, waits=(), sig=False):
            rd = [in0, in1] + ([scalar] if not isinstance(scalar, (int, float)) else [])
            return S.add("dve", lambda e: e.scalar_tensor_tensor(out=out, in0=in0, scalar=scalar, in1=in1, op0=op0, op1=op1), waits, sig,
                         rd=rd, wr=[out])

        def ts(eng, out, in0, s1, s2, op0, op1=None, waits=(), sig=False):
            def fn(e):
                if op1 is None:
                    return e.tensor_scalar(out=out, in0=in0, scalar1=s1, scalar2=None, op0=op0)
                return e.tensor_scalar(out=out, in0=in0, scalar1=s1, scalar2=s2, op0=op0, op1=op1)
            rd = [in0] + [x for x in (s1, s2) if x is not None and not isinstance(x, (int, float))]
            return S.add(eng, fn, waits, sig, rd=rd, wr=[out])

        def cp(eng, out, in_, waits=(), sig=False):
            if eng == "act":
                return act(out, in_, AF.Copy, waits, sig)
            return S.add(eng, lambda e: e.tensor_copy(out=out, in_=in_), waits, sig, rd=[in_], wr=[out])

        def mm(out, lhsT, rhs, start, stop, waits=(), sig=False):
            return S.add("pe", lambda e: e.matmul(out, lhsT, rhs, start=start, stop=stop), waits, sig)

        def tr(out, in_, idn, waits=(), sig=False):
            return S.add("pe", lambda e: e.transpose(out, in_, idn), waits, sig)

        stage_free = [[], []]
        stage_n = [0]

        def stage_alloc():
            s = stage_n[0] % 2
            stage_n[0] += 1
            w = stage_free[s]
            stage_free[s] = []
            return s, list(w)

        ring_n = [0]
        ring_free = [[] for _ in range(R_SLOTS)]

        def ring_load(pieces):
            slot = ring_n[0] % R_SLOTS
            ring_n[0] += 1
            w = ring_free[slot]
            ring_free[slot] = []
            evt = None
            for i, (src, nkc, dcol, ncols) in enumerate(pieces):
                def fn(e, src=src, nkc=nkc, dcol=dcol, ncols=ncols, slot=slot):
                    return e.dma_start(out=ring[:, slot, 0:nkc, dcol:dcol + ncols], in_=src)
                evt = S.dma("pool", fn, "ring%d" % slot, w if i == 0 else ())
            return slot, evt

        def wview(w_ap):
            return w_ap.rearrange("(kc p) n -> p kc n", p=128)

        def gemm(groups, KC, in_chunk, NT, in_waits, evac, mode="ws", kc_evts=None):
            first = True
            deferred = []
            for gi, (pieces, gmode) in enumerate(groups):
                nslots = (KC + 7) // 8
                slot_info = []
                for si in range(nslots):
                    kc0 = si * 8
                    nkc = min(8, KC - kc0)
                    pl = []
                    dcol = 0
                    for (wv, c0, ncols) in pieces:
                        pl.append((wv[:, kc0:kc0 + nkc, c0:c0 + ncols], nkc, dcol, ncols))
                        dcol += ncols
                    slot_info.append((ring_load(pl), kc0, nkc))
                n_out = 4 if gmode == "ws" else (NT + 127) // 128
                banks = []
                for i in range(n_out):
                    banks.append(P.alloc())
                last_evt = [None] * n_out
                for si, ((slot, levt), kc0, nkc) in enumerate(slot_info):
                    for i in range(n_out):
                        b, bw = banks[i]
                        for k in range(nkc):
                            kc = kc0 + k
                            w = []
                            if k == 0:
                                w = [levt] + (bw if si == 0 else [])
                                if first:
                                    w += list(in_waits)
                                    first = False
                            if kc_evts is not None and gi == 0 and i == 0:
                                w = list(w) + [kc_evts[kc]]
                            is_last_k = (k == nkc - 1)
                            sig = is_last_k and (si == nslots - 1 or i == n_out - 1)
                            if gmode == "ws":
                                e = mm(ps[:, b, 0:NT], ring[:, slot, k, i * 128:(i + 1) * 128], in_chunk(kc)[:, 0:NT],
                                       kc == 0, kc == KC - 1, w, sig)
                            else:
                                rows = min(128, NT - i * 128)
                                e = mm(ps[0:rows, b, 0:512], in_chunk(kc)[:, i * 128:i * 128 + rows], ring[:, slot, k, 0:512],
                                       kc == 0, kc == KC - 1, w, sig)
                            if sig:
                                last_evt[i] = e
                    ring_free[slot] = [S.last("pe")]
                for d in deferred:
                    d()
                deferred = []
                for i in range(n_out):
                    r = evac(gi, i, banks[i][0], last_evt[i])
                    if r is not None:
                        deferred.append(r)
            for d in deferred:
                d()

        cst = []

        def cdma(out, in_, eng="sp", waits=()):
            e = S.dma(eng, lambda e_: e_.dma_start(out=out, in_=in_), "cst", waits)
            cst.append(e)
            return e

        VC = {}
        row = 0
        s0 = stage[:, 0, :]
        s1 = stage[:, 1, :]

        def vrows(name, ap2d, n):
            nonlocal row
            VC[name] = row
            cdma(s0[row:row + n, 0:128], ap2d)
            row += n

        vrows("nme", nme.rearrange("(c p) -> c p", p=128), 8)
        vrows("nmo", nmo.rearrange("(c p) -> c p", p=128), 8)
        vrows("nf", nf.rearrange("l (c p) -> (l c) p", p=128), 16)
        vrows("cab", cab.rearrange("(c p) -> c p", p=128), 4)
        vrows("lag", lag.rearrange("(c p) -> c p", p=128), 4)
        vrows("lab", lab.rearrange("(c p) -> c p", p=128), 4)
        vrows("cbw", cbw.rearrange("k (c p) -> (k c) p", p=128), 12)
        qg2 = qg.rearrange("(o d) -> o d", o=1)
        kg2 = kg.rearrange("(o d) -> o d", o=1)
        VC["qg"] = row
        cdma(s0[row:row + 1, 0:64], qg2)
        cdma(s0[row:row + 1, 64:128], qg2)
        row += 1
        VC["kg"] = row
        cdma(s0[row:row + 1, 0:64], kg2)
        cdma(s0[row:row + 1, 64:128], kg2)
        row += 1
        NV = row
        cdma(s0[0:124, 128:256], caw.rearrange("k (c p) -> (k c) p", p=128))
        for g in range(4):
            cdma(s0[:, 256 + g * 128:256 + (g + 1) * 128], sw[g])
        cdma(bsrow32[0:1, :], sb.rearrange("(o g) i -> o (g i)", o=1))
        cdma(slg_bc[:, :], bass.AP(tensor=slg.tensor, offset=0, ap=[[0, 128], [1, 512]]))
        cdma(slb_bc[:, :], bass.AP(tensor=slb.tensor, offset=0, ap=[[0, 128], [1, 512]]))
        W_ = 768
        GS = 128 * (W_ + 1)
        d0 = S.dma("sp", lambda e_: e_.dma_start(out=extS[:, 0:384], in_=rb[:, 129:513]), "ex1")
        ts("dve", extS[:, 384:768], extS[:, 0:384], 0.0, extS[:, 383:384], ALU.mult, ALU.add, waits=[d0], sig=True)
        ef = ts("dve", extS[:, :], extS[:, :], 8.0, None, ALU.mult, sig=True)
        cst_all = [("cst", S.dmac["cst"])]
        late = {"done": False, "done2": False}

        def late_prologue():
            if late["done"]:
                return
            late["done"] = True
            strict0 = S.strict
            S.strict = False
            for j in range(124):
                ts("dve", dg[:, j, :], ident[:, :], cawT[:, j:j + 1], None, ALU.mult)
            late["dg_ready"] = [S.add("dve", lambda e: e.nop(), sig=True)]
            S.strict = strict0
            gl = []
            for h in range(8):
                gl.append(S.dma("act", lambda e_, h=h: e_.dma_start(
                    out=bass.AP(tensor=ext_h, offset=h * GS, ap=[[W_ + 1, 128], [1, 768]]),
                    in_=bass.AP(tensor=VbF, offset=h * 2048 + 1280, ap=[[2048, 1], [0, 128], [1, 768]])),
                    "tbl", [ef] if h == 0 else ()))
            late["tbl_done"] = gl[-1]

        def late_prologue2():
            if late["done2"]:
                return
            late["done2"] = True
            tbl_done = late["tbl_done"]
            last = None
            for jb in range(5):
                for par in range(2):
                    src = bass.AP(tensor=ext_h, offset=par * GS + 639 - jb * 128, ap=[[W_, 128], [2 * GS, 4], [1, 128]])
                    last = S.dma("pool", lambda e_, jb=jb, par=par, src=src: e_.dma_start(
                        out=Tb[:, jb, par * 4:(par + 1) * 4, :], in_=src), "tb2", [tbl_done] if (jb == 0 and par == 0) else ())
            S.add("pool", lambda e: e.memset(Tb[0:64, 0, :, 64:128], NEG), waits=[last])
            late["tb_ready"] = S.add("pool", lambda e: e.memset(Tb[64:128, 4, :, 0:64], NEG), sig=True)

        S.strict = True
        S.add("pool", lambda e: e.memset(ident[:, :], 1.0))
        S.add("pool", lambda e: e.affine_select(out=ident[:, :], in_=ident[:, :], pattern=[[-1, 128]],
                                                compare_op=ALU.is_equal, fill=0.0, base=0, channel_multiplier=1))
        S.add("pool", lambda e: e.memset(o1024[:, :], 1.0 / 1024))
        S.add("pool", lambda e: e.memset(o512[:, :], 1.0 / 512))
        S.add("pool", lambda e: e.memset(ones1[:, :], 1.0))
        S.add("pool", lambda e: e.memset(blk64[:, :], 0.0))
        S.add("pool", lambda e: e.memset(blk64[0:64, 0:64], 1.0 / 64))
        S.add("pool", lambda e: e.memset(blk64[64:128, 64:128], 1.0 / 64))
        S.add("pool", lambda e: e.memset(eps6[:, :], 1e-6))
        S.add("pool", lambda e: e.memset(eps5[:, :], 1e-5))
        S.add("pool", lambda e: e.memset(hist_a[:, :, :], 0.0))
        pool_c = S.add("pool", lambda e: e.memset(hist_b[:, :, :], 0.0), sig=True)
        S.strict = False

        b0, w0 = P.alloc()
        tr(ps[:, b0, 0:NV], s0[0:NV, 0:128], ident[0:NV, 0:NV], waits=cst_all + [pool_c] + w0)
        tr(ps[:, b0, 128:252], s0[0:124, 128:256], ident[0:124, 0:124])
        b1, w1 = P.alloc()
        for g in range(4):
            ev = tr(ps[:, b1, g * 128:(g + 1) * 128], s0[:, 256 + g * 128:256 + (g + 1) * 128], ident[:, :],
                    waits=w1 if g == 0 else (), sig=(g == 3))
        cp("dve", vecT[:, 0:NV], ps[:, b0, 0:NV], waits=[ev])
        cp("dve", cawT[:, :], ps[:, b0, 128:252])
        cp("dve", identb[:, :], ident[:, :])
        cp("dve", sgs[:, 0, :], ps[:, b1, :])
        ev2 = cp("dve", bsrow[0:1, :], bsrow32[0:1, :], sig=True)
        P.release(b0, ev2)
        P.release(b1, ev2)
        for g in range(4):
            S.add("pool", lambda e, g=g: e.affine_select(out=wsT[:, g, :], in_=sgs[:, 0, g * 128:(g + 1) * 128],
                                                         pattern=[[1, 128]], compare_op=ALU.is_ge, fill=0.0, base=0,
                                                         channel_multiplier=-1), waits=[ev2] if g == 0 else ())
        pool_c2 = S.add("pool", lambda e: e.nop(), sig=True)
        stage_free[0] = [ev]

        if do_sample:
            for half in range(2):
                sslot, sw_ = stage_alloc()
                ld = S.dma("sp", lambda e_, half=half, sslot=sslot: e_.dma_start(
                    out=stage[:, sslot, :].rearrange("p (a b) -> p a b", a=2),
                    in_=ck[half * 256:(half + 1) * 256, :].rearrange("(a p) n -> p a n", p=128)), "stl%d" % sslot, sw_)
                lastpe = None
                for a in range(2):
                    blk = half * 2 + a
                    b, bw = P.alloc()
                    for c in range(4):
                        lastpe = tr(ps[:, b, c * 128:(c + 1) * 128], stage[:, sslot, a * 512 + c * 128:a * 512 + (c + 1) * 128],
                                    ident[:, :], waits=[ld, ev2] + bw if c == 0 else (), sig=(c == 3))
                    e = cp("act", kT[:, :, blk * 128:(blk + 1) * 128], ps[:, b, :].rearrange("p (c t) -> p c t", c=4),
                           waits=[lastpe], sig=True)
                    P.release(b, e)
                stage_free[sslot] = [lastpe]
            cvl = S.dma("pool", lambda e_: e_.dma_start(out=Vb[:, 0:4, :], in_=cv.rearrange("(a p) n -> p a n", p=128)), "cvl")
        else:
            cvl = None

        out_evts = []
        prefetch = {}
        y_dmas = []
        ystage = uf32(0, 4, 1024)
        xpre = uf32(14336, 4, 1024)

        def process_tile(is_sample, ti):
            NT = NS if is_sample else TT
            nblk = (NT + 127) // 128
            xsrc = xs if is_sample else xp
            ydst = ys if is_sample else yp
            t0 = 0 if is_sample else ti * TT
            last_tile = (not is_sample) and (ti == n_prompt_tiles - 1)

            x_ready = []
            for blk in range(nblk):
                rows = min(128, NT - blk * 128)
                if (is_sample, ti, blk) in prefetch:
                    ld = prefetch.pop((is_sample, ti, blk))
                    xin = xpre[:, blk, :]
                    sslot = None
                else:
                    sslot, sw_ = stage_alloc()
                    ld = S.dma("sp", lambda e_, blk=blk, rows=rows, sslot=sslot: e_.dma_start(
                        out=stage[0:rows, sslot, :], in_=xsrc[t0 + blk * 128:t0 + blk * 128 + rows, :]), "stl%d" % sslot, sw_)
                    xin = stage[:, sslot, :]
                lastpe = None
                for half in range(2):
                    b, bw = P.alloc()
                    for c4 in range(4):
                        c = half * 4 + c4
                        lastpe = tr(ps[:, b, c4 * 128:c4 * 128 + rows], xin[0:rows, c * 128:(c + 1) * 128],
                                    ident[0:rows, 0:rows], waits=([ld] + bw + (S.join("act", "dve", "pool") + list(out_evts) if (blk == 0 and half == 0) else [])) if c4 == 0 else (), sig=(c4 == 3))
                    e = cp("dve", xT[:, half * 4:(half + 1) * 4, blk * 128:blk * 128 + rows],
                           ps[:, b, :].rearrange("p (c t) -> p c t", c=4)[:, :, 0:rows], waits=[lastpe], sig=True)
                    P.release(b, e)
                    x_ready = [e]
                if sslot is not None:
                    stage_free[sslot] = [lastpe]

            if stage_lim < 1:
                return
            def rmsnorm(gcol, x_waits, chunk_evts=None, next_func=None):
                hfree = S.join("pe")
                lut_prefetch(AF.Ln)
                sq = []
                for c in range(8):
                    w = [chunk_evts[c]] if chunk_evts else []
                    if c == 0:
                        w += list(x_waits) + hfree
                    sq.append(act(hT[:, c, 0:NT], xT[:, c, 0:NT], AF.Square, waits=w, sig=True))
                b, bw = P.alloc()
                e2_ = None
                for c in range(8):
                    e2_ = mm(ps[:, b, 0:NT], o1024[:, :], hT[:, c, 0:NT], c == 0, c == 7, waits=[sq[c]] + (bw if c == 0 else []), sig=(c == 7))
                act(rstd[:, 0:NT], ps[:, b, 0:NT], AF.Ln, waits=[e2_], bias=eps6[:, :])
                e3_ = act(ps[:, b, 0:NT], rstd[:, 0:NT], AF.Exp, scale=-0.5, sig=True)
                if next_func is not None:
                    lut_prefetch(next_func)
                hev = []
                for c in range(8):
                    hev.append(stt(hT[:, c, 0:NT], ps[:, b, 0:NT], vecT[:, gcol + c:gcol + c + 1], xT[:, c, 0:NT], ALU.mult, ALU.mult,
                                   waits=[e3_] if c == 0 else (), sig=True))
                P.release(b, hev[-1])
                return hev

            res_evt = {}

            def residual_evac(gi, i, b, pe_evt):
                c = gi * 4 + i
                e = stt(xT[:, c, 0:NT], ps[:, b, 0:NT], 1.0, xT[:, c, 0:NT], ALU.mult, ALU.add, waits=[pe_evt], sig=True)
                res_evt[c] = e
                P.release(b, e)

            def ffn(layer, x_waits):
                h_ready = rmsnorm(VC["nf"] + 8 * layer, [], dict(res_evt), next_func=AF.Silu)
                if stage_lim < 5.3:
                    return S.join("dve")
                wg = wview(wgu[layer])
                groups = []
                for j in range(11):
                    groups.append(([(wg, j * 256, 256), (wg, 2816 + j * 256, 256)], "ws"))
                pend = {}

                def evac_gu(gi, i, b, pe_evt):
                    if i < 2:
                        e = act(stmp[:, i, 0:NT], ps[:, b, 0:NT], AF.Silu, waits=[pe_evt] + S.join("dve"), sig=True)
                        P.release(b, e)
                        pend[i] = e
                    else:
                        e = stt(hid[:, 2 * gi + (i - 2), 0:NT], ps[:, b, 0:NT], 1.0, stmp[:, i - 2, 0:NT], ALU.mult, ALU.mult,
                               waits=[pe_evt, pend[i - 2]], sig=True)
                        P.release(b, e)

                gemm(groups, 8, lambda kc: hT[:, kc, :], NT, [], evac_gu, kc_evts=h_ready)
                if stage_lim < 5.6:
                    return S.join("dve")
                wdv = wview(wd[layer])
                groups = [([(wdv, g * 512, 512)], "ws") for g in range(2)]
                gemm(groups, 22, lambda kc: hid[:, kc, :], NT, S.join("act", "dve"), residual_evac)
                return S.join("dve")

            h_ready = rmsnorm(VC["nme"], x_ready, next_func=AF.Sigmoid)
            first_tile = not late["done"]
            late_prologue()
            if stage_lim < 0.5:
                return
            if is_sample:
                sslot, sw_ = stage_alloc()
                ld = S.dma("sp", lambda e_, sslot=sslot: e_.dma_start(out=stage[0:30, sslot, 0:512], in_=cca[:, :]), "stl%d" % sslot, sw_)
                ld = S.dma("sp", lambda e_, sslot=sslot: e_.dma_start(out=stage[0:2, sslot, 512:1024], in_=ccb[:, :]), "stl%d" % sslot)
                b, bw = P.alloc()
                lastpe = None
                for c in range(4):
                    tr(ps[:, b, c * 32:c * 32 + 30], stage[0:30, sslot, c * 128:(c + 1) * 128], ident[0:30, 0:30],
                       waits=[ld] + bw + S.join("act", "dve", "pool") if c == 0 else ())
                    lastpe = tr(ps[:, b, 128 + c * 32:128 + c * 32 + 2], stage[0:2, sslot, 512 + c * 128:512 + (c + 1) * 128],
                                ident[0:2, 0:2], sig=(c == 3))
                cp("act", abuf[:, :, 0:30], ps[:, b, 0:128].rearrange("p (c t) -> p c t", c=4)[:, :, 0:30], waits=[lastpe])
                h16 = cp("act", abuf16[:, :, 0:30], ps[:, b, 0:128].rearrange("p (c t) -> p c t", c=4)[:, :, 0:30], sig=True)
                e = cp("act", zbuf[:, :, 0:2], ps[:, b, 128:256].rearrange("p (c t) -> p c t", c=4)[:, :, 0:2], sig=True)
                hz = e
                P.release(b, e)
                stage_free[sslot] = [lastpe]
            else:
                cp("act", abuf[:, :, 0:30], hist_a[:, :, :], waits=S.join("pe", "dve", "pool") + list(y_dmas))
                h16 = cp("act", abuf16[:, :, 0:30], hist_a[:, :, :], sig=True)
                hz = cp("act", zbuf[:, :, 0:2], hist_b[:, :, :], sig=True)

            if stage_lim < 0.7:
                return
            wv0 = wview(wie)
            l0_order = [0, 1, 3, 4, 2]
            groups = [([(wv0, g * 512, 512)], "ws") for g in l0_order]
            sig_evt = {}
            a16_evt = {}
            convA_state = {}

            def convA_all():
                for c in range(4):
                    b, bw = P.alloc()
                    lp = None
                    for k in range(31):
                        lp = mm(ps[:, b, 0:NT], dg[:, k * 4 + c, :], abuf16[:, c, k:k + NT], k == 0, k == 30,
                                waits=([a16_evt[c], h16] + bw + late["dg_ready"]) if k == 0 else (), sig=(k == 30))
                    cabc = vecT[:, VC["cab"] + c:VC["cab"] + c + 1]
                    act(acv[:, c, 0:NT], ps[:, b, 0:NT], AF.Identity, waits=[lp], bias=cabc)
                    act(xbf[:, c, 0:NT], ps[:, b, 0:NT], AF.Identity, bias=cabc)
                    e = act(sqb0[:, c, 0:NT], ps[:, b, 0:NT], AF.Square, bias=cabc, sig=True)
                    P.release(b, e)
                convA_state["done"] = S.last("act")

            def convB_taps():
                cb0 = VC["cbw"]
                for c in range(4):
                    acc2 = sgs[:, c, 0:NT]
                    ts("dve", acc2, zbuf[:, c, 0:NT], vecT[:, cb0 + c:cb0 + c + 1], None, ALU.mult, waits=[hz] if c == 0 else ())
                    stt(acc2, zbuf[:, c, 1:1 + NT], vecT[:, cb0 + 4 + c:cb0 + 4 + c + 1], acc2, ALU.mult, ALU.add)
                    stt(acc2, zbuf[:, c, 2:2 + NT], vecT[:, cb0 + 8 + c:cb0 + 8 + c + 1], acc2, ALU.mult, ALU.add)

            def evac_l0(gi, i, b, pe_evt):
                gi = l0_order[gi]
                if gi == 0:
                    e = cp("act", abuf[:, i, 30:30 + NT], ps[:, b, 0:NT], waits=[pe_evt], sig=True)
                    P.release(b, e)
                elif gi == 1:
                    e = act(sgs[:, i, 0:NT], ps[:, b, 0:NT], AF.Sigmoid, waits=[pe_evt], sig=True)
                    P.release(b, e)
                    tt("dve", abuf[:, i, 30:30 + NT], abuf[:, i, 30:30 + NT], sgs[:, i, 0:NT], ALU.mult, waits=[e])
                    a16_evt[i] = cp("dve", abuf16[:, i, 30:30 + NT], abuf[:, i, 30:30 + NT], sig=True)
                    if i == 3:
                        return convA_all
                elif gi == 2:
                    e = cp("act", gbb[:, i, 0:NT], ps[:, b, 0:NT], waits=[pe_evt], sig=True)
                    P.release(b, e)
                elif gi == 3:
                    e = cp("act", zbuf[:, i, 2:2 + NT], ps[:, b, 0:NT], waits=[pe_evt], sig=True)
                    P.release(b, e)
                    sig_evt[i] = e
                else:
                    e = stt(zbuf[:, i, 2:2 + NT], ps[:, b, 0:NT], 1.0, zbuf[:, i, 2:2 + NT], ALU.mult, ALU.mult,
                           waits=[pe_evt, sig_evt[i]], sig=True)
                    P.release(b, e)
                    if i == 3:
                        return convB_taps

            gemm(groups, 8, lambda kc: hT[:, kc, :], NT, S.join("act") + list(y_dmas), evac_l0, kc_evts=h_ready)

            if stage_lim < 2:
                return
            join0 = S.join("act", "pe", "pool", "dve")
            convA_done = convA_state["done"]
            for c in range(4):
                e = tt("dve", cat[:, 4 + c, 0:NT], gbb[:, c, 0:NT], sgs[:, c, 0:NT], ALU.mult, waits=join0 if c == 0 else (), sig=True)
            conv_done = S.last("dve")
            if not is_sample:
                cp("pool", hist_a[:, :, :], abuf[:, :, NT:NT + 30], waits=[conv_done] + S.join("act"))
                cp("pool", hist_b[:, :, :], zbuf[:, :, NT:NT + 2], sig=True)
            if is_sample or last_tile:
                oa, ob = (cas, cbs) if is_sample else (cap, cbp)
                b, bw = P.alloc()
                lastpe = None
                for c in range(4):
                    tr(ps[0:30, b, c * 128:(c + 1) * 128], abuf[:, c, NT:NT + 30], ident[:, :], waits=[conv_done] + bw if c == 0 else ())
                b2, bw2 = P.alloc()
                for c in range(4):
                    lastpe = tr(ps[0:2, b2, c * 128:(c + 1) * 128], zbuf[:, c, NT:NT + 2], ident[:, :], waits=bw2 if c == 0 else (), sig=(c == 3))
                sslot, sw_ = stage_alloc()
                cp("act", stage[0:30, sslot, 0:512], ps[0:30, b, :], waits=[lastpe] + sw_)
                e = cp("act", stage[0:2, sslot, 512:1024], ps[0:2, b2, :], sig=True)
                P.release(b, e)
                P.release(b2, e)
                d1 = S.dma("sp", lambda e_, sslot=sslot: e_.dma_start(out=oa[:, :], in_=stage[0:30, sslot, 0:512]), "sts%d" % sslot, [e])
                d2 = S.dma("sp", lambda e_, sslot=sslot: e_.dma_start(out=ob[:, :], in_=stage[0:2, sslot, 512:1024]), "sts%d" % sslot)
                stage_free[sslot] = [d2]
            if stage_lim < 3:
                return
            e1_ = convA_done
            bm, bwm = P.alloc()
            bq, bwq = P.alloc()
            for c in range(4):
                mm(ps[:, bm, 0:NT], o512[:, :], xbf[:, c, 0:NT], c == 0, c == 3, waits=[e1_] + bwm + bwq if c == 0 else ())
            e2_ = None
            for c in range(4):
                e2_ = mm(ps[:, bq, 0:NT], o512[:, :], sqb0[:, c, 0:NT], c == 0, c == 3, sig=(c == 3))
            lut_prefetch(AF.Ln)
            e3_ = act(sgs[:, 0, 0:NT], ps[:, bm, 0:NT], AF.Square, waits=[e2_] + S.join("dve"), sig=True)
            e4_ = stt(sgs[:, 1, 0:NT], ps[:, bq, 0:NT], 1.0, sgs[:, 0, 0:NT], ALU.mult, ALU.subtract, waits=[e3_], sig=True)
            act(rstd[:, 0:NT], sgs[:, 1, 0:NT], AF.Ln, waits=[e4_], bias=eps5[:, :])
            e5_ = act(rstd[:, 0:NT], rstd[:, 0:NT], AF.Exp, scale=-0.5, sig=True)
            lut_prefetch(AF.Silu)
            P.release(bq, e4_)
            e7_ = {}
            for c in range(4):
                t1 = lnt[:, c % 2, 0:NT]
                t2 = lnt[:, 2 + c % 2, 0:NT]
                stt(t1, ps[:, bm, 0:NT], -1.0, acv[:, c, 0:NT], ALU.mult, ALU.add, waits=[e5_, e7_.get(c - 2)])
                e6_ = tt("dve", t2, t1, rstd[:, 0:NT], ALU.mult, sig=True)
                e7_[c] = act(cat[:, c, 0:NT], t2, AF.Silu, waits=[e6_],
                             scale=vecT[:, VC["lag"] + c:VC["lag"] + c + 1], bias=vecT[:, VC["lab"] + c:VC["lab"] + c + 1], sig=True)
            P.release(bm, S.last("dve"))
            if stage_lim < 4:
                return
            wvo = wview(woe)
            groups = [([(wvo, g * 512, 512)], "ws") for g in range(2)]
            gemm(groups, 8, lambda kc: cat[:, kc, :], NT, S.join("act", "dve", "pool"), residual_evac)
            if stage_lim < 5:
                return
            xw = ffn(0, S.join("dve"))

            if stage_lim < 6:
                return
            late_prologue2()
            h_ready = rmsnorm(VC["nmo"], [], dict(res_evt))
            wv1 = wview(wio)
            groups = [([(wv1, 0, 512)], "ws"), ([(wv1, 512, 512)], "ws"), ([(wv1, 1024, 512)], "as"),
                      ([(wv1, 1536, 512)], "ws"), ([(wv1, 2048, 512)], "as")]
            Q0 = 4 if is_sample else 4 * ti
            kcol0 = (Q0 % 8) * 128
            qk_state = {"stt": None}
            qstat_evt = {}
            kv_out = is_sample or last_tile
            okd, ovd = (ks, vs) if is_sample else (kp, vp)

            def headnorm(gi, i, b, pe_evt):
                xf = qf if gi == 0 else kf
                cp("act", xf[:, i, 0:NT], ps[:, b, 0:NT], waits=[pe_evt, qstat_evt.get(i) if gi == 1 else None])
                e = act(sqb1[:, i, 0:NT], ps[:, b, 0:NT], AF.Square, sig=True)
                P.release(b, e)
                return lambda: headnorm2(gi, i, e)

            def headnorm2(gi, i, e):
                b2, bw2 = P.alloc()
                ep = mm(ps[:, b2, 0:NT], blk64[:, :], sqb1[:, i, 0:NT], True, True, waits=[e] + bw2, sig=True)
                if gi == 0:
                    qstat_evt[i] = ep
                act(rstd[:, 0:NT], ps[:, b2, 0:NT], AF.Ln, waits=[ep], bias=eps6[:, :])
                e3 = act(ps[:, b2, 0:NT], rstd[:, 0:NT], AF.Exp, scale=-0.5, sig=True)
                if gi == 0:
                    es_ = stt(qT[:, i, 0:NT], ps[:, b2, 0:NT], vecT[:, VC["qg"]:VC["qg"] + 1], qf[:, i, 0:NT], ALU.mult, ALU.mult,
                              waits=[e3], sig=True)
                    P.release(b2, es_)
                else:
                    es_ = stt(kf[:, i, 0:NT], ps[:, b2, 0:NT], vecT[:, VC["kg"]:VC["kg"] + 1], kf[:, i, 0:NT], ALU.mult, ALU.mult,
                              waits=[e3], sig=True)
                    P.release(b2, es_)
                    es_ = cp("dve", kT[:, i, kcol0:kcol0 + NT], kf[:, i, 0:NT], sig=True)
                if i == 3:
                    qk_state["stt"] = es_
                if gi == 1 and i == 3 and kv_out:
                    for blk in range(nblk):
                        rows = min(128, NT - blk * 128)
                        bo, bwo = P.alloc()
                        lp = None
                        for c in range(4):
                            lp = tr(ps[0:rows, bo, c * 128:(c + 1) * 128], kf[:, c, blk * 128:blk * 128 + rows], ident[:, :],
                                    waits=[es_] + bwo if c == 0 else (), sig=(c == 3))
                        sslot, sw_ = stage_alloc()
                        ee = cp("act", stage[0:rows, sslot, 0:512], ps[0:rows, bo, :], waits=[lp] + sw_, sig=True)
                        P.release(bo, ee)
                        dd = S.dma("sp", lambda e_, sslot=sslot, rows=rows, blk=blk: e_.dma_start(
                            out=okd[blk * 128:blk * 128 + rows, :], in_=stage[0:rows, sslot, 0:512]), "sts%d" % sslot, [ee])
                        stage_free[sslot] = [dd]
                    qk_state["stt"] = S.last("pe")

            sv_pending = []

            def evac_l1(gi, i, b, pe_evt):
                if gi in (0, 1):
                    return headnorm(gi, i, b, pe_evt)
                elif gi == 2:
                    rows = min(128, NT - i * 128)
                    vslot = (Q0 + i) % 8
                    e = cp("act", Vb[0:rows, vslot, :], ps[0:rows, b, :], waits=[pe_evt], sig=True)
                    if kv_out:
                        sslot, sw_ = stage_alloc()
                        e = cp("dve", stage[0:rows, sslot, 0:512], ps[0:rows, b, :], waits=[pe_evt, e] + sw_, sig=True)
                        dd = S.dma("sp", lambda e_, sslot=sslot, rows=rows, i=i: e_.dma_start(
                            out=ovd[i * 128:i * 128 + rows, :], in_=stage[0:rows, sslot, 0:512]), "sts%d" % sslot, [e])
                        stage_free[sslot] = [dd]
                        P.release(b, e, S.last("act"))
                    else:
                        P.release(b, e)
                elif gi == 3:
                    e = cp("dve", uf[:, i, 0:NT], ps[:, b, 0:NT], waits=[pe_evt], sig=True)
                    P.release(b, e)
                else:
                    sv_pending.append((i, b, pe_evt))
                    if i < nblk - 1:
                        return None
                    strict0 = S.strict
                    S.strict = True
                    ea = None
                    for (tb, bb, pev) in sv_pending:
                        rows = min(128, NT - tb * 128)
                        S.add("dve", lambda e_, tb=tb, bb=bb, rows=rows: e_.bn_stats(out=st6[0:rows, tb, :], in_=ps[0:rows, bb, :]),
                              waits=[pev] + (S.join("act") if tb == 0 else []))
                        ea = S.add("dve", lambda e_, tb=tb, rows=rows: e_.bn_aggr(out=mv[0:rows, tb, :], in_=st6[0:rows, tb, :]), sig=True)
                    r0 = min(128, NT)
                    act(rsd[0:r0, 0:nblk], mv[0:r0, 0:nblk, 1], AF.Ln, waits=[ea], bias=eps5[0:r0, :])
                    eb_ = act(rsd[0:r0, 0:nblk], rsd[0:r0, 0:nblk], AF.Exp, scale=-0.5, sig=True)
                    S.strict = strict0
                    cast_evt = {}
                    for (tb, bb, pev) in sv_pending:
                        rows = min(128, NT - tb * 128)
                        tsl = tb % 2
                        e = ts("dve", tsv[0:rows, tsl, :], ps[0:rows, bb, :], mv[0:rows, tb, 0:1], rsd[0:rows, tb:tb + 1],
                               ALU.subtract, ALU.mult, waits=[eb_, cast_evt.get(tb - 2)], sig=True)
                        P.release(bb, e)
                        tt("dve", tsv[0:rows, tsl, :], tsv[0:rows, tsl, :], slg_bc[0:rows, :], ALU.mult)
                        e = tt("dve", tsv[0:rows, tsl, :], tsv[0:rows, tsl, :], slb_bc[0:rows, :], ALU.add, sig=True)
                        cast_evt[tb] = cp("act", svn[0:rows, tb, :], tsv[0:rows, tsl, :], waits=[e], sig=True)
                        if is_sample:
                            out_evts.append(S.dma("sp", lambda e_, rows=rows, tsl=tsl: e_.dma_start(out=svs[:, :], in_=tsv[0:rows, tsl, :]),
                                                  "outs", [e]))
                return None

            gemm(groups, 8, lambda kc: hT[:, kc, :], NT, [], evac_l1, kc_evts=h_ready)

            if stage_lim < 7:
                return
            sv_ready = S.join("act", "dve")
            for g in range(4):
                b, bw = P.alloc()
                lp = None
                for tb in range(nblk):
                    rows = min(128, NT - tb * 128)
                    mm(ps[:, b, tb * 128:tb * 128 + rows], svn[0:rows, tb, g * 128:(g + 1) * 128], wsT[0:rows, g, 0:rows], True, False,
                       waits=(sv_ready + bw + [pool_c2]) if tb == 0 else ())
                    lp = mm(ps[:, b, tb * 128:tb * 128 + rows], ones1[0:1, :], bsrow[0:1, g * 128:g * 128 + rows], False, True,
                            sig=(tb == nblk - 1))
                e = stt(cat[:, 4 + g, 0:NT], ps[:, b, 0:NT], 1.0, uf[:, g, 0:NT], ALU.mult, ALU.mult, waits=[lp], sig=True)
                P.release(b, e)

            if stage_lim < 8:
                return
            att_state = {}
            att_in_ready = S.join("act", "dve")
            tbuf_rd = [None, None]
            pv_done = {}
            norm_done = {}
            def att_scores(qb):
                    ntq = min(128, NT - qb * 128)
                    Q = Q0 + qb
                    jbs = []
                    for jb in range(5):
                        kb = Q - 4 + jb
                        if (not is_sample) and kb < 0:
                            continue
                        nkeys = NS if (is_sample and jb == 4) else 128
                        jbs.append((jb, kb % 8, nkeys))
                    ebuf = qb % 2
                    for ji, (jb, slot, nkeys) in enumerate(jbs):
                        bA, wA = P.alloc()
                        bB, wB = P.alloc()
                        lp = None
                        w0_ = wA + wB + att_in_ready + [cvl, late["tb_ready"]]
                        if not is_sample:
                            for par, bk in ((0, bA), (1, bB)):
                                mm(ps[:, bk, :], identb[:, :], Tb[:, jb, par * 4:(par + 1) * 4, :].rearrange("p a b -> p (a b)"),
                                   True, False, waits=w0_ if par == 0 else ())
                        for h in range(8):
                            c, par = h // 2, h % 2
                            pb = par * 64
                            bk = bB if par else bA
                            lp = mm(ps[0:nkeys, bk, c * 128:c * 128 + ntq], kT[pb:pb + 64, c, slot * 128:slot * 128 + nkeys],
                                    qT[pb:pb + 64, c, qb * 128:qb * 128 + ntq], is_sample, is_sample or h >= 6,
                                    waits=w0_ if (h == 0 and is_sample) else (), sig=(h == 7))
                        if is_sample:
                            tsl = ji % 2
                            ed = None
                            for par, bk in ((0, bA), (1, bB)):
                                ed = stt(tbuf[0:nkeys, tsl, par * 4:(par + 1) * 4, 0:ntq],
                                         ps[0:nkeys, bk, :].rearrange("p (a b) -> p a b", a=4)[:, :, 0:ntq], 1.0,
                                         Tb[0:nkeys, jb, par * 4:(par + 1) * 4, 0:ntq], ALU.mult, ALU.add,
                                         waits=[lp, tbuf_rd[tsl]] if par == 0 else (), sig=(par == 1))
                            P.release(bA, ed)
                            P.release(bB, ed)
                            tbuf_rd[tsl] = act(Eb[0:nkeys, ebuf, ji, :].rearrange("p (a b) -> p a b", a=8)[:, :, 0:ntq],
                                               tbuf[0:nkeys, tsl, :, 0:ntq], AF.Exp, scale=0.125, waits=[ed, pv_done.get(qb - 2)], sig=True)
                        else:
                            for par, bk in ((0, bA), (1, bB)):
                                ee_ = act(Eb[:, ebuf, ji, par * 512:(par + 1) * 512], ps[:, bk, :], AF.Exp, scale=0.125,
                                          waits=[lp, pv_done.get(qb - 2)] if par == 0 else (), sig=True)
                                P.release(bk, ee_)
                    e_ready = S.last("act")
                    att_state[qb] = (ntq, jbs, ebuf, e_ready)
            def att_pv(qb):
                    ntq, jbs, ebuf, e_ready = att_state[qb]
                    dbs = []
                    for par in range(2):
                        bd, wd_ = P.alloc()
                        lp = None
                        for ji, (jb, slot, nkeys) in enumerate(jbs):
                            lp = mm(ps[:, bd, 0:4 * ntq], ones1[0:nkeys, :],
                                    Eb[0:nkeys, ebuf, ji, :].rearrange("p (a b) -> p a b", a=8)[:, par * 4:(par + 1) * 4, 0:ntq],
                                    ji == 0, ji == len(jbs) - 1, waits=[e_ready] + wd_ if ji == 0 else (), sig=(ji == len(jbs) - 1))
                        act(rden[:, par, 0:4 * ntq], ps[:, bd, 0:4 * ntq], AF.Ln, waits=[lp, norm_done.get(qb - 1)])
                        e = act(rden[:, par, 0:4 * ntq], rden[:, par, 0:4 * ntq], AF.Exp, scale=-1.0, sig=True)
                        P.release(bd, e)
                    rd_ready = S.last("act")
                    bo, wo = P.alloc()
                    lp = None
                    for h in range(8):
                        c, par = h // 2, h % 2
                        pb = par * 64
                        for ji, (jb, slot, nkeys) in enumerate(jbs):
                            lp = mm(ps[pb:pb + 64, bo, c * 128:c * 128 + ntq], Vb[0:nkeys, slot, h * 64:(h + 1) * 64],
                                    Eb[0:nkeys, ebuf, ji, :].rearrange("p (a b) -> p a b", a=8)[:, par * 4 + c, 0:ntq],
                                    ji == 0, ji == len(jbs) - 1, waits=wo if (h == 0 and ji == 0) else (),
                                    sig=(h == 7 and ji == len(jbs) - 1))
                    e = None
                    for par in range(2):
                        pb = par * 64
                        e = stt(cat[pb:pb + 64, 0:4, qb * 128:qb * 128 + ntq],
                               ps[pb:pb + 64, bo, :].rearrange("p (a b) -> p a b", a=4)[:, :, 0:ntq], 1.0,
                               rden[pb:pb + 64, par, 0:4 * ntq].rearrange("p (a b) -> p a b", a=4), ALU.mult, ALU.mult,
                               waits=[lp, rd_ready] if par == 0 else (), sig=(par == 1))
                    P.release(bo, e)
                    pv_done[qb] = lp
                    norm_done[qb] = e
            if nblk > 0:
                att_scores(0)
            for qb in range(nblk):
                if qb + 1 < nblk:
                    att_scores(qb + 1)
                att_pv(qb)

            if stage_lim < 9:
                return
            wvo1 = wview(woo)
            groups = [([(wvo1, g * 512, 512)], "ws") for g in range(2)]
            gemm(groups, 8, lambda kc: cat[:, kc, :], NT, S.join("act", "dve", "pool"), residual_evac)
            if is_sample and n_prompt_tiles > 0:
                nxt = (False, 0)
            elif (not is_sample) and ti + 1 < n_prompt_tiles:
                nxt = (False, ti + 1)
            else:
                nxt = None
            if nxt is not None:
                pw = S.join("pe", "act", "dve")
                for pblk in range(4):
                    r0 = nxt[1] * TT + pblk * 128
                    ld = S.dma("sp", lambda e_, pblk=pblk, r0=r0: e_.dma_start(out=xpre[:, pblk, :], in_=xp[r0:r0 + 128, :]),
                               "xpre", pw if pblk == 0 else ())
                    prefetch[(nxt[0], nxt[1], pblk)] = ld
                for pblk in range(4):
                    prefetch[(nxt[0], nxt[1], pblk)] = ld
            xw = ffn(1, S.join("dve"))

            del y_dmas[:]
            ee_both = []
            for blk in range(nblk):
                rows = min(128, NT - blk * 128)
                ee = None
                for half in range(2):
                    b, bw = P.alloc()
                    lp = None
                    for c4 in range(4):
                        c = half * 4 + c4
                        lp = tr(ps[0:rows, b, c4 * 128:(c4 + 1) * 128], xT[:, c, blk * 128:blk * 128 + rows], ident[:, :],
                                waits=(list(xw) + bw) if c4 == 0 else (), sig=(c4 == 3))
                    ee = cp("act" if half == 0 else "dve", ystage[0:rows, blk, half * 512:(half + 1) * 512], ps[0:rows, b, :],
                            waits=[lp], sig=True)
                    P.release(b, ee)
                    ee_both.append(ee)
                y_dmas.append(S.dma("sp", lambda e_, rows=rows, blk=blk: e_.dma_start(
                    out=ydst[t0 + blk * 128:t0 + blk * 128 + rows, :], in_=ystage[0:rows, blk, :]), "ysem", ee_both[-2:]))

        if do_sample:
            S.strict = True
            process_tile(True, 0)
            S.strict = False
        for ti in range(n_prompt_tiles):
            process_tile(False, ti)
        if stage_lim < 99:
            jw = S.join()
            out_evts.append(S.dma("sp", lambda e_: e_.dma_start(out=dbg_x[:, :], in_=xT[:, :, :].rearrange("p a b -> p (a b)")), "outs", jw))
            out_evts.append(S.dma("pool", lambda e_: e_.dma_start(out=dbg_c[:, :], in_=cat[:, :, :].rearrange("p a b -> p (a b)")), "outs", jw))
            out_evts.append(S.dma("pool", lambda e_: e_.dma_start(out=dbg_h[:, :], in_=hT[:, :, :].rearrange("p a b -> p (a b)")), "outs", jw))

        final_waits = [(k, S.dmac[k]) for k in ("sts0", "sts1", "outs", "ysem") if S.dmac.get(k, 0) > 0]
        S.add("sp", lambda e: e.nop(), waits=final_waits + S.join())

        handles = {}
        with nc.Block() as block:
            CENG = ("pe", "act", "dve", "pool")
            waited = {e: set() for e in CENG}
            for eng_ in ENG:
                for fn, waits, sigkey, inc in S.ops[eng_]:
                    wm = {}
                    for (k, v) in waits:
                        wm[k] = max(wm.get(k, 0), v)
                    for (k, v) in wm.items():
                        if k in waited:
                            waited[k].add(v)
            newval = {e: {} for e in CENG}
            for eng_ in CENG:
                n_old = 0
                n_new = 0
                for fn, waits, sigkey, inc in S.ops[eng_]:
                    if sigkey == eng_:
                        n_old += 1
                        if n_old in waited[eng_]:
                            n_new += 1
                            newval[eng_][n_old] = n_new

            def emit(eng_handle, eng):
                seen = {}
                n_old = 0
                for fn, waits, sigkey, inc in S.ops[eng]:
                    wmax = {}
                    for (k, v) in waits:
                        wmax[k] = max(wmax.get(k, 0), v)
                    for (k, v) in wmax.items():
                        if k in newval:
                            v = newval[k][v]
                        if seen.get(k, 0) >= v:
                            continue
                        eng_handle.wait_ge(sems[k], v)
                        seen[k] = v
                    ins = fn(eng_handle)
                    if sigkey is not None:
                        if sigkey in newval:
                            n_old += 1
                            if n_old in newval[sigkey]:
                                ins.then_inc(sems[sigkey], inc)
                        else:
                            ins.then_inc(sems[sigkey], inc)

            @block.tensor
            def _(e):
                emit(e, "pe")

            @block.scalar
            def _(e):
                emit(e, "act")

            @block.vector
            def _(e):
                emit(e, "dve")

            @block.gpsimd
            def _(e):
                emit(e, "pool")

            @block.sync
            def _(e):
                emit(e, "sp")
    return nc


_CACHE = {}


def kernel(x_prompt, x_sample, cache_conv_a, cache_conv_b, cache_k, cache_v,
           norm_mix_even, w_in_even, conv_a_w, conv_a_b, ln_a_g, ln_a_b, conv_b_w, w_out_even,
           norm_mix_odd, w_in_odd, q_norm_g, k_norm_g, rel_bias, sgu_ln_g, sgu_ln_b, sgu_w, sgu_b,
           w_out_odd, norm_ffn, w_gate_up, w_down):
    npt = NPT
    f = lambda a: np.ascontiguousarray(np.asarray(a, dtype=np.float32))
    key = npt
    if key not in _CACHE:
        _CACHE[key] = build_program(npt, True)
    nc = _CACHE[key]
    shared = {
        "nme": f(norm_mix_even[0]), "wie": f(w_in_even[0]), "caw": f(conv_a_w[0]), "cab": f(conv_a_b[0]),
        "lag": f(ln_a_g[0]), "lab": f(ln_a_b[0]), "cbw": f(conv_b_w[0]), "woe": f(w_out_even[0]),
        "nmo": f(norm_mix_odd[0]), "wio": f(w_in_odd[0]), "qg": f(q_norm_g[0]), "kg": f(k_norm_g[0]),
        "rb": f(rel_bias[0]), "slg": f(sgu_ln_g[0]), "slb": f(sgu_ln_b[0]), "sw": f(sgu_w[0]), "sb": f(sgu_b[0]),
        "woo": f(w_out_odd[0]), "nf": f(norm_ffn), "wgu": f(w_gate_up), "wd": f(w_down),
    }
    xpn, xsn = np.asarray(x_prompt), np.asarray(x_sample)
    cca, ccb = np.asarray(cache_conv_a), np.asarray(cache_conv_b)
    ckn, cvn = np.asarray(cache_k), np.asarray(cache_v)
    in_maps = []
    for b in range(8):
        m = dict(shared)
        m["xp"] = f(xpn[b]); m["xs"] = f(xsn[b])
        m["cca"] = f(cca[0, b]); m["ccb"] = f(ccb[0, b])
        m["ck"] = f(ckn[0, b].reshape(512, 512)); m["cv"] = f(cvn[0, b].reshape(512, 512))
        in_maps.append(m)
    res = run_bass_kernel_spmd(nc, in_maps, core_ids=list(range(8)))
    r = res.results
    st = lambda k: np.stack([np.asarray(r[b][k], dtype=np.float32) for b in range(8)])
    y_p = st("yp"); y_s = st("ys")
    ca_p = st("cap")[None]; cb_p = st("cbp")[None]
    k_p = st("kp").reshape(8, 512, 8, 64)[None]; v_p = st("vp").reshape(8, 512, 8, 64)[None]
    ca_s = st("cas")[None]; cb_s = st("cbs")[None]
    k_s = st("ks").reshape(8, NS, 8, 64)[None]; v_s = st("vs").reshape(8, NS, 8, 64)[None]
    sv_s = st("svs")[None]
    return (y_p, y_s, ca_p, cb_p, k_p, v_p, ca_s, cb_s, k_s, v_s, sv_s)
```
